# Optimizing a Trainium2 kernel written in Bass

```python
import math
import jax, jax.numpy as jnp
from jax import lax
import numpy as np

D_MODEL = 2048
BATCH = 8
SEQ = 2048
DEPTH = 2

N_BRANCH = 4
BRANCH_WIDTH = D_MODEL // 4
ATT_HEADS = 8
ATT_HEAD_DIM = BRANCH_WIDTH // ATT_HEADS
Q_BLOCK = 128
CONF_KERNEL = 31
POOL_WINDOWS = (2, 4, 8, 16)
POOL_GROUPS = len(POOL_WINDOWS)
POOL_GROUP_DIM = BRANCH_WIDTH // POOL_GROUPS
SHORT_KERNEL = 3
D_FF = 4 * D_MODEL
RMS_EPS = 1e-6
LN_EPS = 1e-5

ATT_COLS = 3 * BRANCH_WIDTH + ATT_HEADS
CONF_COLS = 2 * BRANCH_WIDTH
POOL_COLS = BRANCH_WIDTH
SCONV_COLS = 3 * BRANCH_WIDTH
GATE_COLS = N_BRANCH * D_MODEL
IN_COLS = ATT_COLS + CONF_COLS + POOL_COLS + SCONV_COLS + GATE_COLS

kernel_name = 'hybrid_fox_conformer_pool_shortconv_block'


def rmsnorm(x, gain):
    xf = x.astype(jnp.float32)
    y = xf * lax.rsqrt(jnp.mean(xf * xf, axis=-1, keepdims=True) + RMS_EPS)
    return (y * gain.astype(jnp.float32)).astype(x.dtype)


def layernorm(x, gain, bias):
    xf = x.astype(jnp.float32)
    mu = jnp.mean(xf, axis=-1, keepdims=True)
    xc = xf - mu
    y = xc * lax.rsqrt(jnp.mean(xc * xc, axis=-1, keepdims=True) + LN_EPS)
    return (y * gain.astype(jnp.float32) + bias.astype(jnp.float32)).astype(x.dtype)


def causal_dwconv(u, w):
    K, C = w.shape
    return lax.conv_general_dilated(
        u, w[:, None, :].astype(u.dtype), window_strides=(1,), padding=[(K - 1, 0)],
        dimension_numbers=('NWC', 'WIO', 'NWC'), feature_group_count=C)


def forgetting_attention(q, k, v, f_logit, b_f, q_gain, k_gain):
    B, T, _ = q.shape
    q = rmsnorm(q.reshape(B, T, ATT_HEADS, ATT_HEAD_DIM), q_gain)
    k = rmsnorm(k.reshape(B, T, ATT_HEADS, ATT_HEAD_DIM), k_gain)
    v = v.reshape(B, T, ATT_HEADS, ATT_HEAD_DIM)
    log_f = jax.nn.log_sigmoid((f_logit + b_f).astype(jnp.float32))
    cum = jnp.cumsum(log_f, axis=1).transpose(0, 2, 1)
    scale = 1.0 / math.sqrt(ATT_HEAD_DIM)
    outs = []
    for i in range(T // Q_BLOCK):
        q0, q1 = i * Q_BLOCK, (i + 1) * Q_BLOCK
        s = jnp.einsum('bqhd,bkhd->bhqk', q[:, q0:q1], k[:, :q1]).astype(jnp.float32) * scale
        s = s + cum[:, :, q0:q1, None] - cum[:, :, None, :q1]
        mask = jnp.arange(q0, q1)[:, None] >= jnp.arange(q1)[None, :]
        s = jnp.where(mask[None, None], s, -jnp.inf)
        p = jax.nn.softmax(s, axis=-1).astype(v.dtype)
        outs.append(jnp.einsum('bhqk,bkhd->bqhd', p, v[:, :q1]))
    return jnp.concatenate(outs, axis=1).reshape(B, T, BRANCH_WIDTH)


def conformer_conv(a, g, dw, db, ln_g, ln_b):
    u = a * jax.nn.sigmoid(g)
    u = causal_dwconv(u, dw) + db
    u = layernorm(u, ln_g, ln_b)
    return jax.nn.silu(u)


def multiscale_pool(p, pool_w, pool_scale):
    B, T, C = p.shape
    pg = p.reshape(B, T, POOL_GROUPS, POOL_GROUP_DIM).astype(jnp.float32)
    cs = jnp.cumsum(pg, axis=1)
    pos = jnp.arange(T)
    pooled = []
    for g, w in enumerate(POOL_WINDOWS):
        csg = cs[:, :, g]
        shifted = jnp.pad(csg, ((0, 0), (w, 0), (0, 0)))[:, :T]
        cnt = jnp.minimum(pos + 1, w).astype(jnp.float32)[None, :, None]
        pooled.append((csg - shifted) / cnt)
    d = (jnp.stack(pooled, axis=2) - pg).astype(p.dtype)
    y = jnp.einsum('btgc,gcd->btgd', d, pool_w).reshape(B, T, C)
    return y * pool_scale


def short_gated_conv(xin, bg, cg, w):
    return bg * causal_dwconv(cg * xin, w)


def setup_inputs(seed: int = 0) -> dict:
    key = jax.random.key(seed)
    ks = jax.random.split(key, 24)
    D, W = D_MODEL, BRANCH_WIDTH
    nrm = lambda k, shape, s: jax.random.normal(k, shape, jnp.float32) * s
    return {
        'x': nrm(ks[0], (BATCH, SEQ, D), 1.0),
        'c': nrm(ks[1], (BATCH, D), 1.0),
        'w_ada': nrm(ks[2], (DEPTH, D, 6 * D), 0.5 * D ** -0.5),
        'b_ada': nrm(ks[3], (DEPTH, 6 * D), 0.01),
        'norm_gain': 1.0 + nrm(ks[4], (DEPTH, 2, D), 0.02),
        'w_in': nrm(ks[5], (DEPTH, D, IN_COLS), D ** -0.5),
        'b_f': jax.random.uniform(ks[6], (DEPTH, ATT_HEADS), jnp.float32, 1.0, 6.0),
        'b_gate': nrm(ks[7], (DEPTH, GATE_COLS), 0.01),
        'q_gain': 1.0 + nrm(ks[8], (DEPTH, ATT_HEAD_DIM), 0.02),
        'k_gain': 1.0 + nrm(ks[9], (DEPTH, ATT_HEAD_DIM), 0.02),
        'conf_dw': nrm(ks[10], (DEPTH, CONF_KERNEL, W), CONF_KERNEL ** -0.5),
        'conf_db': nrm(ks[11], (DEPTH, W), 0.01),
        'conf_ln_g': 1.0 + nrm(ks[12], (DEPTH, W), 0.02),
        'conf_ln_b': nrm(ks[13], (DEPTH, W), 0.01),
        'pool_w': nrm(ks[14], (DEPTH, POOL_GROUPS, POOL_GROUP_DIM, POOL_GROUP_DIM), POOL_GROUP_DIM ** -0.5),
        'pool_scale': 1.0 + nrm(ks[15], (DEPTH, W), 0.02),
        'sconv_w': nrm(ks[16], (DEPTH, SHORT_KERNEL, W), SHORT_KERNEL ** -0.5),
        'w_branch': nrm(ks[17], (DEPTH, N_BRANCH, W, D), W ** -0.5),
        'w_out': nrm(ks[18], (DEPTH, D, D), D ** -0.5),
        'w_mlp1': nrm(ks[19], (DEPTH, D, D_FF), D ** -0.5),
        'w_mlp2': nrm(ks[20], (DEPTH, D_FF, D), D_FF ** -0.5),
    }


def reference(x, c, w_ada, b_ada, norm_gain, w_in, b_f, b_gate, q_gain, k_gain,
              conf_dw, conf_db, conf_ln_g, conf_ln_b, pool_w, pool_scale, sconv_w,
              w_branch, w_out, w_mlp1, w_mlp2):
    B, T, D = x.shape
    W = BRANCH_WIDTH
    for l in range(DEPTH):
        mod = (c @ w_ada[l] + b_ada[l]).astype(x.dtype)
        sh_m, sc_m, g_m, sh_f, sc_f, g_f = jnp.split(mod[:, None, :], 6, axis=-1)

        h = rmsnorm(x, norm_gain[l, 0]) * (1.0 + sc_m) + sh_m
        z = h @ w_in[l]
        o = 0
        zq, zk, zv = z[..., o:o + W], z[..., o + W:o + 2 * W], z[..., o + 2 * W:o + 3 * W]
        zf = z[..., o + 3 * W:o + ATT_COLS]; o += ATT_COLS
        za, zg = z[..., o:o + W], z[..., o + W:o + 2 * W]; o += CONF_COLS
        zp = z[..., o:o + W]; o += POOL_COLS
        zx, zb, zc = z[..., o:o + W], z[..., o + W:o + 2 * W], z[..., o + 2 * W:o + 3 * W]; o += SCONV_COLS
        gates = jax.nn.sigmoid(z[..., o:o + GATE_COLS] + b_gate[l]).reshape(B, T, N_BRANCH, D)

        y_att = forgetting_attention(zq, zk, zv, zf, b_f[l], q_gain[l], k_gain[l])
        y_conf = conformer_conv(za, zg, conf_dw[l], conf_db[l], conf_ln_g[l], conf_ln_b[l])
        y_pool = multiscale_pool(zp, pool_w[l], pool_scale[l])
        y_sconv = short_gated_conv(zx, zb, zc, sconv_w[l])

        merged = (gates[:, :, 0] * (y_att @ w_branch[l, 0])
                  + gates[:, :, 1] * (y_conf @ w_branch[l, 1])
                  + gates[:, :, 2] * (y_pool @ w_branch[l, 2])
                  + gates[:, :, 3] * (y_sconv @ w_branch[l, 3]))
        x = x + g_m * (merged @ w_out[l])

        h = rmsnorm(x, norm_gain[l, 1]) * (1.0 + sc_f) + sh_f
        x = x + g_f * (jnp.square(jax.nn.relu(h @ w_mlp1[l])) @ w_mlp2[l])
    return x
```

```python
import numpy as np
import concourse.bass as bass
import concourse.mybir as mybir
from concourse.bass_utils import run_bass_kernel_spmd

F32 = mybir.dt.float32
BF16 = mybir.dt.bfloat16
AF = mybir.ActivationFunctionType
ALU = mybir.AluOpType

D = 2048
T = 2048
L = 2
NCH = 16
KC = 16
IN_COLS = 12808
Q0, K0, V0, F0, A0, G0, P0, SX0, SB0, SC0, GATE0 = 0, 512, 1024, 1536, 1544, 2056, 2568, 3080, 3592, 4104, 4616
DFF = 8192
C_BADA, C_GAIN, C_BGATE, C_DW, C_DB, C_LNG, C_LNB, C_PSC, C_SW, C_QG, C_KG, C_BF, C_CORR = (
    0, 96, 128, 192, 316, 320, 324, 328, 332, 344, 345, 346, 347)
NPP = 416
POOLW = (2, 4, 8, 16)
SB_LO, SB_HI = 16512, 229344


class Buf:
    __slots__ = ("name", "w", "r")

    def __init__(self, name):
        self.name = name
        self.w = None
        self.r = []


class Op:
    __slots__ = ("eng", "fns", "deps", "dma", "lane", "ev", "sig")

    def __init__(self, eng, fns, deps, dma, lane):
        self.eng, self.fns, self.deps, self.dma, self.lane = eng, fns, deps, dma, lane
        self.ev = None
        self.sig = False


ENGS = ("pe", "act", "dve", "pool", "sp")


class Sched:
    def __init__(self):
        self.ops = []
        self.pending = {e: None for e in ENGS}
        self.last = {}
        self.dma_since = []

    def add(self, eng, fns, reads=(), writes=(), dma=False, lane=None):
        idx = len(self.ops)
        deps = set()
        for b in reads:
            if b.w is not None:
                deps.add(b.w)
        for b in writes:
            if b.w is not None:
                deps.add(b.w)
            deps.update(b.r)
        for b in reads:
            b.r.append(idx)
        for b in writes:
            b.w = idx
            b.r = []
        if self.pending[eng] is not None:
            deps.update(self.pending[eng])
            self.pending[eng] = None
        deps.discard(idx)
        self.ops.append(Op(eng, fns if isinstance(fns, list) else [fns], deps, dma, lane))
        if dma:
            self.dma_since.append(idx)
        else:
            self.last[eng] = idx
        return idx

    def barrier(self):
        deps = set(self.last.values()) | set(self.dma_since)
        for e in ENGS:
            if self.pending[e] is None:
                self.pending[e] = set(deps)
            else:
                self.pending[e] |= deps
        self.dma_since = []

    def finalize(self):
        needed = set()
        for op in self.ops:
            needed |= op.deps
        cnt = {e: 0 for e in ENGS}
        lanes = {}
        for i, op in enumerate(self.ops):
            if op.dma:
                lanes[op.lane] = lanes.get(op.lane, 0) + 16
                op.ev = (("L", op.lane), lanes[op.lane])
                op.sig = True
            elif i in needed:
                cnt[op.eng] += 1
                ep = (cnt[op.eng] - 1) // 30000
                op.ev = (("E", op.eng, ep), cnt[op.eng] - 30000 * ep)
                op.sig = True
        semkeys = []
        for op in self.ops:
            if op.ev is not None and op.ev[0] not in semkeys:
                semkeys.append(op.ev[0])
        return semkeys


def build_nc(layers=(0, 1), first=True, last=True, dbg=None):
    nc = bass.Bass("TRN2", target_bir_lowering=False)
    S = Sched()

    xin = nc.dram_tensor("xT", [NCH, 128, T], F32, kind="ExternalInput").ap()
    cT_d = nc.dram_tensor("cT", [128, KC], F32, kind="ExternalInput").ap()
    w_ada = nc.dram_tensor("w_ada", [L, D, 6 * D], F32, kind="ExternalInput").ap()
    w_in = nc.dram_tensor("w_in", [L, D, IN_COLS], F32, kind="ExternalInput").ap()
    w_br = nc.dram_tensor("w_branch", [L, 4, 512, D], F32, kind="ExternalInput").ap()
    w_out = nc.dram_tensor("w_out", [L, D, D], F32, kind="ExternalInput").ap()
    w_m1 = nc.dram_tensor("w_mlp1", [L, D, DFF], F32, kind="ExternalInput").ap()
    w_m2 = nc.dram_tensor("w_mlp2", [L, DFF, D], F32, kind="ExternalInput").ap()
    pool_w = nc.dram_tensor("pool_w", [L, 4, 128, 128], F32, kind="ExternalInput").ap()
    pp_d = nc.dram_tensor("pp", [L, 128, NPP], F32, kind="ExternalInput").ap()
    masks_d = nc.dram_tensor("masks", [4, 128, 512], F32, kind="ExternalInput").ap()
    yout = nc.dram_tensor("yT", [NCH, 128, T], F32, kind="ExternalOutput").ap()
    xs1 = nc.dram_tensor("xs1", [NCH, 128, T], F32, kind="Internal").ap()
    xs2 = nc.dram_tensor("xs2", [NCH, 128, T], F32, kind="Internal").ap()
    xs3 = nc.dram_tensor("xs3", [NCH, 128, T], F32, kind="Internal").ap()
    mrg = nc.dram_tensor("mrg", [NCH, 128, T], BF16, kind="Internal").ap()
    cumsc = nc.dram_tensor("cumsc", [2, 8, 3, T], BF16, kind="Internal").ap()
    if dbg is not None:
        dbg32 = nc.dram_tensor("dbg32", [128, dbg], F32, kind="ExternalOutput").ap()
        dbgB = Buf("dbg32")
    xinB = [Buf(f"xin{c}") for c in range(NCH)]
    xs1B = [Buf(f"xs1_{c}") for c in range(NCH)]
    xs2B = [Buf(f"xs2_{c}") for c in range(NCH)]
    xs3B = [Buf(f"xs3_{c}") for c in range(NCH)]
    youtB = [Buf(f"yout{c}") for c in range(NCH)]
    mrgB = [Buf(f"mrg{c}") for c in range(NCH)]
    cumB = Buf("cumsc")

    st = {"off": SB_LO, "n": 0}

    def alloc(shape, dtype, name="t"):
        per = 1
        for s_ in shape[1:]:
            per *= s_
        nbytes = per * (4 if dtype == F32 else 2)
        nbytes = (nbytes + 63) // 64 * 64
        st["n"] += 1
        t = nc.alloc_sbuf_tensor_at(f"{name}_{st['n']}", list(shape), dtype, offset=st["off"])
        st["off"] += nbytes
        assert st["off"] <= SB_HI, f"SBUF overflow at {name}: {st['off']}"
        return t

    def mark():
        return st["off"]

    def release(m):
        st["off"] = m
        S.barrier()

    ps = [nc.alloc_psum_tensor(f"ps{i}", [128, 512], F32) for i in range(8)]
    psB = [Buf(f"ps{i}") for i in range(8)]
    pst = {"i": 0}

    def bank(lo=0, hi=8):
        n = hi - lo
        b = lo + pst["i"] % n
        pst["i"] += 1
        return b

    def MM(out, lhsT, rhs, start, stop):
        return lambda e: e.matmul(out, lhsT, rhs, start=start, stop=stop)

    def ACT(out, in_, func, bias=None, scale=None):
        kw = {}
        if bias is not None:
            kw["bias"] = bias
        if scale is not None:
            kw["scale"] = scale
        return lambda e: e.activation(out, in_, func, **kw)

    def TT(out, a, b, op):
        return lambda e: e.tensor_tensor(out, a, b, op)

    def TS(out, a, s1, s2, op0, op1=None):
        if op1 is None:
            return lambda e: e.tensor_scalar(out, a, s1, None, op0)
        return lambda e: e.tensor_scalar(out, a, s1, s2, op0, op1)

    def STT(out, a, s, b, op0, op1):
        return lambda e: e.scalar_tensor_tensor(out, a, s, b, op0, op1)

    def CP(out, in_):
        return lambda e: e.tensor_copy(out, in_)

    def RCP(out, in_):
        return lambda e: e.reciprocal(out, in_)

    def MS(ap, v):
        return lambda e: e.memset(ap, v)

    def DMA(out, in_):
        return lambda e: e.dma_start(out=out, in_=in_)

    ts = lambda i, n=512: slice(i * n, (i + 1) * n)

    ones_bf = alloc([128, 128], BF16, "ones")
    onesB = Buf("ones")
    one_f = alloc([1, 2], F32, "onef")
    eps_t = alloc([128, 2], F32, "eps")
    masks = alloc([128, 4, 512], F32, "masks")
    masksB = Buf("masks")
    pp = alloc([128, L, NPP], F32, "pp")
    ppB = Buf("pp")
    c_f = alloc([128, KC], F32, "cf")
    c_bf = alloc([128, KC], BF16, "cbf")
    cB = Buf("c")
    modT = alloc([128, L, 96], F32, "modT")
    modB = [Buf(f"mod{l}") for l in range(L)]
    sc = alloc([128, L, 40], F32, "scal")
    scB = [Buf(f"sc{l}") for l in range(L)]
    h = alloc([128, KC, T], BF16, "h")
    hB = [Buf(f"h{c}") for c in range(KC)]
    wt = [alloc([128, KC, 512], BF16, f"w{i}") for i in range(2)]
    wB = [Buf(f"w{i}") for i in range(2)]
    wst = {"i": 0}

    def wload(src, kind="full"):
        s_ = wst["i"] % 2
        wst["i"] += 1
        if kind == "full":
            dst = wt[s_][:, :, :]
        elif kind == "c256":
            dst = wt[s_][:, :, 0:256]
        elif kind == "r8":
            dst = wt[s_][:, 0:8, :]
        S.add("pool", DMA(dst, src), writes=[wB[s_]], dma=True, lane=f"w{s_}")
        return wt[s_], wB[s_]

    def colblk(wap, c0, n=512):
        return wap[:, c0:c0 + n].rearrange("(kc p) n -> p kc n", p=128)

    R0 = mark()

    S.add("dve", MS(ones_bf[:], 1.0), writes=[onesB])
    S.add("dve", [MS(one_f[:], 1.0), MS(eps_t[:, 0:1], 1e-6), MS(eps_t[:, 1:2], 1e-5)], writes=[onesB])
    S.add("sp", DMA(masks[:], masks_d.rearrange("i k q -> k i q")), writes=[masksB], dma=True, lane="masks")
    S.add("sp", DMA(pp[:], pp_d.rearrange("l p n -> p l n")), writes=[ppB], dma=True, lane="pp")
    S.add("sp", DMA(c_f[:], cT_d), writes=[cB], dma=True, lane="c")
    S.add("dve", CP(c_bf[:], c_f[:]), reads=[cB], writes=[cB])

    def dump(ap, col, n, bufs):
        if dbg is None:
            return
        S.add("sp", DMA(dbg32[0:ap.shape[0], col:col + n], ap), reads=bufs, writes=[dbgB], dma=True, lane="dbg")

    def mod_phase(l):
        m0 = mark()
        row = [alloc([1, 512], F32, "modrow") for _ in range(2)]
        rowB = [Buf(f"row{i}") for i in range(2)]
        bT = 7
        for j in range(24):
            Wt, WB = wload(colblk(w_ada[l], j * 512))
            b = bank(0, 4)
            S.add("pe", [MM(ps[b][0:1, :], c_bf[:, kc:kc + 1], Wt[:, kc, :], kc == 0, kc == KC - 1) for kc in range(KC)],
                  reads=[cB, WB], writes=[psB[b]])
            r = j % 2
            S.add("act", ACT(row[r][:], ps[b][0:1, :], AF.Identity), reads=[psB[b]], writes=[rowB[r]])
            S.add("pe", [MM(ps[bT][:, j * 4 + i:j * 4 + i + 1], row[r][0:1, ts(i, 128)], one_f[0:1, 0:1], True, True)
                         for i in range(4)], reads=[rowB[r], onesB], writes=[psB[bT]])
        S.add("dve", TT(modT[:, l, :], ps[bT][:, 0:96], pp[:, l, C_BADA:C_BADA + 96], ALU.add),
              reads=[psB[bT], ppB], writes=[modB[l]])
        fns = []
        for (dst0, scoff, goff) in ((0, 16, C_GAIN), (16, 64, C_GAIN + 16)):
            fns.append(TS(sc[:, l, dst0:dst0 + 16], modT[:, l, scoff:scoff + 16], 1.0, None, ALU.add))
            fns.append(TT(sc[:, l, dst0:dst0 + 16], sc[:, l, dst0:dst0 + 16], pp[:, l, goff:goff + 16], ALU.mult))
        fns.append(TS(sc[:, l, 32:33], pp[:, l, C_QG:C_QG + 1], 0.125, None, ALU.mult))
        fns.append(CP(sc[:, l, 33:34], pp[:, l, C_KG:C_KG + 1]))
        fns.append(TS(sc[:, l, 34:35], pp[:, l, C_BF:C_BF + 1], -1.0, None, ALU.mult))
        for f_ in fns:
            S.add("dve", f_, reads=[modB[l], ppB], writes=[scB[l]])
        release(m0)

    def norm_phase(X, XB, l, aoff, shoff):
        m0 = mark()
        xb = [alloc([128, T], F32, "xb") for _ in range(3)]
        xbB = [Buf(f"xb{i}") for i in range(3)]
        sq = [alloc([128, T], BF16, "sq") for _ in range(2)]
        sqB = [Buf(f"sq{i}") for i in range(2)]
        rstd = alloc([128, T], F32, "rstd")
        rstdB = Buf("rstd")
        tmp = [alloc([128, T], F32, "ntmp") for _ in range(2)]
        tmpB = [Buf(f"ntmp{i}") for i in range(2)]
        for c in range(NCH):
            s_ = c % 3
            S.add("sp", DMA(xb[s_][:], X[c]), reads=[XB[c]], writes=[xbB[s_]], dma=True, lane=f"xb{s_}")
            S.add("act", ACT(sq[c % 2][:], xb[s_][:], AF.Square), reads=[xbB[s_]], writes=[sqB[c % 2]])
            S.add("pe", [MM(ps[tt][:, :], ones_bf[:, :], sq[c % 2][:, ts(tt)], c == 0, c == NCH - 1) for tt in range(4)],
                  reads=[sqB[c % 2], onesB], writes=[psB[0], psB[1], psB[2], psB[3]])
        for tt in range(4):
            S.add("act", ACT(rstd[:, ts(tt)], ps[tt][:, :], AF.Sqrt, bias=eps_t[:, 0:1], scale=1.0 / D),
                  reads=[psB[tt], onesB], writes=[rstdB])
        S.add("dve", RCP(rstd[:], rstd[:]), reads=[rstdB], writes=[rstdB])
        for c in range(NCH):
            s_ = c % 3
            S.add("sp", DMA(xb[s_][:], X[c]), reads=[XB[c]], writes=[xbB[s_]], dma=True, lane=f"xb{s_}")
            S.add("dve", TT(tmp[c % 2][:], xb[s_][:], rstd[:], ALU.mult), reads=[xbB[s_], rstdB], writes=[tmpB[c % 2]])
            S.add("act", ACT(h[:, c, :], tmp[c % 2][:], AF.Identity, bias=modT[:, l, shoff + c:shoff + c + 1],
                             scale=sc[:, l, aoff + c:aoff + c + 1]),
                  reads=[tmpB[c % 2], modB[l], scB[l]], writes=[hB[c]])
        release(m0)

    def proj_fm(Wt, WB, col0, m, tt, b, rows=128):
        S.add("pe", [MM(ps[b][0:rows, :], Wt[:, kc, col0:col0 + rows], h[:, kc, ts(tt)], kc == 0, kc == KC - 1)
                     for kc in range(KC)], reads=hB + [WB], writes=[psB[b]])

    def mixer_phase(l, Xsrc, XsrcB, Xdst, XdstB):
        mR = mark()
        ya = alloc([128, 4, T], BF16, "yatt")
        yaB = Buf("yatt")
        mA = mark()
        V = alloc([128, 16, 512], BF16, "V")
        VB = Buf("V")
        Wv, WvB = wload(colblk(w_in[l], V0))
        for tc in range(16):
            b = bank(0, 4)
            S.add("pe", [MM(ps[b][:, :], h[:, kc, ts(tc, 128)], Wv[:, kc, :], kc == 0, kc == KC - 1) for kc in range(KC)],
                  reads=hB + [WvB], writes=[psB[b]])
            if tc % 2 == 0:
                S.add("act", ACT(V[:, tc, :], ps[b][:, :], AF.Identity), reads=[psB[b]], writes=[VB])
            else:
                S.add("dve", CP(V[:, tc, :], ps[b][:, :]), reads=[psB[b]], writes=[VB])
        mF = mark()
        wf = alloc([128, KC, 8], BF16, "wf")
        wfB = Buf("wf")
        fA = alloc([8, T], F32, "fA")
        fC = alloc([8, T], F32, "fC")
        fO = alloc([8, T], F32, "fO")
        fR = alloc([8, T], F32, "fR")
        ksp = alloc([8, 3, T], BF16, "ksp")
        qsp = alloc([8, 3, T], BF16, "qsp")
        fB_ = Buf("fstuff")
        S.add("pool", DMA(wf[:], colblk(w_in[l], F0, 8)), writes=[wfB], dma=True, lane="wf")
        S.add("dve", MS(fO[:], 1.0), writes=[fB_])
        for tt in range(4):
            b = bank(0, 4)
            S.add("pe", [MM(ps[b][0:8, :], wf[:, kc, :], h[:, kc, ts(tt)], kc == 0, kc == KC - 1) for kc in range(KC)],
                  reads=hB + [wfB], writes=[psB[b]])
            S.add("act", ACT(fA[:, ts(tt)], ps[b][0:8, :], AF.Exp, bias=sc[0:8, l, 34:35], scale=-1.0),
                  reads=[psB[b], scB[l]], writes=[fB_])
        S.add("act", ACT(fA[:], fA[:], AF.Ln, bias=1.0), reads=[fB_], writes=[fB_])
        S.add("dve", lambda e: e.tensor_tensor_scan(fC[:], fO[:], fA[:], 0.0, ALU.mult, ALU.add), reads=[fB_], writes=[fB_])
        for f_ in (CP(ksp[:, 0, :], fC[:]), TT(fR[:], fC[:], ksp[:, 0, :], ALU.subtract), CP(ksp[:, 1, :], fR[:]),
                   TT(fR[:], fR[:], ksp[:, 1, :], ALU.subtract), CP(ksp[:, 2, :], fR[:]),
                   TS(qsp[:, 0, :], ksp[:, 0, :], -1.0, None, ALU.mult), TS(qsp[:, 1, :], ksp[:, 1, :], -1.0, None, ALU.mult),
                   TS(qsp[:, 2, :], ksp[:, 2, :], -1.0, None, ALU.mult)):
            S.add("dve", f_, reads=[fB_], writes=[fB_])
        S.add("sp", DMA(cumsc[0], ksp[:]), reads=[fB_], writes=[cumB], dma=True, lane="ksp")
        S.add("sp", DMA(cumsc[1], qsp[:]), reads=[fB_], writes=[cumB], dma=True, lane="qsp")
        if dbg is not None and l == 0:
            S.add("dve", CP(dbgt[0:8, 128:192], fC[:, 0:64]), reads=[fB_], writes=[dbgtB])
            S.add("dve", CP(dbgt[0:8, 192:256], fC[:, 1984:2048]), reads=[fB_], writes=[dbgtB])
        release(mF)
        qa = [alloc([70, T], BF16, "qa") for _ in range(4)]
        ka = [alloc([70, T], BF16, "ka") for _ in range(4)]
        qaB = [Buf(f"qa{i}") for i in range(4)]
        kaB = [Buf(f"ka{i}") for i in range(4)]
        sqq = [alloc([64, 512], BF16, "sqq") for _ in range(2)]
        sqqB = [Buf(f"sqq{i}") for i in range(2)]
        rs = [alloc([64, 512], F32, "rs") for _ in range(2)]
        rsB = [Buf(f"rs{i}") for i in range(2)]
        pt = [alloc([128, 512], BF16, "pt") for _ in range(3)]
        ptB = [Buf(f"pt{i}") for i in range(3)]
        rden = [alloc([128, 512], F32, "rden") for _ in range(2)]
        rdenB = [Buf(f"rden{i}") for i in range(2)]
        cnt = {"s": 0, "p": 0, "o": 0, "m": 0}
        sm = [alloc([128, 512], F32, "sm") for _ in range(2)]
        smB = [Buf(f"sm{i}") for i in range(2)]
        for half in range(2):
            Wq, WqB = wload(colblk(w_in[l], Q0 + half * 256, 256), "c256")
            Wk, WkB = wload(colblk(w_in[l], K0 + half * 256, 256), "c256")
            for i in range(4):
                hd = half * 4 + i
                S.add("dve", [MS(qa[i][64:70, :], 1.0)], writes=[qaB[i]])
                S.add("dve", [MS(ka[i][64:70, :], 1.0)], writes=[kaB[i]])
                S.add("sp", DMA(ka[i][64:67, :], cumsc[0, hd]), reads=[cumB], writes=[kaB[i]], dma=True, lane=f"ka{i}")
                S.add("sp", DMA(qa[i][67:70, :], cumsc[1, hd]), reads=[cumB], writes=[qaB[i]], dma=True, lane=f"qa{i}")
                for (dst, dstB, Wt, WB, gcol) in ((qa[i], qaB[i], Wq, WqB, 32), (ka[i], kaB[i], Wk, WkB, 33)):
                    for tt in range(4):
                        b = bank(0, 2)
                        proj_fm(Wt, WB, i * 64, None, tt, b, rows=64)
                        s_ = cnt["s"] % 2
                        cnt["s"] += 1
                        S.add("act", ACT(sqq[s_][:], ps[b][0:64, :], AF.Square), reads=[psB[b]], writes=[sqqB[s_]])
                        b2 = 2 + s_
                        S.add("pe", MM(ps[b2][0:64, :], ones_bf[0:64, 0:64], sqq[s_][:], True, True),
                              reads=[sqqB[s_], onesB], writes=[psB[b2]])
                        S.add("act", ACT(rs[s_][:], ps[b2][0:64, :], AF.Sqrt, bias=eps_t[0:64, 0:1], scale=1.0 / 64),
                              reads=[psB[b2], onesB], writes=[rsB[s_]])
                        S.add("dve", RCP(rs[s_][:], rs[s_][:]), reads=[rsB[s_]], writes=[rsB[s_]])
                        S.add("dve", STT(dst[0:64, ts(tt)], ps[b][0:64, :], sc[0:64, l, gcol:gcol + 1], rs[s_][:],
                                         ALU.mult, ALU.mult), reads=[psB[b], rsB[s_], scB[l]], writes=[dstB])
            if dbg is not None and l == 0 and half == 0:
                S.add("dve", CP(dbgt[0:70, 256:320], qa[0][0:70, 0:64]), reads=[qaB[0]], writes=[dbgtB])
                S.add("dve", CP(dbgt[0:70, 320:384], ka[0][0:70, 0:64]), reads=[kaB[0]], writes=[dbgtB])
            for i in range(4):
                hd = half * 4 + i
                pr = 64 * (hd % 2)
                for j in range(4):
                    o_ = cnt["o"] % 2
                    cnt["o"] += 1
                    ob, db = 4 + o_, 6 + o_
                    nk = 4 * j + 4
                    for ik in range(nk):
                        sbk = 2 + (cnt["p"] % 2)
                        p_ = cnt["p"] % 3
                        cnt["p"] += 1
                        S.add("pe", MM(ps[sbk][:, :], ka[i][0:70, ts(ik, 128)], qa[i][0:70, ts(j)], True, True),
                              reads=[kaB[i], qaB[i]], writes=[psB[sbk]])
                        if ik >= 4 * j:
                            m_ = cnt["m"] % 2
                            cnt["m"] += 1
                            S.add("dve", TT(sm[m_][:], ps[sbk][:, :], masks[:, ik - 4 * j, :], ALU.add),
                                  reads=[psB[sbk], masksB], writes=[smB[m_]])
                            S.add("act", ACT(pt[p_][:], sm[m_][:], AF.Exp), reads=[smB[m_]], writes=[ptB[p_]])
                        else:
                            S.add("act", ACT(pt[p_][:], ps[sbk][:, :], AF.Exp), reads=[psB[sbk]], writes=[ptB[p_]])
                        S.add("pe", [MM(ps[ob][pr:pr + 64, :], V[:, ik, hd * 64:(hd + 1) * 64], pt[p_][:], ik == 0, ik == nk - 1),
                                     MM(ps[db][pr:pr + 64, :], ones_bf[:, 0:64], pt[p_][:], ik == 0, ik == nk - 1)],
                              reads=[VB, ptB[p_], onesB], writes=[psB[ob], psB[db]])
                    S.add("dve", RCP(rden[o_][pr:pr + 64, :], ps[db][pr:pr + 64, :]), reads=[psB[db]], writes=[rdenB[o_]])
                    S.add("dve", TT(ya[pr:pr + 64, hd // 2, ts(j)], ps[ob][pr:pr + 64, :], rden[o_][pr:pr + 64, :], ALU.mult),
                          reads=[psB[ob], rdenB[o_]], writes=[yaB])
        release(mA)
        if dbg is not None and l == 0:
            S.add("dve", CP(dbgt[:, 0:64], ya[:, 0, 0:64]), reads=[yaB], writes=[dbgtB])
            S.add("dve", CP(dbgt[:, 64:128], ya[:, 3, 1984:2048]), reads=[yaB], writes=[dbgtB])

        yc = alloc([128, 4, T], BF16, "yconf")
        ycB = Buf("yconf")
        mC = mark()
        Wa, WaB = wload(colblk(w_in[l], A0))
        Wg, WgB = wload(colblk(w_in[l], G0))
        u = alloc([128, 30 + T], F32, "u")
        uB = Buf("u")
        v = [alloc([128, T], F32, "v") for _ in range(4)]
        vB = [Buf(f"v{i}") for i in range(4)]
        sg = [alloc([128, 512], F32, "sg") for _ in range(2)]
        sgB = [Buf(f"sg{i}") for i in range(2)]
        S.add("dve", MS(u[:, 0:30], 0.0), writes=[uB])
        k_ = 0
        for c in range(4):
            for tt in range(4):
                ba = bank(0, 4)
                proj_fm(Wa, WaB, c * 128, None, tt, ba)
                bg = bank(4, 8)
                proj_fm(Wg, WgB, c * 128, None, tt, bg)
                s_ = k_ % 2
                k_ += 1
                S.add("act", ACT(sg[s_][:], ps[bg][:, :], AF.Sigmoid), reads=[psB[bg]], writes=[sgB[s_]])
                S.add("dve", TT(u[:, 30 + tt * 512:30 + (tt + 1) * 512], ps[ba][:, :], sg[s_][:], ALU.mult),
                      reads=[psB[ba], sgB[s_]], writes=[uB])
            dwc = C_DW + c * 31
            S.add("dve", TS(v[c][:], u[:, 30:30 + T], pp[:, l, dwc + 30:dwc + 31], pp[:, l, C_DB + c:C_DB + c + 1],
                            ALU.mult, ALU.add), reads=[uB, ppB], writes=[vB[c]])
            for k in range(30):
                S.add("dve", STT(v[c][:], u[:, k:k + T], pp[:, l, dwc + k:dwc + k + 1], v[c][:], ALU.mult, ALU.add),
                      reads=[uB, ppB, vB[c]], writes=[vB[c]])
        vb = alloc([128, 4, 512], BF16, "vb")
        vs = alloc([128, 4, 512], BF16, "vs")
        vbB, vsB = Buf("vb"), Buf("vs")
        st_ = [alloc([128, 512], F32, f"lnst{i}") for i in range(3)]
        stB = [Buf(f"lnst{i}") for i in range(3)]
        t1 = [alloc([128, 512], F32, "t1") for _ in range(2)]
        t1B = [Buf(f"t1{i}") for i in range(2)]
        for tt in range(4):
            for c in range(4):
                S.add("act", ACT(vb[:, c, :], v[c][:, ts(tt)], AF.Identity), reads=[vB[c]], writes=[vbB])
                S.add("act", ACT(vs[:, c, :], v[c][:, ts(tt)], AF.Square), reads=[vB[c]], writes=[vsB])
            bm, bq = bank(0, 4), bank(4, 8)
            S.add("pe", [MM(ps[bm][:, :], ones_bf[:, :], vb[:, c, :], c == 0, c == 3) for c in range(4)],
                  reads=[vbB, onesB], writes=[psB[bm]])
            S.add("pe", [MM(ps[bq][:, :], ones_bf[:, :], vs[:, c, :], c == 0, c == 3) for c in range(4)],
                  reads=[vsB, onesB], writes=[psB[bq]])
            mu, musq, var = st_
            S.add("act", ACT(mu[:], ps[bm][:, :], AF.Identity, scale=1.0 / 512), reads=[psB[bm]], writes=[stB[0]])
            S.add("act", ACT(musq[:], ps[bm][:, :], AF.Square, scale=1.0 / 512), reads=[psB[bm]], writes=[stB[1]])
            S.add("dve", STT(var[:], ps[bq][:, :], 1.0 / 512, musq[:], ALU.mult, ALU.subtract),
                  reads=[psB[bq], stB[1]], writes=[stB[2]])
            S.add("act", ACT(var[:], var[:], AF.Sqrt, bias=eps_t[:, 1:2]), reads=[stB[2], onesB], writes=[stB[2]])
            S.add("dve", RCP(var[:], var[:]), reads=[stB[2]], writes=[stB[2]])
            for c in range(4):
                s_ = c % 2
                S.add("dve", TT(t1[s_][:], v[c][:, ts(tt)], mu[:], ALU.subtract), reads=[vB[c], stB[0]], writes=[t1B[s_]])
                S.add("dve", TT(t1[s_][:], t1[s_][:], var[:], ALU.mult), reads=[t1B[s_], stB[2]], writes=[t1B[s_]])
                S.add("act", ACT(yc[:, c, ts(tt)], t1[s_][:], AF.Silu, bias=pp[:, l, C_LNB + c:C_LNB + c + 1],
                                 scale=pp[:, l, C_LNG + c:C_LNG + c + 1]), reads=[t1B[s_], ppB], writes=[ycB])
        if dbg is not None and l == 0:
            S.add("dve", CP(dbgt[:, 384:448], yc[:, 0, 0:64]), reads=[ycB], writes=[dbgtB])
        release(mC)

        ysc = alloc([128, 4, T], BF16, "ysc")
        yscB = Buf("ysc")
        mS = mark()
        Wx, WxB = wload(colblk(w_in[l], SX0))
        Wc, WcB = wload(colblk(w_in[l], SC0))
        vp = alloc([128, 2 + T], F32, "vp")
        vpB = Buf("vp")
        cv = [alloc([128, T], F32, "cv") for _ in range(4)]
        cvB = [Buf(f"cv{i}") for i in range(4)]
        xt = [alloc([128, 512], F32, "xt") for _ in range(2)]
        xtB = [Buf(f"xt{i}") for i in range(2)]
        S.add("dve", MS(vp[:, 0:2], 0.0), writes=[vpB])
        k_ = 0
        for c in range(4):
            for tt in range(4):
                bx = bank(0, 4)
                proj_fm(Wx, WxB, c * 128, None, tt, bx)
                bc = bank(4, 8)
                proj_fm(Wc, WcB, c * 128, None, tt, bc)
                s_ = k_ % 2
                k_ += 1
                S.add("act", ACT(xt[s_][:], ps[bx][:, :], AF.Identity), reads=[psB[bx]], writes=[xtB[s_]])
                S.add("dve", TT(vp[:, 2 + tt * 512:2 + (tt + 1) * 512], ps[bc][:, :], xt[s_][:], ALU.mult),
                      reads=[psB[bc], xtB[s_]], writes=[vpB])
            swc = C_SW + c * 3
            S.add("dve", TS(cv[c][:], vp[:, 2:2 + T], pp[:, l, swc + 2:swc + 3], None, ALU.mult), reads=[vpB, ppB], writes=[cvB[c]])
            S.add("dve", STT(cv[c][:], vp[:, 1:1 + T], pp[:, l, swc + 1:swc + 2], cv[c][:], ALU.mult, ALU.add),
                  reads=[vpB, ppB, cvB[c]], writes=[cvB[c]])
            S.add("dve", STT(cv[c][:], vp[:, 0:T], pp[:, l, swc:swc + 1], cv[c][:], ALU.mult, ALU.add),
                  reads=[vpB, ppB, cvB[c]], writes=[cvB[c]])
        Wb, WbB = wload(colblk(w_in[l], SB0))
        for c in range(4):
            for tt in range(4):
                bb = bank(0, 8)
                proj_fm(Wb, WbB, c * 128, None, tt, bb)
                S.add("dve", TT(ysc[:, c, ts(tt)], ps[bb][:, :], cv[c][:, ts(tt)], ALU.mult),
                      reads=[psB[bb], cvB[c]], writes=[yscB])
        if dbg is not None and l == 0:
            S.add("dve", CP(dbgt[:, 512:576], ysc[:, 0, 0:64]), reads=[yscB], writes=[dbgtB])
        release(mS)

        ypl = alloc([128, 4, T], BF16, "ypool")
        yplB = Buf("ypool")
        mP = mark()
        Wp, WpB = wload(colblk(w_in[l], P0))
        pw = alloc([128, 4, 128], BF16, "poolw")
        pwB = Buf("poolw")
        S.add("pool", DMA(pw[:], pool_w[l].rearrange("g c d -> c g d")), writes=[pwB], dma=True, lane="pw")
        pb = alloc([128, 16 + T], F32, "pb")
        PA = alloc([128, 16 + T], F32, "PA")
        PBt = alloc([128, 16 + T], F32, "PB")
        pbB, PAB, PBB = Buf("pb"), Buf("PA"), Buf("PB")
        dbf = alloc([128, T], BF16, "dbf")
        dbfB = Buf("dbf")
        S.add("dve", [MS(pb[:, 0:16], 0.0)], writes=[pbB])
        S.add("dve", [MS(PA[:, 0:16], 0.0)], writes=[PAB])
        S.add("dve", [MS(PBt[:, 0:16], 0.0)], writes=[PBB])
        for g in range(4):
            wdw = POOLW[g]
            for tt in range(4):
                b = bank(0, 8)
                proj_fm(Wp, WpB, g * 128, None, tt, b)
                S.add("act", ACT(pb[:, 16 + tt * 512:16 + (tt + 1) * 512], ps[b][:, :], AF.Identity), reads=[psB[b]], writes=[pbB])
            src, srcB = pb, pbB
            bufs = [(PA, PAB), (PBt, PBB)]
            for lev in range(g + 1):
                d_ = 1 << lev
                dst, dstB = bufs[lev % 2]
                S.add("dve", TT(dst[:, 16:16 + T], src[:, 16:16 + T], src[:, 16 - d_:16 - d_ + T], ALU.add),
                      reads=[srcB], writes=[dstB])
                src, srcB = dst, dstB
            S.add("dve", TS(src[:, 16:16 + T], src[:, 16:16 + T], 1.0 / wdw, None, ALU.mult), reads=[srcB], writes=[srcB])
            S.add("dve", TT(src[:, 16:32], src[:, 16:32], pp[:, l, C_CORR + g * 16:C_CORR + (g + 1) * 16], ALU.mult),
                  reads=[srcB, ppB], writes=[srcB])
            S.add("dve", TT(dbf[:], src[:, 16:16 + T], pb[:, 16:16 + T], ALU.subtract), reads=[srcB, pbB], writes=[dbfB])
            for tt in range(4):
                b = bank(0, 8)
                S.add("pe", MM(ps[b][:, :], pw[:, g, :], dbf[:, ts(tt)], True, True), reads=[pwB, dbfB], writes=[psB[b]])
                S.add("act", ACT(ypl[:, g, ts(tt)], ps[b][:, :], AF.Identity, scale=pp[:, l, C_PSC + g:C_PSC + g + 1]),
                      reads=[psB[b], ppB], writes=[yplB])
        if dbg is not None and l == 0:
            S.add("dve", CP(dbgt[:, 448:512], ypl[:, 3, 0:64]), reads=[yplB], writes=[dbgtB])
            S.add("dve", CP(dbgt[:, 576:640], h[:, 0, 0:64]), reads=hB, writes=[dbgtB])
        release(mP)

        mM = mark()
        acc = [[alloc([128, 512], F32, "acc") for _ in range(4)] for _ in range(2)]
        accB = [[Buf(f"acc{a}{b_}") for b_ in range(4)] for a in range(2)]
        wbr = [alloc([128, 4, 256], BF16, "wbr") for _ in range(2)]
        wbrB = [Buf(f"wbr{i}") for i in range(2)]
        gt = [alloc([128, 512], F32, "gt") for _ in range(2)]
        gtB = [Buf(f"gt{i}") for i in range(2)]
        prod = [alloc([128, 512], F32, "prod") for _ in range(2)]
        prodB = [Buf(f"prod{i}") for i in range(2)]
        stg = [alloc([128, 512], BF16, "stg") for _ in range(2)]
        stgB = [Buf(f"stg{i}") for i in range(2)]
        ys = [(ya, yaB), (yc, ycB), (ypl, yplB), (ysc, yscB)]
        k_ = 0
        kb = 0
        ks = 0
        for mg in range(8):
            for br in range(4):
                Wgt, WgtB = wload(colblk(w_in[l], GATE0 + br * 2048 + mg * 256, 256), "c256")
                wb_ = kb % 2
                kb += 1
                S.add("pool", DMA(wbr[wb_][:], w_br[l, br][:, mg * 256:(mg + 1) * 256].rearrange("(kc p) n -> p kc n", p=128)),
                      writes=[wbrB[wb_]], dma=True, lane=f"wbr{wb_}")
                yt_, ytB = ys[br]
                for m2 in range(2):
                    m = mg * 2 + m2
                    for tt in range(4):
                        bg = bank(0, 4)
                        proj_fm(Wgt, WgtB, m2 * 128, None, tt, bg)
                        bp = bank(4, 8)
                        S.add("pe", [MM(ps[bp][:, :], wbr[wb_][:, k2, m2 * 128:(m2 + 1) * 128], yt_[:, k2, ts(tt)], k2 == 0, k2 == 3)
                                     for k2 in range(4)], reads=[wbrB[wb_], ytB], writes=[psB[bp]])
                        s_ = k_ % 2
                        k_ += 1
                        bgc = C_BGATE + br * 16 + m
                        S.add("act", ACT(gt[s_][:], ps[bg][:, :], AF.Sigmoid, bias=pp[:, l, bgc:bgc + 1]),
                              reads=[psB[bg], ppB], writes=[gtB[s_]])
                        a_, aB = acc[m2][tt], accB[m2][tt]
                        if br == 0:
                            S.add("dve", TT(a_[:], ps[bp][:, :], gt[s_][:], ALU.mult), reads=[psB[bp], gtB[s_]], writes=[aB])
                        else:
                            S.add("dve", TT(prod[s_][:], ps[bp][:, :], gt[s_][:], ALU.mult), reads=[psB[bp], gtB[s_]], writes=[prodB[s_]])
                            if br < 3:
                                S.add("dve", TT(a_[:], a_[:], prod[s_][:], ALU.add), reads=[aB, prodB[s_]], writes=[aB])
                            else:
                                g_ = ks % 2
                                ks += 1
                                S.add("dve", TT(stg[g_][:], a_[:], prod[s_][:], ALU.add), reads=[aB, prodB[s_]], writes=[stgB[g_]])
                                S.add("sp", DMA(mrg[m][:, ts(tt)], stg[g_][:]), reads=[stgB[g_]], writes=[mrgB[m]], dma=True, lane=f"stg{g_}")
        release(mM)
        release(mR)
        if dbg is not None and l == 0:
            S.add("sp", DMA(h[:, 0, :], mrg[0]), reads=[mrgB[0]], writes=[hB[0]], dma=True, lane="h0")
            S.add("dve", CP(dbgt[:, 640:704], h[:, 0, 0:64]), reads=hB, writes=[dbgtB])

        m0 = mark()
        for c in range(NCH):
            S.add("sp", DMA(h[:, c, :], mrg[c]), reads=[mrgB[c]], writes=[hB[c]], dma=True, lane=f"h{c % 4}")
        xb = [alloc([128, T], F32, "xo") for _ in range(3)]
        xbB = [Buf(f"xo{i}") for i in range(3)]
        for nb in range(4):
            Wo, WoB = wload(colblk(w_out[l], nb * 512))
            for m4 in range(4):
                m = nb * 4 + m4
                s_ = m % 3
                S.add("sp", DMA(xb[s_][:], Xsrc[m]), reads=[XsrcB[m]], writes=[xbB[s_]], dma=True, lane=f"xo{s_}")
                for tt in range(4):
                    b = bank(0, 8)
                    proj_fm(Wo, WoB, m4 * 128, None, tt, b)
                    S.add("dve", STT(xb[s_][:, ts(tt)], ps[b][:, :], modT[:, l, 32 + m:33 + m], xb[s_][:, ts(tt)], ALU.mult, ALU.add),
                          reads=[psB[b], modB[l], xbB[s_]], writes=[xbB[s_]])
                S.add("sp", DMA(Xdst[m], xb[s_][:]), reads=[xbB[s_]], writes=[XdstB[m]], dma=True, lane=f"xo{s_}")
        release(m0)

    def mlp_phase(l, Xsrc, XsrcB, Xdst, XdstB):
        m0 = mark()
        acc = [alloc([128, 1024], F32, "macc") for _ in range(NCH)]
        accB = [Buf(f"macc{i}") for i in range(NCH)]
        hid = [alloc([128, 8, 1024], BF16, "hid") for _ in range(1)]
        hidB = [Buf(f"hid{i}") for i in range(1)]
        xb = [alloc([128, 1024], F32, "xm") for _ in range(2)]
        xbB = [Buf(f"xm{i}") for i in range(2)]
        rl = [alloc([128, 512], F32, "rl") for _ in range(2)]
        rlB = [Buf(f"rl{i}") for i in range(2)]
        kk = 0
        for th in range(2):
            for G in range(8):
                hs = 0
                for jb in range(2):
                    W1, W1B = wload(colblk(w_m1[l], G * 1024 + jb * 512))
                    for j4 in range(4):
                        j = jb * 4 + j4
                        for t2 in range(2):
                            tt = th * 2 + t2
                            b = bank(0, 4)
                            proj_fm(W1, W1B, j4 * 128, None, tt, b)
                            r_ = kk % 2
                            kk += 1
                            S.add("act", ACT(rl[r_][:], ps[b][:, :], AF.Relu), reads=[psB[b]], writes=[rlB[r_]])
                            S.add("dve", TT(hid[hs][:, j, ts(t2)], rl[r_][:], rl[r_][:], ALU.mult), reads=[rlB[r_]], writes=[hidB[hs]])
                for nb in range(4):
                    W2, W2B = wload(w_m2[l][G * 1024:(G + 1) * 1024, nb * 512:(nb + 1) * 512].rearrange("(j p) n -> p j n", p=128), "r8")
                    for m4 in range(4):
                        m = nb * 4 + m4
                        for t2 in range(2):
                            b = bank(4, 8)
                            S.add("pe", [MM(ps[b][:, :], W2[:, j, m4 * 128:(m4 + 1) * 128], hid[hs][:, j, ts(t2)], j == 0, j == 7)
                                         for j in range(8)], reads=[W2B, hidB[hs]], writes=[psB[b]])
                            if G == 0:
                                S.add("act", ACT(acc[m][:, ts(t2)], ps[b][:, :], AF.Identity), reads=[psB[b]], writes=[accB[m]])
                            else:
                                S.add("dve", TT(acc[m][:, ts(t2)], acc[m][:, ts(t2)], ps[b][:, :], ALU.add),
                                      reads=[psB[b], accB[m]], writes=[accB[m]])
            for m in range(NCH):
                s_ = kk % 2
                kk += 1
                S.add("sp", DMA(xb[s_][:], Xsrc[m][:, ts(th, 1024)]), reads=[XsrcB[m]], writes=[xbB[s_]], dma=True, lane=f"xm{s_}")
                S.add("dve", STT(xb[s_][:], acc[m][:], modT[:, l, 80 + m:81 + m], xb[s_][:], ALU.mult, ALU.add),
                      reads=[accB[m], modB[l], xbB[s_]], writes=[xbB[s_]])
                S.add("sp", DMA(Xdst[m][:, ts(th, 1024)], xb[s_][:]), reads=[xbB[s_]], writes=[XdstB[m]], dma=True, lane=f"xm{s_}")
        release(m0)

    if dbg is not None:
        dbgt = alloc([128, dbg], F32, "dbgt")
        dbgtB = Buf("dbgt")
        R0 = mark()
    for l in layers:
        mod_phase(l)
    nl = len(layers)
    for li, l in enumerate(layers):
        Xa, XaB = (xin, xinB) if li == 0 else (xs2, xs2B)
        Xb, XbB = (xs1, xs1B) if li == 0 else (xs3, xs3B)
        Xc, XcB = (yout, youtB) if li == nl - 1 else (xs2, xs2B)
        norm_phase(Xa, XaB, l, 0, 0)
        mixer_phase(l, Xa, XaB, Xb, XbB)
        norm_phase(Xb, XbB, l, 16, 48)
        mlp_phase(l, Xb, XbB, Xc, XcB)
    if dbg is not None:
        S.add("sp", DMA(dbg32[:, :], dbgt[:]), reads=[dbgtB], writes=[dbgB], dma=True, lane="dbg")
    S.barrier()
    S.add("sp", [])
    semkeys = S.finalize()
    return nc, S, semkeys


def emit(nc, S, semkeys):
    from contextlib import ExitStack
    with ExitStack() as es:
        sems = {}
        for i, k in enumerate(semkeys):
            sems[k] = es.enter_context(nc.semaphore(f"sm{i}"))
        block = es.enter_context(nc.Block())
        by_eng = {e: [] for e in ENGS}
        for op in S.ops:
            by_eng[op.eng].append(op)

        def run(eng_name, eng):
            seen = {}
            for op in by_eng[eng_name]:
                waits = {}
                for d in op.deps:
                    dop = S.ops[d]
                    if dop.ev is None:
                        continue
                    if eng_name == "pe" and dop.eng == "pe" and not dop.dma:
                        continue
                    k, v = dop.ev
                    if waits.get(k, 0) < v:
                        waits[k] = v
                for k, v in waits.items():
                    if seen.get(k, 0) >= v:
                        continue
                    eng.wait_ge(sems[k], v)
                    seen[k] = v
                n = len(op.fns)
                for i, fn in enumerate(op.fns):
                    ins = fn(eng)
                    if i == n - 1 and op.sig:
                        ins.then_inc(sems[op.ev[0]], 16 if op.dma else 1)

        @block.tensor
        def _(e):
            run("pe", e)

        @block.scalar
        def _(e):
            run("act", e)

        @block.vector
        def _(e):
            run("dve", e)

        @block.gpsimd
        def _(e):
            run("pool", e)

        @block.sync
        def _(e):
            run("sp", e)
    return nc


def _pack_pp(inp):
    pp = np.zeros((L, 128, NPP), np.float32)
    for l in range(L):
        p = pp[l]
        p[:, C_BADA:C_BADA + 96] = inp["b_ada"][l].reshape(96, 128).T
        p[:, C_GAIN:C_GAIN + 16] = inp["norm_gain"][l, 0].reshape(16, 128).T
        p[:, C_GAIN + 16:C_GAIN + 32] = inp["norm_gain"][l, 1].reshape(16, 128).T
        p[:, C_BGATE:C_BGATE + 64] = inp["b_gate"][l].reshape(64, 128).T
        p[:, C_DW:C_DW + 124] = inp["conf_dw"][l].reshape(31, 4, 128).transpose(2, 1, 0).reshape(128, 124)
        p[:, C_DB:C_DB + 4] = inp["conf_db"][l].reshape(4, 128).T
        p[:, C_LNG:C_LNG + 4] = inp["conf_ln_g"][l].reshape(4, 128).T
        p[:, C_LNB:C_LNB + 4] = inp["conf_ln_b"][l].reshape(4, 128).T
        p[:, C_PSC:C_PSC + 4] = inp["pool_scale"][l].reshape(4, 128).T
        p[:, C_SW:C_SW + 12] = inp["sconv_w"][l].reshape(3, 4, 128).transpose(2, 1, 0).reshape(128, 12)
        p[0:64, C_QG] = inp["q_gain"][l]
        p[64:128, C_QG] = inp["q_gain"][l]
        p[0:64, C_KG] = inp["k_gain"][l]
        p[64:128, C_KG] = inp["k_gain"][l]
        p[0:8, C_BF] = inp["b_f"][l]
        for g, w in enumerate(POOLW):
            for t in range(16):
                p[:, C_CORR + g * 16 + t] = float(w) / float(min(t + 1, w))
    return pp


def _masks():
    k = np.arange(128)[:, None]
    q = np.arange(512)[None, :]
    return np.stack([np.where(q >= k + 128 * i, 0.0, -30000.0).astype(np.float32) for i in range(4)], axis=0)


def make_in_maps(inp, xT_list):
    pp = _pack_pp(inp)
    masks = _masks()
    shared = {
        "w_ada": np.ascontiguousarray(inp["w_ada"], dtype=np.float32),
        "w_in": np.ascontiguousarray(inp["w_in"], dtype=np.float32),
        "w_branch": np.ascontiguousarray(inp["w_branch"], dtype=np.float32),
        "w_out": np.ascontiguousarray(inp["w_out"], dtype=np.float32),
        "w_mlp1": np.ascontiguousarray(inp["w_mlp1"], dtype=np.float32),
        "w_mlp2": np.ascontiguousarray(inp["w_mlp2"], dtype=np.float32),
        "pool_w": np.ascontiguousarray(inp["pool_w"], dtype=np.float32),
        "pp": pp,
        "masks": masks,
    }
    maps = []
    for b in range(8):
        m = dict(shared)
        m["xT"] = xT_list[b]
        m["cT"] = np.ascontiguousarray(np.asarray(inp["c"][b], np.float32).reshape(KC, 128).T)
        maps.append(m)
    return maps


_NC_CACHE = {}


def get_nc(layers=(0, 1), dbg=None):
    key = (tuple(layers), dbg)
    if key not in _NC_CACHE:
        nc, S, semkeys = build_nc(layers=layers, dbg=dbg)
        emit(nc, S, semkeys)
        _NC_CACHE[key] = nc
    return _NC_CACHE[key]


def kernel(**inputs):
    inp = {k: np.asarray(v) for k, v in inputs.items()}
    x = np.asarray(inp["x"], np.float32)
    xT = [np.ascontiguousarray(x[b].T).reshape(NCH, 128, T) for b in range(8)]
    nc = get_nc((0, 1))
    maps = make_in_maps(inp, xT)
    res = run_bass_kernel_spmd(nc, maps, core_ids=list(range(8)))
    out = np.empty((8, T, D), np.float32)
    for b in range(8):
        out[b] = res.results[b]["yT"].reshape(D, T).T
    return out
```

```python
import numpy as np
import concourse.bass as bass
import concourse.mybir as mybir
from concourse.bass_utils import run_bass_kernel_spmd

F32 = mybir.dt.float32
BF16 = mybir.dt.bfloat16
AF = mybir.ActivationFunctionType
ALU = mybir.AluOpType

D = 2048
T = 2048
L = 2
NCH = 16
KC = 16
IN_COLS = 12808
Q0, K0, V0, F0, A0, G0, P0, SX0, SB0, SC0, GATE0 = 0, 512, 1024, 1536, 1544, 2056, 2568, 3080, 3592, 4104, 4616
DFF = 8192
C_BADA, C_GAIN, C_BGATE, C_DW, C_DB, C_LNG, C_LNB, C_PSC, C_SW, C_QG, C_KG, C_BF, C_CORR = (
    0, 96, 128, 192, 316, 320, 324, 328, 332, 344, 345, 346, 347)
NPP = 416
POOLW = (2, 4, 8, 16)
SB_LO, SB_HI = 16512, 229344


class Buf:
    __slots__ = ("name", "w", "r")

    def __init__(self, name):
        self.name = name
        self.w = None
        self.r = []


class Op:
    __slots__ = ("eng", "fns", "deps", "dma", "lane", "ev", "sig")

    def __init__(self, eng, fns, deps, dma, lane):
        self.eng, self.fns, self.deps, self.dma, self.lane = eng, fns, deps, dma, lane
        self.ev = None
        self.sig = False


ENGS = ("pe", "act", "dve", "pool", "sp")


class Sched:
    def __init__(self):
        self.ops = []
        self.pending = {e: None for e in ENGS}
        self.last = {}
        self.dma_since = []

    def add(self, eng, fns, reads=(), writes=(), dma=False, lane=None):
        idx = len(self.ops)
        deps = set()
        for b in reads:
            if b.w is not None:
                deps.add(b.w)
        for b in writes:
            if b.w is not None:
                deps.add(b.w)
            deps.update(b.r)
        for b in reads:
            b.r.append(idx)
        for b in writes:
            b.w = idx
            b.r = []
        if self.pending[eng] is not None:
            deps.update(self.pending[eng])
            self.pending[eng] = None
        deps.discard(idx)
        self.ops.append(Op(eng, fns if isinstance(fns, list) else [fns], deps, dma, lane))
        if dma:
            self.dma_since.append(idx)
        else:
            self.last[eng] = idx
        return idx

    def barrier(self):
        deps = set(self.last.values()) | set(self.dma_since)
        for e in ENGS:
            if self.pending[e] is None:
                self.pending[e] = set(deps)
            else:
                self.pending[e] |= deps
        self.dma_since = []

    def finalize(self):
        needed = set()
        for op in self.ops:
            needed |= op.deps
        cnt = {e: 0 for e in ENGS}
        lanes = {}
        for i, op in enumerate(self.ops):
            if op.dma:
                lanes[op.lane] = lanes.get(op.lane, 0) + 16
                op.ev = (("L", op.lane), lanes[op.lane])
                op.sig = True
            elif i in needed:
                cnt[op.eng] += 1
                ep = (cnt[op.eng] - 1) // 30000
                op.ev = (("E", op.eng, ep), cnt[op.eng] - 30000 * ep)
                op.sig = True
        semkeys = []
        for op in self.ops:
            if op.ev is not None and op.ev[0] not in semkeys:
                semkeys.append(op.ev[0])
        return semkeys


def build_nc(layers=(0, 1), first=True, last=True, dbg=None):
    nc = bass.Bass("TRN2", target_bir_lowering=False)
    S = Sched()

    xin = nc.dram_tensor("xT", [NCH, 128, T], F32, kind="ExternalInput").ap()
    cT_d = nc.dram_tensor("cT", [128, KC], F32, kind="ExternalInput").ap()
    w_ada = nc.dram_tensor("w_ada", [L, D, 6 * D], F32, kind="ExternalInput").ap()
    w_in = nc.dram_tensor("w_in", [L, D, IN_COLS], F32, kind="ExternalInput").ap()
    w_br = nc.dram_tensor("w_branch", [L, 4, 512, D], F32, kind="ExternalInput").ap()
    w_out = nc.dram_tensor("w_out", [L, D, D], F32, kind="ExternalInput").ap()
    w_m1 = nc.dram_tensor("w_mlp1", [L, D, DFF], F32, kind="ExternalInput").ap()
    w_m2 = nc.dram_tensor("w_mlp2", [L, DFF, D], F32, kind="ExternalInput").ap()
    pool_w = nc.dram_tensor("pool_w", [L, 4, 128, 128], F32, kind="ExternalInput").ap()
    pp_d = nc.dram_tensor("pp", [L, 128, NPP], F32, kind="ExternalInput").ap()
    masks_d = nc.dram_tensor("masks", [4, 128, 512], F32, kind="ExternalInput").ap()
    ident_d = nc.dram_tensor("ident", [128, 128], F32, kind="ExternalInput").ap()
    yout = nc.dram_tensor("yT", [NCH, 128, T], F32, kind="ExternalOutput").ap()
    xs1 = nc.dram_tensor("xs1", [NCH, 128, T], F32, kind="Internal").ap()
    xs2 = nc.dram_tensor("xs2", [NCH, 128, T], F32, kind="Internal").ap()
    xs3 = nc.dram_tensor("xs3", [NCH, 128, T], F32, kind="Internal").ap()
    mrg = nc.dram_tensor("mrg", [NCH, 128, T], BF16, kind="Internal").ap()
    cumsc = nc.dram_tensor("cumsc", [2, 8, 3, T], BF16, kind="Internal").ap()
    if dbg is not None:
        dbg32 = nc.dram_tensor("dbg32", [128, dbg], F32, kind="ExternalOutput").ap()
        dbgB = Buf("dbg32")
    xinB = [Buf(f"xin{c}") for c in range(NCH)]
    xs1B = [Buf(f"xs1_{c}") for c in range(NCH)]
    xs2B = [Buf(f"xs2_{c}") for c in range(NCH)]
    xs3B = [Buf(f"xs3_{c}") for c in range(NCH)]
    youtB = [Buf(f"yout{c}") for c in range(NCH)]
    mrgB = [Buf(f"mrg{c}") for c in range(NCH)]
    cumB = Buf("cumsc")

    st = {"off": SB_LO, "n": 0}

    def alloc(shape, dtype, name="t"):
        per = 1
        for s_ in shape[1:]:
            per *= s_
        nbytes = per * (4 if dtype == F32 else 2)
        nbytes = (nbytes + 63) // 64 * 64
        st["n"] += 1
        t = nc.alloc_sbuf_tensor_at(f"{name}_{st['n']}", list(shape), dtype, offset=st["off"])
        st["off"] += nbytes
        assert st["off"] <= SB_HI, f"SBUF overflow at {name}: {st['off']}"
        return t

    def mark():
        return st["off"]

    def release(m):
        st["off"] = m
        S.barrier()

    ps = [nc.alloc_psum_tensor(f"ps{i}", [128, 512], F32) for i in range(8)]
    psB = [Buf(f"ps{i}") for i in range(8)]
    pst = {"i": 0}

    def bank(lo=0, hi=8):
        n = hi - lo
        b = lo + pst["i"] % n
        pst["i"] += 1
        return b

    def MM(out, lhsT, rhs, start, stop):
        return lambda e: e.matmul(out, lhsT, rhs, start=start, stop=stop)

    def ACT(out, in_, func, bias=None, scale=None):
        kw = {}
        if bias is not None:
            kw["bias"] = bias
        if scale is not None:
            kw["scale"] = scale
        return lambda e: e.activation(out, in_, func, **kw)

    def TT(out, a, b, op):
        return lambda e: e.tensor_tensor(out, a, b, op)

    def TS(out, a, s1, s2, op0, op1=None):
        if op1 is None:
            return lambda e: e.tensor_scalar(out, a, s1, None, op0)
        return lambda e: e.tensor_scalar(out, a, s1, s2, op0, op1)

    def STT(out, a, s, b, op0, op1):
        return lambda e: e.scalar_tensor_tensor(out, a, s, b, op0, op1)

    def CP(out, in_):
        return lambda e: e.tensor_copy(out, in_)

    def RCP(out, in_):
        return lambda e: e.reciprocal(out, in_)

    def MS(ap, v):
        return lambda e: e.memset(ap, v)

    def DMA(out, in_):
        return lambda e: e.dma_start(out=out, in_=in_)

    ts = lambda i, n=512: slice(i * n, (i + 1) * n)

    ones_bf = alloc([128, 128], BF16, "ones")
    onesB = Buf("ones")
    one_f = alloc([1, 2], F32, "onef")
    eps_t = alloc([128, 2], F32, "eps")
    masks = alloc([128, 4, 512], F32, "masks")
    masksB = Buf("masks")
    ident = alloc([128, 128], BF16, "ident")
    identB = Buf("ident")
    pp = alloc([128, L, NPP], F32, "pp")
    ppB = Buf("pp")
    c_f = alloc([128, KC], F32, "cf")
    c_bf = alloc([128, KC], BF16, "cbf")
    cB = Buf("c")
    modT = alloc([128, L, 96], F32, "modT")
    modB = [Buf(f"mod{l}") for l in range(L)]
    sc = alloc([128, L, 40], F32, "scal")
    scB = [Buf(f"sc{l}") for l in range(L)]
    h = alloc([128, KC, T], BF16, "h")
    hB = [Buf(f"h{c}") for c in range(KC)]
    wt = [alloc([128, KC, 512], BF16, f"w{i}") for i in range(2)]
    wB = [Buf(f"w{i}") for i in range(2)]
    wst = {"i": 0}

    def wload(src, kind="full"):
        s_ = wst["i"] % 2
        wst["i"] += 1
        if kind == "full":
            dst = wt[s_][:, :, :]
        elif kind == "c256":
            dst = wt[s_][:, :, 0:256]
        elif kind == "r8":
            dst = wt[s_][:, 0:8, :]
        S.add("pool", DMA(dst, src), writes=[wB[s_]], dma=True, lane=f"w{s_}")
        return wt[s_], wB[s_]

    def colblk(wap, c0, n=512):
        return wap[:, c0:c0 + n].rearrange("(kc p) n -> p kc n", p=128)

    R0 = mark()

    S.add("dve", MS(ones_bf[:], 1.0), writes=[onesB])
    S.add("dve", [MS(one_f[:], 1.0), MS(eps_t[:, 0:1], 1e-6), MS(eps_t[:, 1:2], 1e-5)], writes=[onesB])
    S.add("sp", DMA(masks[:], masks_d.rearrange("i k q -> k i q")), writes=[masksB], dma=True, lane="masks")
    S.add("sp", DMA(pp[:], pp_d.rearrange("l p n -> p l n")), writes=[ppB], dma=True, lane="pp")
    S.add("sp", DMA(c_f[:], cT_d), writes=[cB], dma=True, lane="c")
    S.add("pool", DMA(ident[:], ident_d), writes=[identB], dma=True, lane="ident")
    S.add("dve", CP(c_bf[:], c_f[:]), reads=[cB], writes=[cB])

    def dump(ap, col, n, bufs):
        if dbg is None:
            return
        S.add("sp", DMA(dbg32[0:ap.shape[0], col:col + n], ap), reads=bufs, writes=[dbgB], dma=True, lane="dbg")

    modrow = alloc([1, 512], F32, "modrow")
    modrowB = Buf("modrow")

    def mod_gen(l):
        bT, bR = 7, 3

        def transposes(j):
            S.add("pe", [MM(ps[bT][:, j * 4 + i:j * 4 + i + 1], modrow[0:1, ts(i, 128)], one_f[0:1, 0:1], True, True)
                         for i in range(4)], reads=[modrowB, onesB], writes=[psB[bT]])

        for j in range(24):
            if j > 0:
                transposes(j - 1)
            Wt, WB = wload(colblk(w_ada[l], j * 512))
            S.add("pe", [MM(ps[bR][0:1, :], c_bf[:, kc:kc + 1], Wt[:, kc, :], kc == 0, kc == KC - 1) for kc in range(KC)],
                  reads=[cB, WB], writes=[psB[bR]])
            S.add("act", ACT(modrow[:], ps[bR][0:1, :], AF.Identity), reads=[psB[bR]], writes=[modrowB])
            yield j
        transposes(23)
        S.add("dve", TT(modT[:, l, :], ps[bT][:, 0:96], pp[:, l, C_BADA:C_BADA + 96], ALU.add),
              reads=[psB[bT], ppB], writes=[modB[l]])
        fns = []
        for (dst0, scoff, goff) in ((0, 16, C_GAIN), (16, 64, C_GAIN + 16)):
            fns.append(TS(sc[:, l, dst0:dst0 + 16], modT[:, l, scoff:scoff + 16], 1.0, None, ALU.add))
            fns.append(TT(sc[:, l, dst0:dst0 + 16], sc[:, l, dst0:dst0 + 16], pp[:, l, goff:goff + 16], ALU.mult))
        fns.append(TS(sc[:, l, 32:33], pp[:, l, C_QG:C_QG + 1], 0.125, None, ALU.mult))
        fns.append(CP(sc[:, l, 33:34], pp[:, l, C_KG:C_KG + 1]))
        fns.append(TS(sc[:, l, 34:35], pp[:, l, C_BF:C_BF + 1], -1.0, None, ALU.mult))
        for f_ in fns:
            S.add("dve", f_, reads=[modB[l], ppB], writes=[scB[l]])
        yield 24

    def norm_phase(X, XB, l, aoff, shoff):
        m0 = mark()
        xb = [alloc([128, T], F32, "xb") for _ in range(3)]
        xbB = [Buf(f"xb{i}") for i in range(3)]
        sq = [alloc([128, T], BF16, "sq") for _ in range(2)]
        sqB = [Buf(f"sq{i}") for i in range(2)]
        rstd = alloc([128, T], F32, "rstd")
        rstdB = Buf("rstd")
        tmp = [alloc([128, T], F32, "ntmp") for _ in range(2)]
        tmpB = [Buf(f"ntmp{i}") for i in range(2)]
        for c in range(NCH):
            s_ = c % 3
            S.add("sp", DMA(xb[s_][:], X[c]), reads=[XB[c]], writes=[xbB[s_]], dma=True, lane=f"xb{s_}")
            S.add("act", ACT(sq[c % 2][:], xb[s_][:], AF.Square), reads=[xbB[s_]], writes=[sqB[c % 2]])
            S.add("pe", [MM(ps[tt][:, :], ones_bf[:, :], sq[c % 2][:, ts(tt)], c == 0, c == NCH - 1) for tt in range(4)],
                  reads=[sqB[c % 2], onesB], writes=[psB[0], psB[1], psB[2], psB[3]])
        for tt in range(4):
            S.add("act", ACT(rstd[:, ts(tt)], ps[tt][:, :], AF.Sqrt, bias=eps_t[:, 0:1], scale=1.0 / D),
                  reads=[psB[tt], onesB], writes=[rstdB])
        S.add("dve", RCP(rstd[:], rstd[:]), reads=[rstdB], writes=[rstdB])
        for c in range(NCH):
            s_ = c % 3
            S.add("sp", DMA(xb[s_][:], X[c]), reads=[XB[c]], writes=[xbB[s_]], dma=True, lane=f"xb{s_}")
            S.add("dve", TT(tmp[c % 2][:], xb[s_][:], rstd[:], ALU.mult), reads=[xbB[s_], rstdB], writes=[tmpB[c % 2]])
            S.add("act", ACT(h[:, c, :], tmp[c % 2][:], AF.Identity, bias=modT[:, l, shoff + c:shoff + c + 1],
                             scale=sc[:, l, aoff + c:aoff + c + 1]),
                  reads=[tmpB[c % 2], modB[l], scB[l]], writes=[hB[c]])
        release(m0)

    def proj_fm(Wt, WB, col0, m, tt, b, rows=128):
        S.add("pe", [MM(ps[b][0:rows, :], Wt[:, kc, col0:col0 + rows], h[:, kc, ts(tt)], kc == 0, kc == KC - 1)
                     for kc in range(KC)], reads=hB + [WB], writes=[psB[b]])

    def mixer_phase(l, Xsrc, XsrcB, Xdst, XdstB):
        mR = mark()
        ya = alloc([128, 4, T], BF16, "yatt")
        yaB = Buf("yatt")
        mA = mark()
        V = alloc([128, 16, 512], BF16, "V")
        VB = Buf("V")
        Wv, WvB = wload(colblk(w_in[l], V0))
        for tc in range(16):
            b = bank(0, 4)
            S.add("pe", [MM(ps[b][:, :], h[:, kc, ts(tc, 128)], Wv[:, kc, :], kc == 0, kc == KC - 1) for kc in range(KC)],
                  reads=hB + [WvB], writes=[psB[b]])
            if tc % 2 == 0:
                S.add("act", ACT(V[:, tc, :], ps[b][:, :], AF.Identity), reads=[psB[b]], writes=[VB])
            else:
                S.add("dve", CP(V[:, tc, :], ps[b][:, :]), reads=[psB[b]], writes=[VB])
        mF = mark()
        wf = alloc([128, KC, 8], BF16, "wf")
        wfB = Buf("wf")
        fA = alloc([8, T], F32, "fA")
        fC = alloc([8, T], F32, "fC")
        fO = alloc([8, T], F32, "fO")
        fR = alloc([8, T], F32, "fR")
        ksp = alloc([8, 3, T], BF16, "ksp")
        qsp = alloc([8, 3, T], BF16, "qsp")
        fB_ = Buf("fstuff")
        S.add("pool", DMA(wf[:], colblk(w_in[l], F0, 8)), writes=[wfB], dma=True, lane="wf")
        S.add("dve", MS(fO[:], 1.0), writes=[fB_])
        for tt in range(4):
            b = bank(0, 4)
            S.add("pe", [MM(ps[b][0:8, :], wf[:, kc, :], h[:, kc, ts(tt)], kc == 0, kc == KC - 1) for kc in range(KC)],
                  reads=hB + [wfB], writes=[psB[b]])
            S.add("act", ACT(fA[:, ts(tt)], ps[b][0:8, :], AF.Exp, bias=sc[0:8, l, 34:35], scale=-1.0),
                  reads=[psB[b], scB[l]], writes=[fB_])
        S.add("act", ACT(fA[:], fA[:], AF.Ln, bias=1.0), reads=[fB_], writes=[fB_])
        S.add("dve", lambda e: e.tensor_tensor_scan(fC[:], fO[:], fA[:], 0.0, ALU.mult, ALU.add), reads=[fB_], writes=[fB_])
        for f_ in (CP(ksp[:, 0, :], fC[:]), TT(fR[:], fC[:], ksp[:, 0, :], ALU.subtract), CP(ksp[:, 1, :], fR[:]),
                   TT(fR[:], fR[:], ksp[:, 1, :], ALU.subtract), CP(ksp[:, 2, :], fR[:]),
                   TS(qsp[:, 0, :], ksp[:, 0, :], -1.0, None, ALU.mult), TS(qsp[:, 1, :], ksp[:, 1, :], -1.0, None, ALU.mult),
                   TS(qsp[:, 2, :], ksp[:, 2, :], -1.0, None, ALU.mult)):
            S.add("dve", f_, reads=[fB_], writes=[fB_])
        S.add("sp", DMA(cumsc[0], ksp[:]), reads=[fB_], writes=[cumB], dma=True, lane="ksp")
        S.add("sp", DMA(cumsc[1], qsp[:]), reads=[fB_], writes=[cumB], dma=True, lane="qsp")
        if dbg is not None and l == 0:
            S.add("dve", CP(dbgt[0:8, 128:192], fC[:, 0:64]), reads=[fB_], writes=[dbgtB])
            S.add("dve", CP(dbgt[0:8, 192:256], fC[:, 1984:2048]), reads=[fB_], writes=[dbgtB])
        release(mF)
        qa = [alloc([70, T], BF16, "qa") for _ in range(4)]
        ka = [alloc([70, T], BF16, "ka") for _ in range(4)]
        qaB = [Buf(f"qa{i}") for i in range(4)]
        kaB = [Buf(f"ka{i}") for i in range(4)]
        sqq = [alloc([64, 512], BF16, "sqq") for _ in range(2)]
        sqqB = [Buf(f"sqq{i}") for i in range(2)]
        rs = [alloc([64, 512], F32, "rs") for _ in range(2)]
        rsB = [Buf(f"rs{i}") for i in range(2)]
        pt = [alloc([128, 512], BF16, "pt") for _ in range(4)]
        ptB = [Buf(f"pt{i}") for i in range(4)]
        rden = [alloc([128, 512], F32, "rden") for _ in range(2)]
        rdenB = [Buf(f"rden{i}") for i in range(2)]
        cnt = {"s": 0, "p": 0, "o": 0, "m": 0}
        sm = [alloc([128, 512], F32, "sm") for _ in range(2)]
        smB = [Buf(f"sm{i}") for i in range(2)]
        for half in range(2):
            Wq, WqB = wload(colblk(w_in[l], Q0 + half * 256, 256), "c256")
            Wk, WkB = wload(colblk(w_in[l], K0 + half * 256, 256), "c256")
            for i in range(4):
                hd = half * 4 + i
                S.add("dve", [MS(qa[i][64:70, :], 1.0)], writes=[qaB[i]])
                S.add("dve", [MS(ka[i][64:70, :], 1.0)], writes=[kaB[i]])
                S.add("sp", DMA(ka[i][64:67, :], cumsc[0, hd]), reads=[cumB], writes=[kaB[i]], dma=True, lane=f"ka{i}")
                S.add("sp", DMA(qa[i][67:70, :], cumsc[1, hd]), reads=[cumB], writes=[qaB[i]], dma=True, lane=f"qa{i}")
                for (dst, dstB, Wt, WB, gcol) in ((qa[i], qaB[i], Wq, WqB, 32), (ka[i], kaB[i], Wk, WkB, 33)):
                    for tt in range(4):
                        b = bank(0, 2)
                        proj_fm(Wt, WB, i * 64, None, tt, b, rows=64)
                        s_ = cnt["s"] % 2
                        cnt["s"] += 1
                        S.add("act", ACT(sqq[s_][:], ps[b][0:64, :], AF.Square), reads=[psB[b]], writes=[sqqB[s_]])
                        b2 = 2 + s_
                        S.add("pe", MM(ps[b2][0:64, :], ones_bf[0:64, 0:64], sqq[s_][:], True, True),
                              reads=[sqqB[s_], onesB], writes=[psB[b2]])
                        S.add("act", ACT(rs[s_][:], ps[b2][0:64, :], AF.Ln, bias=eps_t[0:64, 0:1], scale=1.0 / 64),
                              reads=[psB[b2], onesB], writes=[rsB[s_]])
                        S.add("act", ACT(rs[s_][:], rs[s_][:], AF.Exp, scale=-0.5), reads=[rsB[s_]], writes=[rsB[s_]])
                        S.add("dve", STT(dst[0:64, ts(tt)], ps[b][0:64, :], sc[0:64, l, gcol:gcol + 1], rs[s_][:],
                                         ALU.mult, ALU.mult), reads=[psB[b], rsB[s_], scB[l]], writes=[dstB])
            if dbg is not None and l == 0 and half == 0:
                S.add("dve", CP(dbgt[0:70, 256:320], qa[0][0:70, 0:64]), reads=[qaB[0]], writes=[dbgtB])
                S.add("dve", CP(dbgt[0:70, 320:384], ka[0][0:70, 0:64]), reads=[kaB[0]], writes=[dbgtB])
            units = []
            for i in range(4):
                for j in range(4):
                    for ik in range(4 * j + 4):
                        units.append((i, j, ik))
            DEPTH = 2
            SBK = (1, 2, 3)

            def emit_S(n):
                i, j, ik = units[n]
                sbk = SBK[n % 3]
                S.add("pe", MM(ps[sbk][:, :], ka[i][0:70, ts(ik, 128)], qa[i][0:70, ts(j)], True, True),
                      reads=[kaB[i], qaB[i]], writes=[psB[sbk]])

            for n in range(min(DEPTH, len(units))):
                emit_S(n)
            for n, (i, j, ik) in enumerate(units):
                hd = half * 4 + i
                pr = 64 * (hd % 2)
                nk = 4 * j + 4
                if ik == 0:
                    o_ = cnt["o"] % 2
                    cnt["o"] += 1
                    ob, db = 4 + o_, 6 + o_
                sbk = SBK[n % 3]
                p_ = cnt["p"] % 4
                cnt["p"] += 1
                if ik >= 4 * j:
                    m_ = cnt["m"] % 2
                    cnt["m"] += 1
                    S.add("dve", TT(sm[m_][:], ps[sbk][:, :], masks[:, ik - 4 * j, :], ALU.add),
                          reads=[psB[sbk], masksB], writes=[smB[m_]])
                    S.add("act", ACT(pt[p_][:], sm[m_][:], AF.Exp), reads=[smB[m_]], writes=[ptB[p_]])
                else:
                    S.add("act", ACT(pt[p_][:], ps[sbk][:, :], AF.Exp), reads=[psB[sbk]], writes=[ptB[p_]])
                if n + DEPTH < len(units):
                    emit_S(n + DEPTH)
                S.add("pe", [MM(ps[ob][pr:pr + 64, :], V[:, ik, hd * 64:(hd + 1) * 64], pt[p_][:], ik == 0, ik == nk - 1),
                             MM(ps[db][pr:pr + 64, :], ones_bf[:, 0:64], pt[p_][:], ik == 0, ik == nk - 1)],
                      reads=[VB, ptB[p_], onesB], writes=[psB[ob], psB[db]])
                if ik == nk - 1:
                    S.add("dve", RCP(rden[o_][pr:pr + 64, :], ps[db][pr:pr + 64, :]), reads=[psB[db]], writes=[rdenB[o_]])
                    S.add("dve", TT(ya[pr:pr + 64, hd // 2, ts(j)], ps[ob][pr:pr + 64, :], rden[o_][pr:pr + 64, :], ALU.mult),
                          reads=[psB[ob], rdenB[o_]], writes=[yaB])
        release(mA)
        if dbg is not None and l == 0:
            S.add("dve", CP(dbgt[:, 0:64], ya[:, 0, 0:64]), reads=[yaB], writes=[dbgtB])
            S.add("dve", CP(dbgt[:, 64:128], ya[:, 3, 1984:2048]), reads=[yaB], writes=[dbgtB])

        yc = alloc([128, 4, T], BF16, "yconf")
        ycB = Buf("yconf")
        mC = mark()
        Wa, WaB = wload(colblk(w_in[l], A0))
        Wg, WgB = wload(colblk(w_in[l], G0))
        u = alloc([128, 30 + T], BF16, "u")
        uB = Buf("u")
        v = [alloc([128, T], F32, "v") for _ in range(4)]
        vB = [Buf(f"v{i}") for i in range(4)]
        sg = [alloc([128, 512], F32, "sg") for _ in range(1)]
        sgB = [Buf(f"sg{i}") for i in range(1)]
        dg = alloc([128, 31, 128], BF16, "dg")
        dgB = Buf("dg")
        S.add("dve", MS(u[:, 0:30], 0.0), writes=[uB])
        k_ = 0
        for c in range(4):
            dwc = C_DW + c * 31
            for k in range(31):
                S.add("dve", TS(dg[:, k, :], ident[:, :], pp[:, l, dwc + k:dwc + k + 1], None, ALU.mult),
                      reads=[identB, ppB], writes=[dgB])
            for tt in range(4):
                ba = bank(0, 4)
                proj_fm(Wa, WaB, c * 128, None, tt, ba)
                bg = bank(4, 8)
                proj_fm(Wg, WgB, c * 128, None, tt, bg)
                s_ = 0
                S.add("act", ACT(sg[s_][:], ps[bg][:, :], AF.Sigmoid), reads=[psB[bg]], writes=[sgB[s_]])
                S.add("dve", TT(u[:, 30 + tt * 512:30 + (tt + 1) * 512], ps[ba][:, :], sg[s_][:], ALU.mult),
                      reads=[psB[ba], sgB[s_]], writes=[uB])
            for tt in range(4):
                bc = bank(0, 8)
                S.add("pe", [MM(ps[bc][:, :], dg[:, k, :], u[:, k + tt * 512:k + tt * 512 + 512], k == 0, k == 30) for k in range(31)],
                      reads=[dgB, uB], writes=[psB[bc]])
                S.add("act", ACT(v[c][:, ts(tt)], ps[bc][:, :], AF.Identity, bias=pp[:, l, C_DB + c:C_DB + c + 1]),
                      reads=[psB[bc], ppB], writes=[vB[c]])
        vb = alloc([128, 4, 512], BF16, "vb")
        vs = alloc([128, 4, 512], BF16, "vs")
        vbB, vsB = Buf("vb"), Buf("vs")
        st_ = [alloc([128, 512], F32, f"lnst{i}") for i in range(3)]
        stB = [Buf(f"lnst{i}") for i in range(3)]
        t1 = [alloc([128, 512], F32, "t1") for _ in range(1)]
        t1B = [Buf(f"t1{i}") for i in range(1)]
        for tt in range(4):
            for c in range(4):
                S.add("act", ACT(vb[:, c, :], v[c][:, ts(tt)], AF.Identity), reads=[vB[c]], writes=[vbB])
                S.add("act", ACT(vs[:, c, :], v[c][:, ts(tt)], AF.Square), reads=[vB[c]], writes=[vsB])
            bm, bq = bank(0, 4), bank(4, 8)
            S.add("pe", [MM(ps[bm][:, :], ones_bf[:, :], vb[:, c, :], c == 0, c == 3) for c in range(4)],
                  reads=[vbB, onesB], writes=[psB[bm]])
            S.add("pe", [MM(ps[bq][:, :], ones_bf[:, :], vs[:, c, :], c == 0, c == 3) for c in range(4)],
                  reads=[vsB, onesB], writes=[psB[bq]])
            mu, musq, var = st_
            S.add("act", ACT(mu[:], ps[bm][:, :], AF.Identity, scale=1.0 / 512), reads=[psB[bm]], writes=[stB[0]])
            S.add("act", ACT(musq[:], ps[bm][:, :], AF.Square, scale=1.0 / 512), reads=[psB[bm]], writes=[stB[1]])
            S.add("dve", STT(var[:], ps[bq][:, :], 1.0 / 512, musq[:], ALU.mult, ALU.subtract),
                  reads=[psB[bq], stB[1]], writes=[stB[2]])
            S.add("act", ACT(var[:], var[:], AF.Sqrt, bias=eps_t[:, 1:2]), reads=[stB[2], onesB], writes=[stB[2]])
            S.add("dve", RCP(var[:], var[:]), reads=[stB[2]], writes=[stB[2]])
            for c in range(4):
                s_ = 0
                S.add("dve", TT(t1[s_][:], v[c][:, ts(tt)], mu[:], ALU.subtract), reads=[vB[c], stB[0]], writes=[t1B[s_]])
                S.add("dve", TT(t1[s_][:], t1[s_][:], var[:], ALU.mult), reads=[t1B[s_], stB[2]], writes=[t1B[s_]])
                S.add("act", ACT(yc[:, c, ts(tt)], t1[s_][:], AF.Silu, bias=pp[:, l, C_LNB + c:C_LNB + c + 1],
                                 scale=pp[:, l, C_LNG + c:C_LNG + c + 1]), reads=[t1B[s_], ppB], writes=[ycB])
        if dbg is not None and l == 0:
            S.add("dve", CP(dbgt[:, 384:448], yc[:, 0, 0:64]), reads=[ycB], writes=[dbgtB])
        release(mC)

        ysc = alloc([128, 4, T], BF16, "ysc")
        yscB = Buf("ysc")
        mS = mark()
        Wx, WxB = wload(colblk(w_in[l], SX0))
        Wc, WcB = wload(colblk(w_in[l], SC0))
        vp = alloc([128, 2 + T], F32, "vp")
        vpB = Buf("vp")
        cv = [alloc([128, T], F32, "cv") for _ in range(4)]
        cvB = [Buf(f"cv{i}") for i in range(4)]
        xt = [alloc([128, 512], F32, "xt") for _ in range(2)]
        xtB = [Buf(f"xt{i}") for i in range(2)]
        S.add("dve", MS(vp[:, 0:2], 0.0), writes=[vpB])
        k_ = 0
        for c in range(4):
            for tt in range(4):
                bx = bank(0, 4)
                proj_fm(Wx, WxB, c * 128, None, tt, bx)
                bc = bank(4, 8)
                proj_fm(Wc, WcB, c * 128, None, tt, bc)
                s_ = k_ % 2
                k_ += 1
                S.add("act", ACT(xt[s_][:], ps[bx][:, :], AF.Identity), reads=[psB[bx]], writes=[xtB[s_]])
                S.add("dve", TT(vp[:, 2 + tt * 512:2 + (tt + 1) * 512], ps[bc][:, :], xt[s_][:], ALU.mult),
                      reads=[psB[bc], xtB[s_]], writes=[vpB])
            swc = C_SW + c * 3
            S.add("dve", TS(cv[c][:], vp[:, 2:2 + T], pp[:, l, swc + 2:swc + 3], None, ALU.mult), reads=[vpB, ppB], writes=[cvB[c]])
            S.add("dve", STT(cv[c][:], vp[:, 1:1 + T], pp[:, l, swc + 1:swc + 2], cv[c][:], ALU.mult, ALU.add),
                  reads=[vpB, ppB, cvB[c]], writes=[cvB[c]])
            S.add("dve", STT(cv[c][:], vp[:, 0:T], pp[:, l, swc:swc + 1], cv[c][:], ALU.mult, ALU.add),
                  reads=[vpB, ppB, cvB[c]], writes=[cvB[c]])
        Wb, WbB = wload(colblk(w_in[l], SB0))
        for c in range(4):
            for tt in range(4):
                bb = bank(0, 8)
                proj_fm(Wb, WbB, c * 128, None, tt, bb)
                S.add("dve", TT(ysc[:, c, ts(tt)], ps[bb][:, :], cv[c][:, ts(tt)], ALU.mult),
                      reads=[psB[bb], cvB[c]], writes=[yscB])
        if dbg is not None and l == 0:
            S.add("dve", CP(dbgt[:, 512:576], ysc[:, 0, 0:64]), reads=[yscB], writes=[dbgtB])
        release(mS)

        ypl = alloc([128, 4, T], BF16, "ypool")
        yplB = Buf("ypool")
        mP = mark()
        Wp, WpB = wload(colblk(w_in[l], P0))
        pw = alloc([128, 4, 128], BF16, "poolw")
        pwB = Buf("poolw")
        S.add("pool", DMA(pw[:], pool_w[l].rearrange("g c d -> c g d")), writes=[pwB], dma=True, lane="pw")
        pb = alloc([128, 16 + T], F32, "pb")
        PA = alloc([128, 16 + T], F32, "PA")
        PBt = alloc([128, 16 + T], F32, "PB")
        pbB, PAB, PBB = Buf("pb"), Buf("PA"), Buf("PB")
        dbf = alloc([128, T], BF16, "dbf")
        dbfB = Buf("dbf")
        S.add("dve", [MS(pb[:, 0:16], 0.0)], writes=[pbB])
        S.add("dve", [MS(PA[:, 0:16], 0.0)], writes=[PAB])
        S.add("dve", [MS(PBt[:, 0:16], 0.0)], writes=[PBB])
        for g in range(4):
            wdw = POOLW[g]
            for tt in range(4):
                b = bank(0, 8)
                proj_fm(Wp, WpB, g * 128, None, tt, b)
                S.add("act", ACT(pb[:, 16 + tt * 512:16 + (tt + 1) * 512], ps[b][:, :], AF.Identity), reads=[psB[b]], writes=[pbB])
            src, srcB = pb, pbB
            bufs = [(PA, PAB), (PBt, PBB)]
            for lev in range(g + 1):
                d_ = 1 << lev
                dst, dstB = bufs[lev % 2]
                S.add("dve", TT(dst[:, 16:16 + T], src[:, 16:16 + T], src[:, 16 - d_:16 - d_ + T], ALU.add),
                      reads=[srcB], writes=[dstB])
                src, srcB = dst, dstB
            S.add("dve", TS(src[:, 16:16 + T], src[:, 16:16 + T], 1.0 / wdw, None, ALU.mult), reads=[srcB], writes=[srcB])
            S.add("dve", TT(src[:, 16:32], src[:, 16:32], pp[:, l, C_CORR + g * 16:C_CORR + (g + 1) * 16], ALU.mult),
                  reads=[srcB, ppB], writes=[srcB])
            S.add("dve", TT(dbf[:], src[:, 16:16 + T], pb[:, 16:16 + T], ALU.subtract), reads=[srcB, pbB], writes=[dbfB])
            for tt in range(4):
                b = bank(0, 8)
                S.add("pe", MM(ps[b][:, :], pw[:, g, :], dbf[:, ts(tt)], True, True), reads=[pwB, dbfB], writes=[psB[b]])
                S.add("act", ACT(ypl[:, g, ts(tt)], ps[b][:, :], AF.Identity, scale=pp[:, l, C_PSC + g:C_PSC + g + 1]),
                      reads=[psB[b], ppB], writes=[yplB])
        if dbg is not None and l == 0:
            S.add("dve", CP(dbgt[:, 448:512], ypl[:, 3, 0:64]), reads=[yplB], writes=[dbgtB])
            S.add("dve", CP(dbgt[:, 576:640], h[:, 0, 0:64]), reads=hB, writes=[dbgtB])
        release(mP)

        mM = mark()
        acc = [[alloc([128, 512], F32, "acc") for _ in range(4)] for _ in range(2)]
        accB = [[Buf(f"acc{a}{b_}") for b_ in range(4)] for a in range(2)]
        wbr = [alloc([128, 4, 256], BF16, "wbr") for _ in range(2)]
        wbrB = [Buf(f"wbr{i}") for i in range(2)]
        gt = [alloc([128, 512], F32, "gt") for _ in range(2)]
        gtB = [Buf(f"gt{i}") for i in range(2)]
        prod = [alloc([128, 512], F32, "prod") for _ in range(2)]
        prodB = [Buf(f"prod{i}") for i in range(2)]
        stg = [alloc([128, 512], BF16, "stg") for _ in range(2)]
        stgB = [Buf(f"stg{i}") for i in range(2)]
        ys = [(ya, yaB), (yc, ycB), (ypl, yplB), (ysc, yscB)]
        k_ = 0
        kb = 0
        ks = 0
        for mg in range(8):
            for br in range(4):
                Wgt, WgtB = wload(colblk(w_in[l], GATE0 + br * 2048 + mg * 256, 256), "c256")
                wb_ = kb % 2
                kb += 1
                S.add("pool", DMA(wbr[wb_][:], w_br[l, br][:, mg * 256:(mg + 1) * 256].rearrange("(kc p) n -> p kc n", p=128)),
                      writes=[wbrB[wb_]], dma=True, lane=f"wbr{wb_}")
                yt_, ytB = ys[br]
                for m2 in range(2):
                    m = mg * 2 + m2
                    for tt in range(4):
                        bg = bank(0, 4)
                        proj_fm(Wgt, WgtB, m2 * 128, None, tt, bg)
                        bp = bank(4, 8)
                        S.add("pe", [MM(ps[bp][:, :], wbr[wb_][:, k2, m2 * 128:(m2 + 1) * 128], yt_[:, k2, ts(tt)], k2 == 0, k2 == 3)
                                     for k2 in range(4)], reads=[wbrB[wb_], ytB], writes=[psB[bp]])
                        s_ = k_ % 2
                        k_ += 1
                        bgc = C_BGATE + br * 16 + m
                        S.add("act", ACT(gt[s_][:], ps[bg][:, :], AF.Sigmoid, bias=pp[:, l, bgc:bgc + 1]),
                              reads=[psB[bg], ppB], writes=[gtB[s_]])
                        a_, aB = acc[m2][tt], accB[m2][tt]
                        if br == 0:
                            S.add("dve", TT(a_[:], ps[bp][:, :], gt[s_][:], ALU.mult), reads=[psB[bp], gtB[s_]], writes=[aB])
                        else:
                            S.add("dve", TT(prod[s_][:], ps[bp][:, :], gt[s_][:], ALU.mult), reads=[psB[bp], gtB[s_]], writes=[prodB[s_]])
                            if br < 3:
                                S.add("dve", TT(a_[:], a_[:], prod[s_][:], ALU.add), reads=[aB, prodB[s_]], writes=[aB])
                            else:
                                g_ = ks % 2
                                ks += 1
                                S.add("dve", TT(stg[g_][:], a_[:], prod[s_][:], ALU.add), reads=[aB, prodB[s_]], writes=[stgB[g_]])
                                S.add("sp", DMA(mrg[m][:, ts(tt)], stg[g_][:]), reads=[stgB[g_]], writes=[mrgB[m]], dma=True, lane=f"stg{g_}")
        release(mM)
        release(mR)
        if dbg is not None and l == 0:
            S.add("sp", DMA(h[:, 0, :], mrg[0]), reads=[mrgB[0]], writes=[hB[0]], dma=True, lane="h0")
            S.add("dve", CP(dbgt[:, 640:704], h[:, 0, 0:64]), reads=hB, writes=[dbgtB])

        m0 = mark()
        for c in range(NCH):
            S.add("sp", DMA(h[:, c, :], mrg[c]), reads=[mrgB[c]], writes=[hB[c]], dma=True, lane=f"h{c % 4}")
        xb = [alloc([128, T], F32, "xo") for _ in range(3)]
        xbB = [Buf(f"xo{i}") for i in range(3)]
        for nb in range(4):
            Wo, WoB = wload(colblk(w_out[l], nb * 512))
            for m4 in range(4):
                m = nb * 4 + m4
                s_ = m % 3
                S.add("sp", DMA(xb[s_][:], Xsrc[m]), reads=[XsrcB[m]], writes=[xbB[s_]], dma=True, lane=f"xo{s_}")
                for tt in range(4):
                    b = bank(0, 8)
                    proj_fm(Wo, WoB, m4 * 128, None, tt, b)
                    S.add("dve", STT(xb[s_][:, ts(tt)], ps[b][:, :], modT[:, l, 32 + m:33 + m], xb[s_][:, ts(tt)], ALU.mult, ALU.add),
                          reads=[psB[b], modB[l], xbB[s_]], writes=[xbB[s_]])
                S.add("sp", DMA(Xdst[m], xb[s_][:]), reads=[xbB[s_]], writes=[XdstB[m]], dma=True, lane=f"xo{s_}")
        release(m0)

    def mlp_phase(l, Xsrc, XsrcB, Xdst, XdstB, side=None):
        m0 = mark()
        acc = [alloc([128, 1024], F32, "macc") for _ in range(NCH)]
        accB = [Buf(f"macc{i}") for i in range(NCH)]
        hid = alloc([128, 8, 1024], BF16, "hid")
        hidB = Buf("hid")
        xb = [alloc([128, 1024], F32, "xm") for _ in range(2)]
        xbB = [Buf(f"xm{i}") for i in range(2)]
        rl = [alloc([128, 512], F32, "rl") for _ in range(2)]
        rlB = [Buf(f"rl{i}") for i in range(2)]
        cn = {"k": 0, "x": 0, "w": 0}

        def side_step():
            cn["w"] += 1
            if side is not None and cn["w"] % 4 == 0:
                next(side, None)

        def xupd(m, th):
            s_ = cn["x"] % 2
            cn["x"] += 1
            S.add("sp", DMA(xb[s_][:], Xsrc[m][:, ts(th, 1024)]), reads=[XsrcB[m]], writes=[xbB[s_]], dma=True, lane=f"xm{s_}")
            S.add("dve", STT(xb[s_][:], acc[m][:], modT[:, l, 80 + m:81 + m], xb[s_][:], ALU.mult, ALU.add),
                  reads=[accB[m], modB[l], xbB[s_]], writes=[xbB[s_]])
            S.add("sp", DMA(Xdst[m][:, ts(th, 1024)], xb[s_][:]), reads=[xbB[s_]], writes=[XdstB[m]], dma=True, lane=f"xm{s_}")

        for th in range(2):
            for G in range(8):
                for jb in range(2):
                    W1, W1B = wload(colblk(w_m1[l], G * 1024 + jb * 512))
                    for j4 in range(4):
                        j = jb * 4 + j4
                        for t2 in range(2):
                            tt = th * 2 + t2
                            b = bank(0, 3)
                            proj_fm(W1, W1B, j4 * 128, None, tt, b)
                            r_ = cn["k"] % 2
                            cn["k"] += 1
                            S.add("act", ACT(rl[r_][:], ps[b][:, :], AF.Relu), reads=[psB[b]], writes=[rlB[r_]])
                            S.add("dve", TT(hid[:, j, ts(t2)], rl[r_][:], rl[r_][:], ALU.mult), reads=[rlB[r_]], writes=[hidB])
                    side_step()
                for nb in range(4):
                    W2, W2B = wload(w_m2[l][G * 1024:(G + 1) * 1024, nb * 512:(nb + 1) * 512].rearrange("(j p) n -> p j n", p=128), "r8")
                    for m4 in range(4):
                        m = nb * 4 + m4
                        if th == 1 and G == 0:
                            xupd(m, 0)
                        for t2 in range(2):
                            b = bank(4, 7)
                            S.add("pe", [MM(ps[b][:, :], W2[:, j, m4 * 128:(m4 + 1) * 128], hid[:, j, ts(t2)], j == 0, j == 7)
                                         for j in range(8)], reads=[W2B, hidB], writes=[psB[b]])
                            if G == 0:
                                S.add("act", ACT(acc[m][:, ts(t2)], ps[b][:, :], AF.Identity), reads=[psB[b]], writes=[accB[m]])
                            else:
                                S.add("dve", TT(acc[m][:, ts(t2)], acc[m][:, ts(t2)], ps[b][:, :], ALU.add),
                                      reads=[psB[b], accB[m]], writes=[accB[m]])
                    side_step()
        for m in range(NCH):
            xupd(m, 1)
        if side is not None:
            for _ in side:
                pass
        release(m0)

    if dbg is not None:
        dbgt = alloc([128, dbg], F32, "dbgt")
        dbgtB = Buf("dbgt")
        R0 = mark()
    nl = len(layers)
    for _ in mod_gen(layers[0]):
        pass
    for li, l in enumerate(layers):
        Xa, XaB = (xin, xinB) if li == 0 else (xs2, xs2B)
        Xb, XbB = (xs1, xs1B) if li == 0 else (xs3, xs3B)
        Xc, XcB = (yout, youtB) if li == nl - 1 else (xs2, xs2B)
        norm_phase(Xa, XaB, l, 0, 0)
        mixer_phase(l, Xa, XaB, Xb, XbB)
        norm_phase(Xb, XbB, l, 16, 48)
        side = mod_gen(layers[li + 1]) if li + 1 < nl else None
        mlp_phase(l, Xb, XbB, Xc, XcB, side)
    if dbg is not None:
        S.add("sp", DMA(dbg32[:, :], dbgt[:]), reads=[dbgtB], writes=[dbgB], dma=True, lane="dbg")
    S.barrier()
    S.add("sp", [])
    semkeys = S.finalize()
    return nc, S, semkeys


def emit(nc, S, semkeys):
    from contextlib import ExitStack
    with ExitStack() as es:
        sems = {}
        for i, k in enumerate(semkeys):
            sems[k] = es.enter_context(nc.semaphore(f"sm{i}"))
        block = es.enter_context(nc.Block())
        by_eng = {e: [] for e in ENGS}
        for op in S.ops:
            by_eng[op.eng].append(op)

        def run(eng_name, eng):
            seen = {}
            for op in by_eng[eng_name]:
                waits = {}
                for d in op.deps:
                    dop = S.ops[d]
                    if dop.ev is None:
                        continue
                    if eng_name == "pe" and dop.eng == "pe" and not dop.dma:
                        continue
                    k, v = dop.ev
                    if waits.get(k, 0) < v:
                        waits[k] = v
                for k, v in waits.items():
                    if seen.get(k, 0) >= v:
                        continue
                    eng.wait_ge(sems[k], v)
                    seen[k] = v
                n = len(op.fns)
                for i, fn in enumerate(op.fns):
                    ins = fn(eng)
                    if i == n - 1 and op.sig:
                        ins.then_inc(sems[op.ev[0]], 16 if op.dma else 1)

        @block.tensor
        def _(e):
            run("pe", e)

        @block.scalar
        def _(e):
            run("act", e)

        @block.vector
        def _(e):
            run("dve", e)

        @block.gpsimd
        def _(e):
            run("pool", e)

        @block.sync
        def _(e):
            run("sp", e)
    return nc


def _pack_pp(inp):
    pp = np.zeros((L, 128, NPP), np.float32)
    for l in range(L):
        p = pp[l]
        p[:, C_BADA:C_BADA + 96] = inp["b_ada"][l].reshape(96, 128).T
        p[:, C_GAIN:C_GAIN + 16] = inp["norm_gain"][l, 0].reshape(16, 128).T
        p[:, C_GAIN + 16:C_GAIN + 32] = inp["norm_gain"][l, 1].reshape(16, 128).T
        p[:, C_BGATE:C_BGATE + 64] = inp["b_gate"][l].reshape(64, 128).T
        p[:, C_DW:C_DW + 124] = inp["conf_dw"][l].reshape(31, 4, 128).transpose(2, 1, 0).reshape(128, 124)
        p[:, C_DB:C_DB + 4] = inp["conf_db"][l].reshape(4, 128).T
        p[:, C_LNG:C_LNG + 4] = inp["conf_ln_g"][l].reshape(4, 128).T
        p[:, C_LNB:C_LNB + 4] = inp["conf_ln_b"][l].reshape(4, 128).T
        p[:, C_PSC:C_PSC + 4] = inp["pool_scale"][l].reshape(4, 128).T
        p[:, C_SW:C_SW + 12] = inp["sconv_w"][l].reshape(3, 4, 128).transpose(2, 1, 0).reshape(128, 12)
        p[0:64, C_QG] = inp["q_gain"][l]
        p[64:128, C_QG] = inp["q_gain"][l]
        p[0:64, C_KG] = inp["k_gain"][l]
        p[64:128, C_KG] = inp["k_gain"][l]
        p[0:8, C_BF] = inp["b_f"][l]
        for g, w in enumerate(POOLW):
            for t in range(16):
                p[:, C_CORR + g * 16 + t] = float(w) / float(min(t + 1, w))
    return pp


def _masks():
    k = np.arange(128)[:, None]
    q = np.arange(512)[None, :]
    return np.stack([np.where(q >= k + 128 * i, 0.0, -30000.0).astype(np.float32) for i in range(4)], axis=0)


def make_in_maps(inp, xT_list):
    pp = _pack_pp(inp)
    masks = _masks()
    shared = {
        "w_ada": np.ascontiguousarray(inp["w_ada"], dtype=np.float32),
        "w_in": np.ascontiguousarray(inp["w_in"], dtype=np.float32),
        "w_branch": np.ascontiguousarray(inp["w_branch"], dtype=np.float32),
        "w_out": np.ascontiguousarray(inp["w_out"], dtype=np.float32),
        "w_mlp1": np.ascontiguousarray(inp["w_mlp1"], dtype=np.float32),
        "w_mlp2": np.ascontiguousarray(inp["w_mlp2"], dtype=np.float32),
        "pool_w": np.ascontiguousarray(inp["pool_w"], dtype=np.float32),
        "pp": pp,
        "masks": masks,
        "ident": np.eye(128, dtype=np.float32),
    }
    maps = []
    for b in range(8):
        m = dict(shared)
        m["xT"] = xT_list[b]
        m["cT"] = np.ascontiguousarray(np.asarray(inp["c"][b], np.float32).reshape(KC, 128).T)
        maps.append(m)
    return maps


_NC_CACHE = {}


def get_nc(layers=(0, 1), dbg=None):
    key = (tuple(layers), dbg)
    if key not in _NC_CACHE:
        nc, S, semkeys = build_nc(layers=layers, dbg=dbg)
        emit(nc, S, semkeys)
        _NC_CACHE[key] = nc
    return _NC_CACHE[key]


def kernel(**inputs):
    inp = {k: np.asarray(v) for k, v in inputs.items()}
    x = np.asarray(inp["x"], np.float32)
    xT = [np.ascontiguousarray(x[b].T).reshape(NCH, 128, T) for b in range(8)]
    nc = get_nc((0, 1))
    maps = make_in_maps(inp, xT)
    res = run_bass_kernel_spmd(nc, maps, core_ids=list(range(8)))
    out = np.empty((8, T, D), np.float32)
    for b in range(8):
        out[b] = res.results[b]["yT"].reshape(D, T).T
    return out
```

```python
import numpy as np
import concourse.bass as bass
import concourse.mybir as mybir
from concourse.bass_utils import run_bass_kernel_spmd

F32 = mybir.dt.float32
BF16 = mybir.dt.bfloat16
AF = mybir.ActivationFunctionType
ALU = mybir.AluOpType

D = 2048
T = 2048
L = 2
NCH = 16
KC = 16
IN_COLS = 12808
Q0, K0, V0, F0, A0, G0, P0, SX0, SB0, SC0, GATE0 = 0, 512, 1024, 1536, 1544, 2056, 2568, 3080, 3592, 4104, 4616
DFF = 8192
C_BADA, C_GAIN, C_BGATE, C_DW, C_DB, C_LNG, C_LNB, C_PSC, C_SW, C_QG, C_KG, C_BF, C_CORR = (
    0, 96, 128, 192, 316, 320, 324, 328, 332, 344, 345, 346, 347)
NPP = 416
POOLW = (2, 4, 8, 16)
SB_LO, SB_HI = 16512, 229344


class Buf:
    __slots__ = ("name", "w", "r")

    def __init__(self, name):
        self.name = name
        self.w = None
        self.r = []


class Op:
    __slots__ = ("eng", "fns", "deps", "dma", "lane", "ev", "sig")

    def __init__(self, eng, fns, deps, dma, lane):
        self.eng, self.fns, self.deps, self.dma, self.lane = eng, fns, deps, dma, lane
        self.ev = None
        self.sig = False


ENGS = ("pe", "act", "dve", "pool", "sp")


class Sched:
    def __init__(self):
        self.ops = []
        self.pending = {e: None for e in ENGS}
        self.last = {}
        self.dma_since = []

    def add(self, eng, fns, reads=(), writes=(), dma=False, lane=None):
        idx = len(self.ops)
        deps = set()
        for b in reads:
            if b.w is not None:
                deps.add(b.w)
        for b in writes:
            if b.w is not None:
                deps.add(b.w)
            deps.update(b.r)
        for b in reads:
            b.r.append(idx)
        for b in writes:
            b.w = idx
            b.r = []
        if self.pending[eng] is not None:
            deps.update(self.pending[eng])
            self.pending[eng] = None
        deps.discard(idx)
        self.ops.append(Op(eng, fns if isinstance(fns, list) else [fns], deps, dma, lane))
        if dma:
            self.dma_since.append(idx)
        else:
            self.last[eng] = idx
        return idx

    def barrier(self):
        deps = set(self.last.values()) | set(self.dma_since)
        for e in ENGS:
            if self.pending[e] is None:
                self.pending[e] = set(deps)
            else:
                self.pending[e] |= deps
        self.dma_since = []

    def finalize(self):
        needed = set()
        for op in self.ops:
            needed |= op.deps
        cnt = {e: 0 for e in ENGS}
        lanes = {}
        for i, op in enumerate(self.ops):
            if op.dma:
                lanes[op.lane] = lanes.get(op.lane, 0) + 16
                op.ev = (("L", op.lane), lanes[op.lane])
                op.sig = True
            elif i in needed:
                cnt[op.eng] += 1
                ep = (cnt[op.eng] - 1) // 30000
                op.ev = (("E", op.eng, ep), cnt[op.eng] - 30000 * ep)
                op.sig = True
        semkeys = []
        for op in self.ops:
            if op.ev is not None and op.ev[0] not in semkeys:
                semkeys.append(op.ev[0])
        return semkeys


def build_nc(layers=(0, 1), first=True, last=True, dbg=None):
    nc = bass.Bass("TRN2", target_bir_lowering=False)
    S = Sched()

    xin = nc.dram_tensor("xT", [NCH, 128, T], F32, kind="ExternalInput").ap()
    cT_d = nc.dram_tensor("cT", [128, KC], F32, kind="ExternalInput").ap()
    w_ada = nc.dram_tensor("w_ada", [L, D, 6 * D], F32, kind="ExternalInput").ap()
    w_in = nc.dram_tensor("w_in", [L, D, IN_COLS], F32, kind="ExternalInput").ap()
    w_br = nc.dram_tensor("w_branch", [L, 4, 512, D], F32, kind="ExternalInput").ap()
    w_out = nc.dram_tensor("w_out", [L, D, D], F32, kind="ExternalInput").ap()
    w_m1 = nc.dram_tensor("w_mlp1", [L, D, DFF], F32, kind="ExternalInput").ap()
    w_m2 = nc.dram_tensor("w_mlp2", [L, DFF, D], F32, kind="ExternalInput").ap()
    pool_w = nc.dram_tensor("pool_w", [L, 4, 128, 128], F32, kind="ExternalInput").ap()
    pp_d = nc.dram_tensor("pp", [L, 128, NPP], F32, kind="ExternalInput").ap()
    masks_d = nc.dram_tensor("masks", [4, 128, 512], F32, kind="ExternalInput").ap()
    ident_d = nc.dram_tensor("ident", [128, 128], F32, kind="ExternalInput").ap()
    swap_d = nc.dram_tensor("swapm", [128, 128], F32, kind="ExternalInput").ap()
    yout = nc.dram_tensor("yT", [NCH, 128, T], F32, kind="ExternalOutput").ap()
    xs1 = nc.dram_tensor("xs1", [NCH, 128, T], F32, kind="Internal").ap()
    xs2 = nc.dram_tensor("xs2", [NCH, 128, T], F32, kind="Internal").ap()
    xs3 = nc.dram_tensor("xs3", [NCH, 128, T], F32, kind="Internal").ap()
    mrg = nc.dram_tensor("mrg", [NCH, 128, T], BF16, kind="Internal").ap()
    cumsc = nc.dram_tensor("cumsc", [2, 8, 3, T], BF16, kind="Internal").ap()
    if dbg is not None:
        dbg32 = nc.dram_tensor("dbg32", [128, dbg], F32, kind="ExternalOutput").ap()
        dbgB = Buf("dbg32")
    xinB = [Buf(f"xin{c}") for c in range(NCH)]
    xs1B = [Buf(f"xs1_{c}") for c in range(NCH)]
    xs2B = [Buf(f"xs2_{c}") for c in range(NCH)]
    xs3B = [Buf(f"xs3_{c}") for c in range(NCH)]
    youtB = [Buf(f"yout{c}") for c in range(NCH)]
    mrgB = [Buf(f"mrg{c}") for c in range(NCH)]
    cumB = Buf("cumsc")

    st = {"off": SB_LO, "n": 0}

    def alloc(shape, dtype, name="t"):
        per = 1
        for s_ in shape[1:]:
            per *= s_
        nbytes = per * (4 if dtype == F32 else 2)
        nbytes = (nbytes + 63) // 64 * 64
        st["n"] += 1
        t = nc.alloc_sbuf_tensor_at(f"{name}_{st['n']}", list(shape), dtype, offset=st["off"])
        st["off"] += nbytes
        assert st["off"] <= SB_HI, f"SBUF overflow at {name}: {st['off']}"
        return t

    def mark():
        return st["off"]

    def release(m):
        st["off"] = m
        S.barrier()

    ps = [nc.alloc_psum_tensor(f"ps{i}", [128, 512], F32) for i in range(8)]
    psB = [Buf(f"ps{i}") for i in range(8)]
    pst = {"i": 0}

    def bank(lo=0, hi=8):
        n = hi - lo
        b = lo + pst["i"] % n
        pst["i"] += 1
        return b

    def MM(out, lhsT, rhs, start, stop):
        return lambda e: e.matmul(out, lhsT, rhs, start=start, stop=stop)

    def ACT(out, in_, func, bias=None, scale=None):
        kw = {}
        if bias is not None:
            kw["bias"] = bias
        if scale is not None:
            kw["scale"] = scale
        return lambda e: e.activation(out, in_, func, **kw)

    def TT(out, a, b, op):
        return lambda e: e.tensor_tensor(out, a, b, op)

    def TS(out, a, s1, s2, op0, op1=None):
        if op1 is None:
            return lambda e: e.tensor_scalar(out, a, s1, None, op0)
        return lambda e: e.tensor_scalar(out, a, s1, s2, op0, op1)

    def STT(out, a, s, b, op0, op1):
        return lambda e: e.scalar_tensor_tensor(out, a, s, b, op0, op1)

    def CP(out, in_):
        return lambda e: e.tensor_copy(out, in_)

    def RCP(out, in_):
        return lambda e: e.reciprocal(out, in_)

    def MS(ap, v):
        return lambda e: e.memset(ap, v)

    def DMA(out, in_):
        return lambda e: e.dma_start(out=out, in_=in_)

    ts = lambda i, n=512: slice(i * n, (i + 1) * n)

    ones_bf = alloc([128, 128], BF16, "ones")
    onesB = Buf("ones")
    bd_bf = alloc([128, 128], BF16, "bd")
    one_f = alloc([1, 2], F32, "onef")
    eps_t = alloc([128, 2], F32, "eps")
    masks = alloc([128, 4, 512], F32, "masks")
    masksB = Buf("masks")
    swapm = alloc([128, 128], F32, "swapm")
    swapB = Buf("swapm")
    ident = alloc([128, 128], BF16, "ident")
    identB = Buf("ident")
    pp = alloc([128, L, NPP], F32, "pp")
    ppB = Buf("pp")
    c_f = alloc([128, KC], F32, "cf")
    c_bf = alloc([128, KC], BF16, "cbf")
    cB = Buf("c")
    modT = alloc([128, L, 96], F32, "modT")
    modB = [Buf(f"mod{l}") for l in range(L)]
    sc = alloc([128, L, 40], F32, "scal")
    scB = [Buf(f"sc{l}") for l in range(L)]
    h = alloc([128, KC, T], BF16, "h")
    hB = [Buf(f"h{c}") for c in range(KC)]
    wt = [alloc([128, KC, 512], BF16, f"w{i}") for i in range(2)]
    wB = [Buf(f"w{i}") for i in range(2)]
    wst = {"i": 0}

    def wload(src, kind="full"):
        s_ = wst["i"] % 2
        wst["i"] += 1
        if kind == "full":
            dst = wt[s_][:, :, :]
        elif kind == "c256":
            dst = wt[s_][:, :, 0:256]
        elif kind == "r8":
            dst = wt[s_][:, 0:8, :]
        S.add("pool", DMA(dst, src), writes=[wB[s_]], dma=True, lane=f"w{s_}")
        return wt[s_], wB[s_]

    def colblk(wap, c0, n=512):
        return wap[:, c0:c0 + n].rearrange("(kc p) n -> p kc n", p=128)

    R0 = mark()

    S.add("dve", MS(ones_bf[:], 1.0), writes=[onesB])
    S.add("dve", [MS(bd_bf[:], 0.0), MS(bd_bf[0:64, 0:64], 1.0), MS(bd_bf[64:128, 64:128], 1.0)], writes=[onesB])
    S.add("dve", [MS(one_f[:], 1.0), MS(eps_t[:, 0:1], 1e-6), MS(eps_t[:, 1:2], 1e-5)], writes=[onesB])
    S.add("sp", DMA(masks[:], masks_d.rearrange("i k q -> k i q")), writes=[masksB], dma=True, lane="masks")
    S.add("sp", DMA(pp[:], pp_d.rearrange("l p n -> p l n")), writes=[ppB], dma=True, lane="pp")
    S.add("sp", DMA(c_f[:], cT_d), writes=[cB], dma=True, lane="c")
    S.add("pool", DMA(ident[:], ident_d), writes=[identB], dma=True, lane="ident")
    S.add("sp", DMA(swapm[:], swap_d), writes=[swapB], dma=True, lane="swapm")
    S.add("dve", CP(c_bf[:], c_f[:]), reads=[cB], writes=[cB])

    def dump(ap, col, n, bufs):
        if dbg is None:
            return
        S.add("sp", DMA(dbg32[0:ap.shape[0], col:col + n], ap), reads=bufs, writes=[dbgB], dma=True, lane="dbg")

    modrow = alloc([1, 512], F32, "modrow")
    modrowB = Buf("modrow")

    def mod_gen(l):
        bT, bR = 7, 3

        def transposes(j):
            S.add("pe", [MM(ps[bT][:, j * 4 + i:j * 4 + i + 1], modrow[0:1, ts(i, 128)], one_f[0:1, 0:1], True, True)
                         for i in range(4)], reads=[modrowB, onesB], writes=[psB[bT]])

        for j in range(24):
            if j > 0:
                transposes(j - 1)
            Wt, WB = wload(colblk(w_ada[l], j * 512))
            S.add("pe", [MM(ps[bR][0:1, :], c_bf[:, kc:kc + 1], Wt[:, kc, :], kc == 0, kc == KC - 1) for kc in range(KC)],
                  reads=[cB, WB], writes=[psB[bR]])
            S.add("act", ACT(modrow[:], ps[bR][0:1, :], AF.Identity), reads=[psB[bR]], writes=[modrowB])
            yield j
        transposes(23)
        S.add("dve", TT(modT[:, l, :], ps[bT][:, 0:96], pp[:, l, C_BADA:C_BADA + 96], ALU.add),
              reads=[psB[bT], ppB], writes=[modB[l]])
        fns = []
        for (dst0, scoff, goff) in ((0, 16, C_GAIN), (16, 64, C_GAIN + 16)):
            fns.append(TS(sc[:, l, dst0:dst0 + 16], modT[:, l, scoff:scoff + 16], 1.0, None, ALU.add))
            fns.append(TT(sc[:, l, dst0:dst0 + 16], sc[:, l, dst0:dst0 + 16], pp[:, l, goff:goff + 16], ALU.mult))
        fns.append(TS(sc[:, l, 32:33], pp[:, l, C_QG:C_QG + 1], 0.125, None, ALU.mult))
        fns.append(CP(sc[:, l, 33:34], pp[:, l, C_KG:C_KG + 1]))
        fns.append(TS(sc[:, l, 34:35], pp[:, l, C_BF:C_BF + 1], -1.0, None, ALU.mult))
        for f_ in fns:
            S.add("dve", f_, reads=[modB[l], ppB], writes=[scB[l]])
        yield 24

    def norm_phase(X, XB, l, aoff, shoff):
        m0 = mark()
        xb = [alloc([128, T], F32, "xb") for _ in range(3)]
        xbB = [Buf(f"xb{i}") for i in range(3)]
        sq = [alloc([128, T], BF16, "sq") for _ in range(2)]
        sqB = [Buf(f"sq{i}") for i in range(2)]
        rstd = alloc([128, T], F32, "rstd")
        rstdB = Buf("rstd")
        tmp = [alloc([128, T], F32, "ntmp") for _ in range(2)]
        tmpB = [Buf(f"ntmp{i}") for i in range(2)]
        for c in range(NCH):
            s_ = c % 3
            S.add("sp", DMA(xb[s_][:], X[c]), reads=[XB[c]], writes=[xbB[s_]], dma=True, lane=f"xb{s_}")
            S.add("act", ACT(sq[c % 2][:], xb[s_][:], AF.Square), reads=[xbB[s_]], writes=[sqB[c % 2]])
            S.add("pe", [MM(ps[tt][:, :], ones_bf[:, :], sq[c % 2][:, ts(tt)], c == 0, c == NCH - 1) for tt in range(4)],
                  reads=[sqB[c % 2], onesB], writes=[psB[0], psB[1], psB[2], psB[3]])
        for tt in range(4):
            S.add("act", ACT(rstd[:, ts(tt)], ps[tt][:, :], AF.Ln, bias=eps_t[:, 0:1], scale=1.0 / D),
                  reads=[psB[tt], onesB], writes=[rstdB])
        S.add("act", ACT(rstd[:], rstd[:], AF.Exp, scale=-0.5), reads=[rstdB], writes=[rstdB])
        for c in range(NCH):
            s_ = c % 3
            S.add("sp", DMA(xb[s_][:], X[c]), reads=[XB[c]], writes=[xbB[s_]], dma=True, lane=f"xb{s_}")
            S.add("dve", TT(tmp[c % 2][:], xb[s_][:], rstd[:], ALU.mult), reads=[xbB[s_], rstdB], writes=[tmpB[c % 2]])
            S.add("act", ACT(h[:, c, :], tmp[c % 2][:], AF.Identity, bias=modT[:, l, shoff + c:shoff + c + 1],
                             scale=sc[:, l, aoff + c:aoff + c + 1]),
                  reads=[tmpB[c % 2], modB[l], scB[l]], writes=[hB[c]])
        release(m0)

    def proj_fm(Wt, WB, col0, m, tt, b, rows=128):
        S.add("pe", [MM(ps[b][0:rows, :], Wt[:, kc, col0:col0 + rows], h[:, kc, ts(tt)], kc == 0, kc == KC - 1)
                     for kc in range(KC)], reads=hB + [WB], writes=[psB[b]])

    def mixer_phase(l, Xsrc, XsrcB, Xdst, XdstB):
        mR = mark()
        ya = alloc([128, 4, T], BF16, "yatt")
        yaB = Buf("yatt")
        mA = mark()
        mF = mark()
        wf = alloc([128, KC, 8], BF16, "wf")
        wfB = Buf("wf")
        fA = alloc([8, T], F32, "fA")
        fC = alloc([8, T], F32, "fC")
        fO = alloc([8, T], F32, "fO")
        fR = alloc([8, T], F32, "fR")
        ksp = alloc([8, 3, T], BF16, "ksp")
        qsp = alloc([8, 3, T], BF16, "qsp")
        fB_ = Buf("fstuff")
        S.add("pool", DMA(wf[:], colblk(w_in[l], F0, 8)), writes=[wfB], dma=True, lane="wf")
        S.add("dve", MS(fO[:], 1.0), writes=[fB_])
        for tt in range(4):
            b = bank(0, 4)
            S.add("pe", [MM(ps[b][0:8, :], wf[:, kc, :], h[:, kc, ts(tt)], kc == 0, kc == KC - 1) for kc in range(KC)],
                  reads=hB + [wfB], writes=[psB[b]])
            S.add("act", ACT(fA[:, ts(tt)], ps[b][0:8, :], AF.Exp, bias=sc[0:8, l, 34:35], scale=-1.0),
                  reads=[psB[b], scB[l]], writes=[fB_])
        S.add("act", ACT(fA[:], fA[:], AF.Ln, bias=1.0), reads=[fB_], writes=[fB_])
        S.add("dve", lambda e: e.tensor_tensor_scan(fC[:], fO[:], fA[:], 0.0, ALU.mult, ALU.add), reads=[fB_], writes=[fB_])
        for f_ in (CP(ksp[:, 0, :], fC[:]), TT(fR[:], fC[:], ksp[:, 0, :], ALU.subtract), CP(ksp[:, 1, :], fR[:]),
                   TT(fR[:], fR[:], ksp[:, 1, :], ALU.subtract), CP(ksp[:, 2, :], fR[:]),
                   TS(qsp[:, 0, :], ksp[:, 0, :], -1.0, None, ALU.mult), TS(qsp[:, 1, :], ksp[:, 1, :], -1.0, None, ALU.mult),
                   TS(qsp[:, 2, :], ksp[:, 2, :], -1.0, None, ALU.mult)):
            S.add("dve", f_, reads=[fB_], writes=[fB_])
        S.add("sp", DMA(cumsc[0], ksp[:]), reads=[fB_], writes=[cumB], dma=True, lane="ksp")
        S.add("sp", DMA(cumsc[1], qsp[:]), reads=[fB_], writes=[cumB], dma=True, lane="qsp")
        if dbg is not None and l == 0:
            pass
            pass
        release(mF)
        V = alloc([128, 16, 4, 2, 128], BF16, "Vaug")
        VB = Buf("V")
        S.add("dve", MS(V[:, :, :, 0, 64:128], 1.0), writes=[VB])
        S.add("dve", MS(V[:, :, :, 1, 0:64], 1.0), writes=[VB])
        Wv, WvB = wload(colblk(w_in[l], V0))
        for tc in range(16):
            b = bank(0, 4)
            S.add("pe", [MM(ps[b][:, :], h[:, kc, ts(tc, 128)], Wv[:, kc, :], kc == 0, kc == KC - 1) for kc in range(KC)],
                  reads=hB + [WvB], writes=[psB[b]])
            pv = ps[b][:, :].rearrange("p (hp two d) -> p hp two d", two=2, d=64)
            S.add("act", ACT(V[:, tc, :, 0, 0:64], pv[:, :, 0, :], AF.Identity), reads=[psB[b]], writes=[VB])
            S.add("dve", CP(V[:, tc, :, 1, 64:128], pv[:, :, 1, :]), reads=[psB[b]], writes=[VB])
        qa = [alloc([128, T], BF16, "qa") for _ in range(4)]
        ka = [alloc([128, T], BF16, "ka") for _ in range(4)]
        qaB = [Buf(f"qa{i}") for i in range(4)]
        kaB = [Buf(f"ka{i}") for i in range(4)]
        sqq = [alloc([128, 512], BF16, "sqq") for _ in range(2)]
        sqqB = [Buf(f"sqq{i}") for i in range(2)]
        rs = [alloc([128, 512], F32, "rs") for _ in range(2)]
        rsB = [Buf(f"rs{i}") for i in range(2)]
        pt = [alloc([128, 512], BF16, "pt") for _ in range(3)]
        ptB = [Buf(f"pt{i}") for i in range(3)]
        rden = [alloc([128, 512], F32, "rden") for _ in range(2)]
        rdenB = [Buf(f"rden{i}") for i in range(2)]
        cnt = {"s": 0, "p": 0, "o": 0, "m": 0}
        for half in range(2):
            Wq, WqB = wload(colblk(w_in[l], Q0 + half * 256, 256), "c256")
            Wk, WkB = wload(colblk(w_in[l], K0 + half * 256, 256), "c256")
            for i in range(4):
                hd = half * 4 + i
                if i % 2 == 0:
                    S.add("dve", [MS(qa[i][64:70, :], 1.0)], writes=[qaB[i]])
                    S.add("dve", [MS(ka[i][64:70, :], 1.0)], writes=[kaB[i]])
                    S.add("sp", DMA(ka[i][64:67, :], cumsc[0, hd]), reads=[cumB], writes=[kaB[i]], dma=True, lane=f"ka{i}")
                    S.add("sp", DMA(qa[i][67:70, :], cumsc[1, hd]), reads=[cumB], writes=[qaB[i]], dma=True, lane=f"qa{i}")
                else:
                    S.add("dve", [MS(qa[i][0:64, :], 0.0), MS(qa[i][0:6, :], 1.0)], writes=[qaB[i]])
                    S.add("dve", [MS(ka[i][0:64, :], 0.0), MS(ka[i][0:6, :], 1.0)], writes=[kaB[i]])
                    S.add("sp", DMA(ka[i][0:3, :], cumsc[0, hd]), reads=[cumB], writes=[kaB[i]], dma=True, lane=f"ka{i}")
                    S.add("sp", DMA(qa[i][3:6, :], cumsc[1, hd]), reads=[cumB], writes=[qaB[i]], dma=True, lane=f"qa{i}")
            for pl in range(2):
                ie, io = 2 * pl, 2 * pl + 1
                for (dt_, dtB, Wt, WB, gcol) in ((qa, qaB, Wq, WqB, 32), (ka, kaB, Wk, WkB, 33)):
                    for tt in range(4):
                        b = bank(0, 2)
                        proj_fm(Wt, WB, pl * 128, None, tt, b, rows=128)
                        s_ = cnt["s"] % 2
                        cnt["s"] += 1
                        S.add("act", ACT(sqq[s_][:], ps[b][:, :], AF.Square), reads=[psB[b]], writes=[sqqB[s_]])
                        b2 = 2 + s_
                        S.add("pe", MM(ps[b2][:, :], bd_bf[:, :], sqq[s_][:], True, True),
                              reads=[sqqB[s_], onesB], writes=[psB[b2]])
                        S.add("act", ACT(rs[s_][:], ps[b2][:, :], AF.Ln, bias=eps_t[:, 0:1], scale=1.0 / 64),
                              reads=[psB[b2], onesB], writes=[rsB[s_]])
                        S.add("act", ACT(rs[s_][:], rs[s_][:], AF.Exp, scale=-0.5), reads=[rsB[s_]], writes=[rsB[s_]])
                        S.add("dve", STT(dt_[ie][0:64, ts(tt)], ps[b][0:64, :], sc[0:64, l, gcol:gcol + 1], rs[s_][0:64, :],
                                         ALU.mult, ALU.mult), reads=[psB[b], rsB[s_], scB[l]], writes=[dtB[ie]])
                        S.add("dve", STT(dt_[io][64:128, ts(tt)], ps[b][64:128, :], sc[64:128, l, gcol:gcol + 1], rs[s_][64:128, :],
                                         ALU.mult, ALU.mult), reads=[psB[b], rsB[s_], scB[l]], writes=[dtB[io]])
            if dbg is not None and l == 0 and half == 0:
                pass
                pass
            units = []
            for pl in range(2):
                for j in range(4):
                    for par in range(2):
                        for ik in range(4 * j + 4):
                            units.append((pl, j, par, ik))
            DEPTH = 3
            ring = {"n": 0}
            ubank = {}

            def ring_bank():
                b_ = ring["n"] % 4
                ring["n"] += 1
                return b_

            def c0_of(j, ik):
                return 128 * (ik - 4 * j) if ik >= 4 * j else 0

            def emit_S(n):
                pl, j, par, ik = units[n]
                i = pl * 2 + par
                sbk = ring_bank()
                ubank[n] = sbk
                kr = 70 if i % 2 == 0 else 128
                c0 = c0_of(j, ik)
                S.add("pe", MM(ps[sbk][:, c0:512], ka[i][0:kr, ts(ik, 128)], qa[i][0:kr, j * 512 + c0:(j + 1) * 512], True, True),
                      reads=[kaB[i], qaB[i]], writes=[psB[sbk]])

            finals = []

            def fin1(pl, j, o_):
                A, Bk = 4 + 2 * o_, 5 + 2 * o_
                S.add("act", ACT(rden[o_][64:128, :], ps[A][64:128, :], AF.Ln), reads=[psB[A]], writes=[rdenB[o_]])
                S.add("act", ACT(rden[o_][0:64, :], ps[Bk][0:64, :], AF.Ln), reads=[psB[Bk]], writes=[rdenB[o_]])
                S.add("act", ACT(rden[o_][:, :], rden[o_][:, :], AF.Exp, scale=-1.0), reads=[rdenB[o_]], writes=[rdenB[o_]])

            def fin2(pl, j, o_):
                A, Bk = 4 + 2 * o_, 5 + 2 * o_
                pair = half * 2 + pl
                pb_ = ring_bank()
                S.add("pe", MM(ps[pb_][:, :], swapm[:, :], rden[o_][:, :], True, True), reads=[rdenB[o_], swapB], writes=[psB[pb_]])
                S.add("act", ACT(rden[o_][:, :], ps[pb_][:, :], AF.Identity), reads=[psB[pb_]], writes=[rdenB[o_]])
                S.add("dve", TT(ya[0:64, pair, ts(j)], ps[A][0:64, :], rden[o_][0:64, :], ALU.mult),
                      reads=[psB[A], rdenB[o_]], writes=[yaB])
                S.add("dve", TT(ya[64:128, pair, ts(j)], ps[Bk][64:128, :], rden[o_][64:128, :], ALU.mult),
                      reads=[psB[Bk], rdenB[o_]], writes=[yaB])

            for n in range(min(DEPTH, len(units))):
                emit_S(n)
            for n, (pl, j, par, ik) in enumerate(units):
                i = pl * 2 + par
                hd = half * 4 + i
                nk = 4 * j + 4
                if ik == 0 and par == 0:
                    o_ = cnt["o"] % 2
                    cnt["o"] += 1
                ob = 4 + 2 * o_ + par
                sbk = ubank[n]
                p_ = cnt["p"] % 3
                cnt["p"] += 1
                c0 = c0_of(j, ik)
                if ik >= 4 * j:
                    S.add("dve", TT(ps[sbk][:, c0:c0 + 128], ps[sbk][:, c0:c0 + 128], masks[:, 0, 0:128], ALU.add),
                          reads=[psB[sbk], masksB], writes=[psB[sbk]])
                S.add("act", ACT(pt[p_][:, c0:512], ps[sbk][:, c0:512], AF.Exp), reads=[psB[sbk]], writes=[ptB[p_]])
                if n + DEPTH < len(units):
                    emit_S(n + DEPTH)
                S.add("pe", MM(ps[ob][:, c0:512], V[:, ik, hd // 2, hd % 2, :], pt[p_][:, c0:512], ik == 0, ik == nk - 1),
                      reads=[VB, ptB[p_]], writes=[psB[ob]])
                for f_ in list(finals):
                    f_[0] -= 1
                    if f_[0] <= 0:
                        fin2(*f_[1])
                        finals.remove(f_)
                if ik == nk - 1 and par == 1:
                    fin1(pl, j, o_)
                    finals.append([3, (pl, j, o_)])
            for f_ in finals:
                fin2(*f_[1])
        release(mA)
        if dbg is not None and l == 0:
            S.add("dve", CP(dbgt[:, 0:64], ya[:, 0, 0:64]), reads=[yaB], writes=[dbgtB])
            S.add("dve", CP(dbgt[:, 64:128], ya[:, 3, 1984:2048]), reads=[yaB], writes=[dbgtB])

        yc = alloc([128, 4, T], BF16, "yconf")
        ycB = Buf("yconf")
        mC = mark()
        Wa, WaB = wload(colblk(w_in[l], A0))
        Wg, WgB = wload(colblk(w_in[l], G0))
        u = alloc([128, 30 + T], BF16, "u")
        uB = Buf("u")
        v = [alloc([128, T], F32, "v") for _ in range(4)]
        vB = [Buf(f"v{i}") for i in range(4)]
        sg = [alloc([128, 512], F32, "sg") for _ in range(1)]
        sgB = [Buf(f"sg{i}") for i in range(1)]
        dg = alloc([128, 31, 128], BF16, "dg")
        dgB = Buf("dg")
        S.add("dve", MS(u[:, 0:30], 0.0), writes=[uB])
        k_ = 0
        for c in range(4):
            dwc = C_DW + c * 31
            for k in range(31):
                S.add("dve", TS(dg[:, k, :], ident[:, :], pp[:, l, dwc + k:dwc + k + 1], None, ALU.mult),
                      reads=[identB, ppB], writes=[dgB])
            for tt in range(4):
                ba = bank(0, 4)
                proj_fm(Wa, WaB, c * 128, None, tt, ba)
                bg = bank(4, 8)
                proj_fm(Wg, WgB, c * 128, None, tt, bg)
                s_ = 0
                S.add("act", ACT(sg[s_][:], ps[bg][:, :], AF.Sigmoid), reads=[psB[bg]], writes=[sgB[s_]])
                S.add("dve", TT(u[:, 30 + tt * 512:30 + (tt + 1) * 512], ps[ba][:, :], sg[s_][:], ALU.mult),
                      reads=[psB[ba], sgB[s_]], writes=[uB])
            for tt in range(4):
                bc = bank(0, 8)
                S.add("pe", [MM(ps[bc][:, :], dg[:, k, :], u[:, k + tt * 512:k + tt * 512 + 512], k == 0, k == 30) for k in range(31)],
                      reads=[dgB, uB], writes=[psB[bc]])
                S.add("act", ACT(v[c][:, ts(tt)], ps[bc][:, :], AF.Identity, bias=pp[:, l, C_DB + c:C_DB + c + 1]),
                      reads=[psB[bc], ppB], writes=[vB[c]])
        vb = alloc([128, 4, 512], BF16, "vb")
        vs = alloc([128, 4, 512], BF16, "vs")
        vbB, vsB = Buf("vb"), Buf("vs")
        st_ = [alloc([128, 512], F32, f"lnst{i}") for i in range(3)]
        stB = [Buf(f"lnst{i}") for i in range(3)]
        t1 = [alloc([128, 512], F32, "t1") for _ in range(1)]
        t1B = [Buf(f"t1{i}") for i in range(1)]
        for tt in range(4):
            for c in range(4):
                S.add("act", ACT(vb[:, c, :], v[c][:, ts(tt)], AF.Identity), reads=[vB[c]], writes=[vbB])
                S.add("act", ACT(vs[:, c, :], v[c][:, ts(tt)], AF.Square), reads=[vB[c]], writes=[vsB])
            bm, bq = bank(0, 4), bank(4, 8)
            S.add("pe", [MM(ps[bm][:, :], ones_bf[:, :], vb[:, c, :], c == 0, c == 3) for c in range(4)],
                  reads=[vbB, onesB], writes=[psB[bm]])
            S.add("pe", [MM(ps[bq][:, :], ones_bf[:, :], vs[:, c, :], c == 0, c == 3) for c in range(4)],
                  reads=[vsB, onesB], writes=[psB[bq]])
            mu, musq, var = st_
            S.add("act", ACT(mu[:], ps[bm][:, :], AF.Identity, scale=1.0 / 512), reads=[psB[bm]], writes=[stB[0]])
            S.add("act", ACT(musq[:], ps[bm][:, :], AF.Square, scale=1.0 / 512), reads=[psB[bm]], writes=[stB[1]])
            S.add("dve", STT(var[:], ps[bq][:, :], 1.0 / 512, musq[:], ALU.mult, ALU.subtract),
                  reads=[psB[bq], stB[1]], writes=[stB[2]])
            S.add("act", ACT(var[:], var[:], AF.Ln, bias=eps_t[:, 1:2]), reads=[stB[2], onesB], writes=[stB[2]])
            S.add("act", ACT(var[:], var[:], AF.Exp, scale=-0.5), reads=[stB[2]], writes=[stB[2]])
            for c in range(4):
                s_ = 0
                S.add("dve", TT(t1[s_][:], v[c][:, ts(tt)], mu[:], ALU.subtract), reads=[vB[c], stB[0]], writes=[t1B[s_]])
                S.add("dve", TT(t1[s_][:], t1[s_][:], var[:], ALU.mult), reads=[t1B[s_], stB[2]], writes=[t1B[s_]])
                S.add("act", ACT(yc[:, c, ts(tt)], t1[s_][:], AF.Silu, bias=pp[:, l, C_LNB + c:C_LNB + c + 1],
                                 scale=pp[:, l, C_LNG + c:C_LNG + c + 1]), reads=[t1B[s_], ppB], writes=[ycB])
        if dbg is not None and l == 0:
            pass
        release(mC)

        ysc = alloc([128, 4, T], BF16, "ysc")
        yscB = Buf("ysc")
        mS = mark()
        Wx, WxB = wload(colblk(w_in[l], SX0))
        Wc, WcB = wload(colblk(w_in[l], SC0))
        vp = alloc([128, 2 + T], F32, "vp")
        vpB = Buf("vp")
        cv = [alloc([128, T], F32, "cv") for _ in range(4)]
        cvB = [Buf(f"cv{i}") for i in range(4)]
        xt = [alloc([128, 512], F32, "xt") for _ in range(2)]
        xtB = [Buf(f"xt{i}") for i in range(2)]
        S.add("dve", MS(vp[:, 0:2], 0.0), writes=[vpB])
        k_ = 0
        for c in range(4):
            for tt in range(4):
                bx = bank(0, 4)
                proj_fm(Wx, WxB, c * 128, None, tt, bx)
                bc = bank(4, 8)
                proj_fm(Wc, WcB, c * 128, None, tt, bc)
                s_ = k_ % 2
                k_ += 1
                S.add("act", ACT(xt[s_][:], ps[bx][:, :], AF.Identity), reads=[psB[bx]], writes=[xtB[s_]])
                S.add("dve", TT(vp[:, 2 + tt * 512:2 + (tt + 1) * 512], ps[bc][:, :], xt[s_][:], ALU.mult),
                      reads=[psB[bc], xtB[s_]], writes=[vpB])
            swc = C_SW + c * 3
            S.add("dve", TS(cv[c][:], vp[:, 2:2 + T], pp[:, l, swc + 2:swc + 3], None, ALU.mult), reads=[vpB, ppB], writes=[cvB[c]])
            S.add("dve", STT(cv[c][:], vp[:, 1:1 + T], pp[:, l, swc + 1:swc + 2], cv[c][:], ALU.mult, ALU.add),
                  reads=[vpB, ppB, cvB[c]], writes=[cvB[c]])
            S.add("dve", STT(cv[c][:], vp[:, 0:T], pp[:, l, swc:swc + 1], cv[c][:], ALU.mult, ALU.add),
                  reads=[vpB, ppB, cvB[c]], writes=[cvB[c]])
        Wb, WbB = wload(colblk(w_in[l], SB0))
        for c in range(4):
            for tt in range(4):
                bb = bank(0, 8)
                proj_fm(Wb, WbB, c * 128, None, tt, bb)
                S.add("dve", TT(ysc[:, c, ts(tt)], ps[bb][:, :], cv[c][:, ts(tt)], ALU.mult),
                      reads=[psB[bb], cvB[c]], writes=[yscB])
        if dbg is not None and l == 0:
            pass
        release(mS)

        ypl = alloc([128, 4, T], BF16, "ypool")
        yplB = Buf("ypool")
        mP = mark()
        Wp, WpB = wload(colblk(w_in[l], P0))
        pw = alloc([128, 4, 128], BF16, "poolw")
        pwB = Buf("poolw")
        S.add("pool", DMA(pw[:], pool_w[l].rearrange("g c d -> c g d")), writes=[pwB], dma=True, lane="pw")
        pb = alloc([128, 16 + T], F32, "pb")
        PA = alloc([128, 16 + T], F32, "PA")
        PBt = alloc([128, 16 + T], F32, "PB")
        pbB, PAB, PBB = Buf("pb"), Buf("PA"), Buf("PB")
        dbf = alloc([128, T], BF16, "dbf")
        dbfB = Buf("dbf")
        S.add("dve", [MS(pb[:, 0:16], 0.0)], writes=[pbB])
        S.add("dve", [MS(PA[:, 0:16], 0.0)], writes=[PAB])
        S.add("dve", [MS(PBt[:, 0:16], 0.0)], writes=[PBB])
        for g in range(4):
            wdw = POOLW[g]
            for tt in range(4):
                b = bank(0, 8)
                proj_fm(Wp, WpB, g * 128, None, tt, b)
                S.add("act", ACT(pb[:, 16 + tt * 512:16 + (tt + 1) * 512], ps[b][:, :], AF.Identity), reads=[psB[b]], writes=[pbB])
            src, srcB = pb, pbB
            bufs = [(PA, PAB), (PBt, PBB)]
            for lev in range(g + 1):
                d_ = 1 << lev
                dst, dstB = bufs[lev % 2]
                S.add("dve", TT(dst[:, 16:16 + T], src[:, 16:16 + T], src[:, 16 - d_:16 - d_ + T], ALU.add),
                      reads=[srcB], writes=[dstB])
                src, srcB = dst, dstB
            S.add("dve", TS(src[:, 16:16 + T], src[:, 16:16 + T], 1.0 / wdw, None, ALU.mult), reads=[srcB], writes=[srcB])
            S.add("dve", TT(src[:, 16:32], src[:, 16:32], pp[:, l, C_CORR + g * 16:C_CORR + (g + 1) * 16], ALU.mult),
                  reads=[srcB, ppB], writes=[srcB])
            S.add("dve", TT(dbf[:], src[:, 16:16 + T], pb[:, 16:16 + T], ALU.subtract), reads=[srcB, pbB], writes=[dbfB])
            for tt in range(4):
                b = bank(0, 8)
                S.add("pe", MM(ps[b][:, :], pw[:, g, :], dbf[:, ts(tt)], True, True), reads=[pwB, dbfB], writes=[psB[b]])
                S.add("act", ACT(ypl[:, g, ts(tt)], ps[b][:, :], AF.Identity, scale=pp[:, l, C_PSC + g:C_PSC + g + 1]),
                      reads=[psB[b], ppB], writes=[yplB])
        if dbg is not None and l == 0:
            pass
            pass
        release(mP)

        mM = mark()
        acc = [[alloc([128, 512], F32, "acc") for _ in range(4)] for _ in range(2)]
        accB = [[Buf(f"acc{a}{b_}") for b_ in range(4)] for a in range(2)]
        wbr = [alloc([128, 4, 256], BF16, "wbr") for _ in range(2)]
        wbrB = [Buf(f"wbr{i}") for i in range(2)]
        gt = [alloc([128, 512], F32, "gt") for _ in range(2)]
        gtB = [Buf(f"gt{i}") for i in range(2)]
        prod = [alloc([128, 512], F32, "prod") for _ in range(2)]
        prodB = [Buf(f"prod{i}") for i in range(2)]
        stg = [alloc([128, 512], BF16, "stg") for _ in range(2)]
        stgB = [Buf(f"stg{i}") for i in range(2)]
        ys = [(ya, yaB), (yc, ycB), (ypl, yplB), (ysc, yscB)]
        k_ = 0
        kb = 0
        ks = 0
        for mg in range(8):
            for br in range(4):
                Wgt, WgtB = wload(colblk(w_in[l], GATE0 + br * 2048 + mg * 256, 256), "c256")
                wb_ = kb % 2
                kb += 1
                S.add("pool", DMA(wbr[wb_][:], w_br[l, br][:, mg * 256:(mg + 1) * 256].rearrange("(kc p) n -> p kc n", p=128)),
                      writes=[wbrB[wb_]], dma=True, lane=f"wbr{wb_}")
                yt_, ytB = ys[br]
                for m2 in range(2):
                    m = mg * 2 + m2
                    for tt in range(4):
                        bg = bank(0, 4)
                        proj_fm(Wgt, WgtB, m2 * 128, None, tt, bg)
                        bp = bank(4, 8)
                        S.add("pe", [MM(ps[bp][:, :], wbr[wb_][:, k2, m2 * 128:(m2 + 1) * 128], yt_[:, k2, ts(tt)], k2 == 0, k2 == 3)
                                     for k2 in range(4)], reads=[wbrB[wb_], ytB], writes=[psB[bp]])
                        s_ = k_ % 2
                        k_ += 1
                        bgc = C_BGATE + br * 16 + m
                        S.add("act", ACT(gt[s_][:], ps[bg][:, :], AF.Sigmoid, bias=pp[:, l, bgc:bgc + 1]),
                              reads=[psB[bg], ppB], writes=[gtB[s_]])
                        a_, aB = acc[m2][tt], accB[m2][tt]
                        if br == 0:
                            S.add("dve", TT(a_[:], ps[bp][:, :], gt[s_][:], ALU.mult), reads=[psB[bp], gtB[s_]], writes=[aB])
                        else:
                            S.add("dve", TT(prod[s_][:], ps[bp][:, :], gt[s_][:], ALU.mult), reads=[psB[bp], gtB[s_]], writes=[prodB[s_]])
                            if br < 3:
                                S.add("dve", TT(a_[:], a_[:], prod[s_][:], ALU.add), reads=[aB, prodB[s_]], writes=[aB])
                            else:
                                g_ = ks % 2
                                ks += 1
                                S.add("dve", TT(stg[g_][:], a_[:], prod[s_][:], ALU.add), reads=[aB, prodB[s_]], writes=[stgB[g_]])
                                S.add("sp", DMA(mrg[m][:, ts(tt)], stg[g_][:]), reads=[stgB[g_]], writes=[mrgB[m]], dma=True, lane=f"stg{g_}")
        release(mM)
        release(mR)
        if dbg is not None and l == 0:
            pass
            pass

        m0 = mark()
        for c in range(NCH):
            S.add("sp", DMA(h[:, c, :], mrg[c]), reads=[mrgB[c]], writes=[hB[c]], dma=True, lane=f"h{c % 4}")
        xb = [alloc([128, T], F32, "xo") for _ in range(3)]
        xbB = [Buf(f"xo{i}") for i in range(3)]
        for nb in range(4):
            Wo, WoB = wload(colblk(w_out[l], nb * 512))
            for m4 in range(4):
                m = nb * 4 + m4
                s_ = m % 3
                S.add("sp", DMA(xb[s_][:], Xsrc[m]), reads=[XsrcB[m]], writes=[xbB[s_]], dma=True, lane=f"xo{s_}")
                for tt in range(4):
                    b = bank(0, 8)
                    proj_fm(Wo, WoB, m4 * 128, None, tt, b)
                    S.add("dve", STT(xb[s_][:, ts(tt)], ps[b][:, :], modT[:, l, 32 + m:33 + m], xb[s_][:, ts(tt)], ALU.mult, ALU.add),
                          reads=[psB[b], modB[l], xbB[s_]], writes=[xbB[s_]])
                S.add("sp", DMA(Xdst[m], xb[s_][:]), reads=[xbB[s_]], writes=[XdstB[m]], dma=True, lane=f"xo{s_}")
        release(m0)

    def mlp_phase(l, Xsrc, XsrcB, Xdst, XdstB, side=None):
        m0 = mark()
        acc = [alloc([128, 1024], F32, "macc") for _ in range(NCH)]
        accB = [Buf(f"macc{i}") for i in range(NCH)]
        hid = alloc([128, 8, 1024], BF16, "hid")
        hidB = Buf("hid")
        xb = [alloc([128, 1024], F32, "xm") for _ in range(2)]
        xbB = [Buf(f"xm{i}") for i in range(2)]
        rl = [alloc([128, 512], F32, "rl") for _ in range(2)]
        rlB = [Buf(f"rl{i}") for i in range(2)]
        cn = {"k": 0, "x": 0, "w": 0}

        def side_step():
            cn["w"] += 1
            if side is not None and cn["w"] % 4 == 0:
                next(side, None)

        def xupd(m, th):
            s_ = cn["x"] % 2
            cn["x"] += 1
            S.add("sp", DMA(xb[s_][:], Xsrc[m][:, ts(th, 1024)]), reads=[XsrcB[m]], writes=[xbB[s_]], dma=True, lane=f"xm{s_}")
            S.add("dve", STT(xb[s_][:], acc[m][:], modT[:, l, 80 + m:81 + m], xb[s_][:], ALU.mult, ALU.add),
                  reads=[accB[m], modB[l], xbB[s_]], writes=[xbB[s_]])
            S.add("sp", DMA(Xdst[m][:, ts(th, 1024)], xb[s_][:]), reads=[xbB[s_]], writes=[XdstB[m]], dma=True, lane=f"xm{s_}")

        for th in range(2):
            for G in range(8):
                for jb in range(2):
                    W1, W1B = wload(colblk(w_m1[l], G * 1024 + jb * 512))
                    for j4 in range(4):
                        j = jb * 4 + j4
                        for t2 in range(2):
                            tt = th * 2 + t2
                            b = bank(0, 3)
                            proj_fm(W1, W1B, j4 * 128, None, tt, b)
                            r_ = cn["k"] % 2
                            cn["k"] += 1
                            S.add("act", ACT(rl[r_][:], ps[b][:, :], AF.Relu), reads=[psB[b]], writes=[rlB[r_]])
                            S.add("dve", TT(hid[:, j, ts(t2)], rl[r_][:], rl[r_][:], ALU.mult), reads=[rlB[r_]], writes=[hidB])
                    side_step()
                for nb in range(4):
                    W2, W2B = wload(w_m2[l][G * 1024:(G + 1) * 1024, nb * 512:(nb + 1) * 512].rearrange("(j p) n -> p j n", p=128), "r8")
                    for m4 in range(4):
                        m = nb * 4 + m4
                        if th == 1 and G == 0:
                            xupd(m, 0)
                        for t2 in range(2):
                            b = bank(4, 7)
                            S.add("pe", [MM(ps[b][:, :], W2[:, j, m4 * 128:(m4 + 1) * 128], hid[:, j, ts(t2)], j == 0, j == 7)
                                         for j in range(8)], reads=[W2B, hidB], writes=[psB[b]])
                            if G == 0:
                                S.add("act", ACT(acc[m][:, ts(t2)], ps[b][:, :], AF.Identity), reads=[psB[b]], writes=[accB[m]])
                            else:
                                S.add("dve", TT(acc[m][:, ts(t2)], acc[m][:, ts(t2)], ps[b][:, :], ALU.add),
                                      reads=[psB[b], accB[m]], writes=[accB[m]])
                    side_step()
        for m in range(NCH):
            xupd(m, 1)
        if side is not None:
            for _ in side:
                pass
        release(m0)

    if dbg is not None:
        dbgt = alloc([128, dbg], F32, "dbgt")
        dbgtB = Buf("dbgt")
        R0 = mark()
    nl = len(layers)
    for _ in mod_gen(layers[0]):
        pass
    for li, l in enumerate(layers):
        Xa, XaB = (xin, xinB) if li == 0 else (xs2, xs2B)
        Xb, XbB = (xs1, xs1B) if li == 0 else (xs3, xs3B)
        Xc, XcB = (yout, youtB) if li == nl - 1 else (xs2, xs2B)
        norm_phase(Xa, XaB, l, 0, 0)
        mixer_phase(l, Xa, XaB, Xb, XbB)
        norm_phase(Xb, XbB, l, 16, 48)
        side = mod_gen(layers[li + 1]) if li + 1 < nl else None
        mlp_phase(l, Xb, XbB, Xc, XcB, side)
    if dbg is not None:
        S.add("sp", DMA(dbg32[:, :], dbgt[:]), reads=[dbgtB], writes=[dbgB], dma=True, lane="dbg")
    S.barrier()
    S.add("sp", [])
    semkeys = S.finalize()
    return nc, S, semkeys


def emit(nc, S, semkeys):
    from contextlib import ExitStack
    with ExitStack() as es:
        sems = {}
        for i, k in enumerate(semkeys):
            sems[k] = es.enter_context(nc.semaphore(f"sm{i}"))
        block = es.enter_context(nc.Block())
        by_eng = {e: [] for e in ENGS}
        for op in S.ops:
            by_eng[op.eng].append(op)

        def run(eng_name, eng):
            seen = {}
            for op in by_eng[eng_name]:
                waits = {}
                for d in op.deps:
                    dop = S.ops[d]
                    if dop.ev is None:
                        continue
                    if eng_name == "pe" and dop.eng == "pe" and not dop.dma:
                        continue
                    k, v = dop.ev
                    if waits.get(k, 0) < v:
                        waits[k] = v
                for k, v in waits.items():
                    if seen.get(k, 0) >= v:
                        continue
                    eng.wait_ge(sems[k], v)
                    seen[k] = v
                n = len(op.fns)
                for i, fn in enumerate(op.fns):
                    ins = fn(eng)
                    if i == n - 1 and op.sig:
                        ins.then_inc(sems[op.ev[0]], 16 if op.dma else 1)

        @block.tensor
        def _(e):
            run("pe", e)

        @block.scalar
        def _(e):
            run("act", e)

        @block.vector
        def _(e):
            run("dve", e)

        @block.gpsimd
        def _(e):
            run("pool", e)

        @block.sync
        def _(e):
            run("sp", e)
    return nc


def _pack_pp(inp):
    pp = np.zeros((L, 128, NPP), np.float32)
    for l in range(L):
        p = pp[l]
        p[:, C_BADA:C_BADA + 96] = inp["b_ada"][l].reshape(96, 128).T
        p[:, C_GAIN:C_GAIN + 16] = inp["norm_gain"][l, 0].reshape(16, 128).T
        p[:, C_GAIN + 16:C_GAIN + 32] = inp["norm_gain"][l, 1].reshape(16, 128).T
        p[:, C_BGATE:C_BGATE + 64] = inp["b_gate"][l].reshape(64, 128).T
        p[:, C_DW:C_DW + 124] = inp["conf_dw"][l].reshape(31, 4, 128).transpose(2, 1, 0).reshape(128, 124)
        p[:, C_DB:C_DB + 4] = inp["conf_db"][l].reshape(4, 128).T
        p[:, C_LNG:C_LNG + 4] = inp["conf_ln_g"][l].reshape(4, 128).T
        p[:, C_LNB:C_LNB + 4] = inp["conf_ln_b"][l].reshape(4, 128).T
        p[:, C_PSC:C_PSC + 4] = inp["pool_scale"][l].reshape(4, 128).T
        p[:, C_SW:C_SW + 12] = inp["sconv_w"][l].reshape(3, 4, 128).transpose(2, 1, 0).reshape(128, 12)
        p[0:64, C_QG] = inp["q_gain"][l]
        p[64:128, C_QG] = inp["q_gain"][l]
        p[0:64, C_KG] = inp["k_gain"][l]
        p[64:128, C_KG] = inp["k_gain"][l]
        p[0:8, C_BF] = inp["b_f"][l]
        for g, w in enumerate(POOLW):
            for t in range(16):
                p[:, C_CORR + g * 16 + t] = float(w) / float(min(t + 1, w))
    return pp


def _masks():
    k = np.arange(128)[:, None]
    q = np.arange(512)[None, :]
    return np.stack([np.where(q >= k + 128 * i, 0.0, -30000.0).astype(np.float32) for i in range(4)], axis=0)


def make_in_maps(inp, xT_list):
    pp = _pack_pp(inp)
    masks = _masks()
    shared = {
        "w_ada": np.ascontiguousarray(inp["w_ada"], dtype=np.float32),
        "w_in": np.ascontiguousarray(inp["w_in"], dtype=np.float32),
        "w_branch": np.ascontiguousarray(inp["w_branch"], dtype=np.float32),
        "w_out": np.ascontiguousarray(inp["w_out"], dtype=np.float32),
        "w_mlp1": np.ascontiguousarray(inp["w_mlp1"], dtype=np.float32),
        "w_mlp2": np.ascontiguousarray(inp["w_mlp2"], dtype=np.float32),
        "pool_w": np.ascontiguousarray(inp["pool_w"], dtype=np.float32),
        "pp": pp,
        "masks": masks,
        "ident": np.eye(128, dtype=np.float32),
        "swapm": np.roll(np.eye(128, dtype=np.float32), 64, axis=0),
    }
    maps = []
    for b in range(8):
        m = dict(shared)
        m["xT"] = xT_list[b]
        m["cT"] = np.ascontiguousarray(np.asarray(inp["c"][b], np.float32).reshape(KC, 128).T)
        maps.append(m)
    return maps


_NC_CACHE = {}


def get_nc(layers=(0, 1), dbg=None):
    key = (tuple(layers), dbg)
    if key not in _NC_CACHE:
        nc, S, semkeys = build_nc(layers=layers, dbg=dbg)
        emit(nc, S, semkeys)
        _NC_CACHE[key] = nc
    return _NC_CACHE[key]


def kernel(**inputs):
    inp = {k: np.asarray(v) for k, v in inputs.items()}
    x = np.asarray(inp["x"], np.float32)
    xT = [np.ascontiguousarray(x[b].T).reshape(NCH, 128, T) for b in range(8)]
    nc = get_nc((0, 1))
    maps = make_in_maps(inp, xT)
    res = run_bass_kernel_spmd(nc, maps, core_ids=list(range(8)))
    out = np.empty((8, T, D), np.float32)
    for b in range(8):
        out[b] = res.results[b]["yT"].reshape(D, T).T
    return out
```

```python
import numpy as np
import concourse.bass as bass
import concourse.mybir as mybir
from concourse.bass_utils import run_bass_kernel_spmd

F32 = mybir.dt.float32
BF16 = mybir.dt.bfloat16
AF = mybir.ActivationFunctionType
ALU = mybir.AluOpType

D = 2048
T = 2048
L = 2
NCH = 16
KC = 16
IN_COLS = 12808
Q0, K0, V0, F0, A0, G0, P0, SX0, SB0, SC0, GATE0 = 0, 512, 1024, 1536, 1544, 2056, 2568, 3080, 3592, 4104, 4616
DFF = 8192
C_BADA, C_GAIN, C_BGATE, C_DW, C_DB, C_LNG, C_LNB, C_PSC, C_SW, C_QG, C_KG, C_BF, C_CORR = (
    0, 96, 128, 192, 316, 320, 324, 328, 332, 344, 345, 346, 347)
NPP = 416
POOLW = (2, 4, 8, 16)
SB_LO, SB_HI = 16512, 229344


class Buf:
    __slots__ = ("name", "w", "r")

    def __init__(self, name):
        self.name = name
        self.w = None
        self.r = []


class Op:
    __slots__ = ("eng", "fns", "deps", "dma", "lane", "ev", "sig")

    def __init__(self, eng, fns, deps, dma, lane):
        self.eng, self.fns, self.deps, self.dma, self.lane = eng, fns, deps, dma, lane
        self.ev = None
        self.sig = False


ENGS = ("pe", "act", "dve", "pool", "sp")


class Sched:
    def __init__(self):
        self.ops = []
        self.pending = {e: None for e in ENGS}
        self.last = {}
        self.dma_since = []

    def add(self, eng, fns, reads=(), writes=(), dma=False, lane=None, nobar=False):
        idx = len(self.ops)
        deps = set()
        for b in reads:
            if b.w is not None:
                deps.add(b.w)
        for b in writes:
            if b.w is not None:
                deps.add(b.w)
            deps.update(b.r)
        for b in reads:
            b.r.append(idx)
        for b in writes:
            b.w = idx
            b.r = []
        if self.pending[eng] is not None and not nobar:
            deps.update(self.pending[eng])
            self.pending[eng] = None
        deps.discard(idx)
        self.ops.append(Op(eng, fns if isinstance(fns, list) else [fns], deps, dma, lane))
        if dma:
            self.dma_since.append(idx)
        else:
            self.last[eng] = idx
        return idx

    def barrier(self):
        deps = set(self.last.values()) | set(self.dma_since)
        for e in ENGS:
            if self.pending[e] is None:
                self.pending[e] = set(deps)
            else:
                self.pending[e] |= deps
        self.dma_since = []

    def finalize(self):
        needed = set()
        for op in self.ops:
            needed |= op.deps
        cnt = {e: 0 for e in ENGS}
        lanes = {}
        for i, op in enumerate(self.ops):
            if op.dma:
                lanes[op.lane] = lanes.get(op.lane, 0) + 16
                op.ev = (("L", op.lane), lanes[op.lane])
                op.sig = True
            elif i in needed:
                cnt[op.eng] += 1
                ep = (cnt[op.eng] - 1) // 30000
                op.ev = (("E", op.eng, ep), cnt[op.eng] - 30000 * ep)
                op.sig = True
        semkeys = []
        for op in self.ops:
            if op.ev is not None and op.ev[0] not in semkeys:
                semkeys.append(op.ev[0])
        return semkeys


def build_nc(layers=(0, 1), first=True, last=True, dbg=None):
    nc = bass.Bass("TRN2", target_bir_lowering=False)
    S = Sched()

    xin = nc.dram_tensor("xT", [NCH, 128, T], F32, kind="ExternalInput").ap()
    cT_d = nc.dram_tensor("cT", [128, KC], F32, kind="ExternalInput").ap()
    w_ada = nc.dram_tensor("w_ada", [L, D, 6 * D], F32, kind="ExternalInput").ap()
    w_in = nc.dram_tensor("w_in", [L, D, IN_COLS], F32, kind="ExternalInput").ap()
    w_br = nc.dram_tensor("w_branch", [L, 4, 512, D], F32, kind="ExternalInput").ap()
    w_out = nc.dram_tensor("w_out", [L, D, D], F32, kind="ExternalInput").ap()
    w_m1 = nc.dram_tensor("w_mlp1", [L, D, DFF], F32, kind="ExternalInput").ap()
    w_m2 = nc.dram_tensor("w_mlp2", [L, DFF, D], F32, kind="ExternalInput").ap()
    pool_w = nc.dram_tensor("pool_w", [L, 4, 128, 128], F32, kind="ExternalInput").ap()
    pp_d = nc.dram_tensor("pp", [L, 128, NPP], F32, kind="ExternalInput").ap()
    masks_d = nc.dram_tensor("masks", [4, 128, 512], F32, kind="ExternalInput").ap()
    ident_d = nc.dram_tensor("ident", [128, 128], F32, kind="ExternalInput").ap()
    swap_d = nc.dram_tensor("swapm", [128, 128], F32, kind="ExternalInput").ap()
    yout = nc.dram_tensor("yT", [NCH, 128, T], F32, kind="ExternalOutput").ap()
    xs1 = nc.dram_tensor("xs1", [NCH, 128, T], F32, kind="Internal").ap()
    xs2 = nc.dram_tensor("xs2", [NCH, 128, T], F32, kind="Internal").ap()
    xs3 = nc.dram_tensor("xs3", [NCH, 128, T], F32, kind="Internal").ap()
    mrg = nc.dram_tensor("mrg", [NCH, 128, T], BF16, kind="Internal").ap()
    cumsc = nc.dram_tensor("cumsc", [2, 8, 3, T], BF16, kind="Internal").ap()
    if dbg is not None:
        dbg32 = nc.dram_tensor("dbg32", [128, dbg], F32, kind="ExternalOutput").ap()
        dbgB = Buf("dbg32")
    xinB = [Buf(f"xin{c}") for c in range(NCH)]
    xs1B = [Buf(f"xs1_{c}") for c in range(NCH)]
    xs2B = [Buf(f"xs2_{c}") for c in range(NCH)]
    xs3B = [Buf(f"xs3_{c}") for c in range(NCH)]
    youtB = [Buf(f"yout{c}") for c in range(NCH)]
    mrgB = [Buf(f"mrg{c}") for c in range(NCH)]
    cumB = Buf("cumsc")

    st = {"off": SB_LO, "n": 0}

    def alloc(shape, dtype, name="t"):
        per = 1
        for s_ in shape[1:]:
            per *= s_
        nbytes = per * (4 if dtype == F32 else 2)
        nbytes = (nbytes + 63) // 64 * 64
        st["n"] += 1
        t = nc.alloc_sbuf_tensor_at(f"{name}_{st['n']}", list(shape), dtype, offset=st["off"])
        st["off"] += nbytes
        assert st["off"] <= SB_HI, f"SBUF overflow at {name}: {st['off']}"
        return t

    def mark():
        return st["off"]

    def release(m):
        st["off"] = m
        S.barrier()

    ps = [nc.alloc_psum_tensor(f"ps{i}", [128, 512], F32) for i in range(8)]
    psB = [Buf(f"ps{i}") for i in range(8)]
    pst = {"i": 0}

    def bank(lo=0, hi=8):
        n = hi - lo
        b = lo + pst["i"] % n
        pst["i"] += 1
        return b

    def MM(out, lhsT, rhs, start, stop):
        return lambda e: e.matmul(out, lhsT, rhs, start=start, stop=stop)

    def ACT(out, in_, func, bias=None, scale=None):
        kw = {}
        if bias is not None:
            kw["bias"] = bias
        if scale is not None:
            kw["scale"] = scale
        return lambda e: e.activation(out, in_, func, **kw)

    def TT(out, a, b, op):
        return lambda e: e.tensor_tensor(out, a, b, op)

    def TS(out, a, s1, s2, op0, op1=None):
        if op1 is None:
            return lambda e: e.tensor_scalar(out, a, s1, None, op0)
        return lambda e: e.tensor_scalar(out, a, s1, s2, op0, op1)

    def STT(out, a, s, b, op0, op1):
        return lambda e: e.scalar_tensor_tensor(out, a, s, b, op0, op1)

    def CP(out, in_):
        return lambda e: e.tensor_copy(out, in_)

    def RCP(out, in_):
        return lambda e: e.reciprocal(out, in_)

    def MS(ap, v):
        return lambda e: e.memset(ap, v)

    def DMA(out, in_):
        return lambda e: e.dma_start(out=out, in_=in_)

    ts = lambda i, n=512: slice(i * n, (i + 1) * n)

    ones_bf = alloc([128, 128], BF16, "ones")
    onesB = Buf("ones")
    bd_bf = alloc([128, 128], BF16, "bd")
    one_f = alloc([1, 2], F32, "onef")
    eps_t = alloc([128, 2], F32, "eps")
    masks = alloc([128, 4, 512], F32, "masks")
    masksB = Buf("masks")
    swapm = alloc([128, 128], F32, "swapm")
    swapB = Buf("swapm")
    ident = alloc([128, 128], BF16, "ident")
    identB = Buf("ident")
    pp = alloc([128, L, NPP], F32, "pp")
    ppB = Buf("pp")
    c_f = alloc([128, KC], F32, "cf")
    c_bf = alloc([128, KC], BF16, "cbf")
    cB = Buf("c")
    modT = alloc([128, L, 96], F32, "modT")
    modB = [Buf(f"mod{l}") for l in range(L)]
    sc = alloc([128, L, 40], F32, "scal")
    scB = [Buf(f"sc{l}") for l in range(L)]
    h = alloc([128, KC, T], BF16, "h")
    hB = [Buf(f"h{c}") for c in range(KC)]
    wt = [alloc([128, KC, 512], BF16, f"w{i}") for i in range(2)]
    wB = [Buf(f"w{i}") for i in range(2)]
    wst = {"i": 0}

    def wload(src, kind="full"):
        s_ = wst["i"] % 2
        wst["i"] += 1
        if kind == "full":
            dst = wt[s_][:, :, :]
        elif kind == "c256":
            dst = wt[s_][:, :, 0:256]
        elif kind == "r8":
            dst = wt[s_][:, 0:8, :]
        S.add("pool", DMA(dst, src), writes=[wB[s_]], dma=True, lane=f"w{s_}", nobar=True)
        return wt[s_], wB[s_]

    def colblk(wap, c0, n=512):
        return wap[:, c0:c0 + n].rearrange("(kc p) n -> p kc n", p=128)

    R0 = mark()

    S.add("dve", MS(ones_bf[:], 1.0), writes=[onesB])
    S.add("dve", MS(bd_bf[:], 0.0), writes=[onesB])
    S.add("dve", MS(bd_bf[0:64, 0:64], 1.0), writes=[onesB])
    S.add("dve", MS(bd_bf[64:128, 64:128], 1.0), writes=[onesB])
    S.add("dve", [MS(one_f[:], 1.0), MS(eps_t[:, 0:1], 1e-6), MS(eps_t[:, 1:2], 1e-5)], writes=[onesB])
    S.add("sp", DMA(masks[:], masks_d.rearrange("i k q -> k i q")), writes=[masksB], dma=True, lane="masks")
    S.add("sp", DMA(pp[:], pp_d.rearrange("l p n -> p l n")), writes=[ppB], dma=True, lane="pp")
    S.add("sp", DMA(c_f[:], cT_d), writes=[cB], dma=True, lane="c")
    S.add("pool", DMA(ident[:], ident_d), writes=[identB], dma=True, lane="ident")
    S.add("sp", DMA(swapm[:], swap_d), writes=[swapB], dma=True, lane="swapm")
    S.add("dve", CP(c_bf[:], c_f[:]), reads=[cB], writes=[cB])

    def dump(ap, col, n, bufs):
        if dbg is None:
            return
        S.add("sp", DMA(dbg32[0:ap.shape[0], col:col + n], ap), reads=bufs, writes=[dbgB], dma=True, lane="dbg")

    modrow = alloc([1, 512], F32, "modrow")
    modrowB = Buf("modrow")

    def mod_gen(l, blocks=range(24), part="all"):
        bT, bR = 7, 3
        blocks = list(blocks)

        def transposes(j):
            S.add("pe", [MM(ps[bT][:, j * 4 + i:j * 4 + i + 1], modrow[0:1, ts(i, 128)], one_f[0:1, 0:1], True, True)
                         for i in range(4)], reads=[modrowB, onesB], writes=[psB[bT]])

        prev = None
        for j in blocks:
            if prev is not None:
                transposes(prev)
            Wt, WB = wload(colblk(w_ada[l], j * 512))
            S.add("pe", [MM(ps[bR][0:1, :], c_bf[:, kc:kc + 1], Wt[:, kc, :], kc == 0, kc == KC - 1) for kc in range(KC)],
                  reads=[cB, WB], writes=[psB[bR]])
            S.add("act", ACT(modrow[:], ps[bR][0:1, :], AF.Identity), reads=[psB[bR]], writes=[modrowB])
            prev = j
            yield j
        transposes(prev)
        c0, c1 = blocks[0] * 4, blocks[-1] * 4 + 4
        S.add("dve", TT(modT[:, l, c0:c1], ps[bT][:, c0:c1], pp[:, l, C_BADA + c0:C_BADA + c1], ALU.add),
              reads=[psB[bT], ppB], writes=[modB[l]])
        fns = []
        if part in ("all", "first"):
            fns.append(TS(sc[:, l, 0:16], modT[:, l, 16:32], 1.0, None, ALU.add))
            fns.append(TT(sc[:, l, 0:16], sc[:, l, 0:16], pp[:, l, C_GAIN:C_GAIN + 16], ALU.mult))
            fns.append(TS(sc[:, l, 32:33], pp[:, l, C_QG:C_QG + 1], 0.125, None, ALU.mult))
            fns.append(CP(sc[:, l, 33:34], pp[:, l, C_KG:C_KG + 1]))
            fns.append(TS(sc[:, l, 34:35], pp[:, l, C_BF:C_BF + 1], -1.0, None, ALU.mult))
        if part in ("all", "second"):
            fns.append(TS(sc[:, l, 16:32], modT[:, l, 64:80], 1.0, None, ALU.add))
            fns.append(TT(sc[:, l, 16:32], sc[:, l, 16:32], pp[:, l, C_GAIN + 16:C_GAIN + 32], ALU.mult))
        for f_ in fns:
            S.add("dve", f_, reads=[modB[l], ppB], writes=[scB[l]])
        yield 24

    def norm_phase(X, XB, l, aoff, shoff):
        m0 = mark()
        xb = [alloc([128, T], F32, "xb") for _ in range(3)]
        xbB = [Buf(f"xb{i}") for i in range(3)]
        sq = [alloc([128, T], BF16, "sq") for _ in range(2)]
        sqB = [Buf(f"sq{i}") for i in range(2)]
        rstd = alloc([128, T], F32, "rstd")
        rstdB = Buf("rstd")
        tmp = [alloc([128, T], F32, "ntmp") for _ in range(2)]
        tmpB = [Buf(f"ntmp{i}") for i in range(2)]
        for c in range(NCH):
            s_ = c % 3
            S.add("sp", DMA(xb[s_][:], X[c]), reads=[XB[c]], writes=[xbB[s_]], dma=True, lane=f"xb{s_}")
            S.add("act", ACT(sq[c % 2][:], xb[s_][:], AF.Square), reads=[xbB[s_]], writes=[sqB[c % 2]])
            S.add("pe", [MM(ps[tt][:, :], ones_bf[:, :], sq[c % 2][:, ts(tt)], c == 0, c == NCH - 1) for tt in range(4)],
                  reads=[sqB[c % 2], onesB], writes=[psB[0], psB[1], psB[2], psB[3]])
        for tt in range(4):
            S.add("act", ACT(rstd[:, ts(tt)], ps[tt][:, :], AF.Ln, bias=eps_t[:, 0:1], scale=1.0 / D),
                  reads=[psB[tt], onesB], writes=[rstdB])
        S.add("act", ACT(rstd[:], rstd[:], AF.Exp, scale=-0.5), reads=[rstdB], writes=[rstdB])
        for c in range(NCH):
            s_ = c % 3
            S.add("sp", DMA(xb[s_][:], X[c]), reads=[XB[c]], writes=[xbB[s_]], dma=True, lane=f"xb{s_}")
            S.add("dve", TT(tmp[c % 2][:], xb[s_][:], rstd[:], ALU.mult), reads=[xbB[s_], rstdB], writes=[tmpB[c % 2]])
            S.add("act", ACT(h[:, c, :], tmp[c % 2][:], AF.Identity, bias=modT[:, l, shoff + c:shoff + c + 1],
                             scale=sc[:, l, aoff + c:aoff + c + 1]),
                  reads=[tmpB[c % 2], modB[l], scB[l]], writes=[hB[c]])
        release(m0)

    def proj_fm(Wt, WB, col0, m, tt, b, rows=128):
        S.add("pe", [MM(ps[b][0:rows, :], Wt[:, kc, col0:col0 + rows], h[:, kc, ts(tt)], kc == 0, kc == KC - 1)
                     for kc in range(KC)], reads=hB + [WB], writes=[psB[b]])

    def mixer_phase(l, Xsrc, XsrcB, Xdst, XdstB):
        mR = mark()
        ya = alloc([128, 4, T], BF16, "yatt")
        yaB = Buf("yatt")
        mA = mark()
        mF = mark()
        wf = alloc([128, KC, 8], BF16, "wf")
        wfB = Buf("wf")
        fA = alloc([8, T], F32, "fA")
        fC = alloc([8, T], F32, "fC")
        fO = alloc([8, T], F32, "fO")
        fR = alloc([8, T], F32, "fR")
        ksp = alloc([8, 3, T], BF16, "ksp")
        qsp = alloc([8, 3, T], BF16, "qsp")
        fB_ = Buf("fstuff")
        S.add("pool", DMA(wf[:], colblk(w_in[l], F0, 8)), writes=[wfB], dma=True, lane="wf")
        S.add("dve", MS(fO[:], 1.0), writes=[fB_])
        for tt in range(4):
            b = bank(0, 4)
            S.add("pe", [MM(ps[b][0:8, :], wf[:, kc, :], h[:, kc, ts(tt)], kc == 0, kc == KC - 1) for kc in range(KC)],
                  reads=hB + [wfB], writes=[psB[b]])
            S.add("act", ACT(fA[:, ts(tt)], ps[b][0:8, :], AF.Exp, bias=sc[0:8, l, 34:35], scale=-1.0),
                  reads=[psB[b], scB[l]], writes=[fB_])
        S.add("act", ACT(fA[:], fA[:], AF.Ln, bias=1.0), reads=[fB_], writes=[fB_])
        S.add("dve", lambda e: e.tensor_tensor_scan(fC[:], fO[:], fA[:], 0.0, ALU.mult, ALU.add), reads=[fB_], writes=[fB_])
        for f_ in (CP(ksp[:, 0, :], fC[:]), TT(fR[:], fC[:], ksp[:, 0, :], ALU.subtract), CP(ksp[:, 1, :], fR[:]),
                   TT(fR[:], fR[:], ksp[:, 1, :], ALU.subtract), CP(ksp[:, 2, :], fR[:]),
                   TS(qsp[:, 0, :], ksp[:, 0, :], -1.0, None, ALU.mult), TS(qsp[:, 1, :], ksp[:, 1, :], -1.0, None, ALU.mult),
                   TS(qsp[:, 2, :], ksp[:, 2, :], -1.0, None, ALU.mult)):
            S.add("dve", f_, reads=[fB_], writes=[fB_])
        S.add("sp", DMA(cumsc[0], ksp[:]), reads=[fB_], writes=[cumB], dma=True, lane="ksp")
        S.add("sp", DMA(cumsc[1], qsp[:]), reads=[fB_], writes=[cumB], dma=True, lane="qsp")
        if dbg is not None and l == 0:
            pass
            pass
        release(mF)
        V = alloc([128, 16, 4, 2, 128], BF16, "Vaug")
        VB = Buf("V")
        S.add("dve", MS(V[:, :, :, 0, 64:128], 1.0), writes=[VB])
        S.add("dve", MS(V[:, :, :, 1, 0:64], 1.0), writes=[VB])
        Wv, WvB = wload(colblk(w_in[l], V0))
        for tc in range(16):
            b = bank(0, 4)
            S.add("pe", [MM(ps[b][:, :], h[:, kc, ts(tc, 128)], Wv[:, kc, :], kc == 0, kc == KC - 1) for kc in range(KC)],
                  reads=hB + [WvB], writes=[psB[b]])
            pv = ps[b][:, :].rearrange("p (hp two d) -> p hp two d", two=2, d=64)
            S.add("act", ACT(V[:, tc, :, 0, 0:64], pv[:, :, 0, :], AF.Identity), reads=[psB[b]], writes=[VB])
            S.add("dve", CP(V[:, tc, :, 1, 64:128], pv[:, :, 1, :]), reads=[psB[b]], writes=[VB])
        qa = [alloc([128, T], BF16, "qa") for _ in range(4)]
        ka = [alloc([128, T], BF16, "ka") for _ in range(4)]
        qaB = [Buf(f"qa{i}") for i in range(4)]
        kaB = [Buf(f"ka{i}") for i in range(4)]
        sqq = [alloc([128, 512], BF16, "sqq") for _ in range(2)]
        sqqB = [Buf(f"sqq{i}") for i in range(2)]
        rs = [alloc([128, 512], F32, "rs") for _ in range(2)]
        rsB = [Buf(f"rs{i}") for i in range(2)]
        pt = [alloc([128, 512], BF16, "pt") for _ in range(3)]
        ptB = [Buf(f"pt{i}") for i in range(3)]
        rden = [alloc([128, 512], F32, "rden") for _ in range(2)]
        rdenB = [Buf(f"rden{i}") for i in range(2)]
        cnt = {"s": 0, "p": 0, "o": 0, "m": 0}
        for half in range(2):
            Wq, WqB = wload(colblk(w_in[l], Q0 + half * 256, 256), "c256")
            Wk, WkB = wload(colblk(w_in[l], K0 + half * 256, 256), "c256")
            for i in range(4):
                hd = half * 4 + i
                if i % 2 == 0:
                    S.add("dve", [MS(qa[i][64:70, :], 1.0)], writes=[qaB[i]])
                    S.add("dve", [MS(ka[i][64:70, :], 1.0)], writes=[kaB[i]])
                    S.add("sp", DMA(ka[i][64:67, :], cumsc[0, hd]), reads=[cumB], writes=[kaB[i]], dma=True, lane=f"ka{i}")
                    S.add("sp", DMA(qa[i][67:70, :], cumsc[1, hd]), reads=[cumB], writes=[qaB[i]], dma=True, lane=f"qa{i}")
                else:
                    S.add("dve", MS(qa[i][0:64, :], 0.0), writes=[qaB[i]])
                    S.add("dve", MS(qa[i][0:6, :], 1.0), writes=[qaB[i]])
                    S.add("dve", MS(ka[i][0:64, :], 0.0), writes=[kaB[i]])
                    S.add("dve", MS(ka[i][0:6, :], 1.0), writes=[kaB[i]])
                    S.add("sp", DMA(ka[i][0:3, :], cumsc[0, hd]), reads=[cumB], writes=[kaB[i]], dma=True, lane=f"ka{i}")
                    S.add("sp", DMA(qa[i][3:6, :], cumsc[1, hd]), reads=[cumB], writes=[qaB[i]], dma=True, lane=f"qa{i}")
            for pl in range(2):
                ie, io = 2 * pl, 2 * pl + 1
                for (dt_, dtB, Wt, WB, gcol) in ((qa, qaB, Wq, WqB, 32), (ka, kaB, Wk, WkB, 33)):
                    for tt in range(4):
                        b = bank(0, 2)
                        proj_fm(Wt, WB, pl * 128, None, tt, b, rows=128)
                        s_ = cnt["s"] % 2
                        cnt["s"] += 1
                        S.add("act", ACT(sqq[s_][:], ps[b][:, :], AF.Square), reads=[psB[b]], writes=[sqqB[s_]])
                        b2 = 2 + s_
                        S.add("pe", MM(ps[b2][:, :], bd_bf[:, :], sqq[s_][:], True, True),
                              reads=[sqqB[s_], onesB], writes=[psB[b2]])
                        S.add("act", ACT(rs[s_][:], ps[b2][:, :], AF.Ln, bias=eps_t[:, 0:1], scale=1.0 / 64),
                              reads=[psB[b2], onesB], writes=[rsB[s_]])
                        S.add("act", ACT(rs[s_][:], rs[s_][:], AF.Exp, scale=-0.5), reads=[rsB[s_]], writes=[rsB[s_]])
                        S.add("dve", STT(dt_[ie][0:64, ts(tt)], ps[b][0:64, :], sc[0:64, l, gcol:gcol + 1], rs[s_][0:64, :],
                                         ALU.mult, ALU.mult), reads=[psB[b], rsB[s_], scB[l]], writes=[dtB[ie]])
                        S.add("dve", STT(dt_[io][64:128, ts(tt)], ps[b][64:128, :], sc[64:128, l, gcol:gcol + 1], rs[s_][64:128, :],
                                         ALU.mult, ALU.mult), reads=[psB[b], rsB[s_], scB[l]], writes=[dtB[io]])
            if dbg is not None and l == 0 and half == 0:
                pass
                pass
            units = []
            for pl in range(2):
                for j in range(4):
                    for par in range(2):
                        for ik in range(4 * j + 4):
                            units.append((pl, j, par, ik))
            DEPTH = 3
            ring = {"n": 0}
            ubank = {}

            def ring_bank():
                b_ = ring["n"] % 4
                ring["n"] += 1
                return b_

            def c0_of(j, ik):
                return 128 * (ik - 4 * j) if ik >= 4 * j else 0

            def emit_S(n):
                pl, j, par, ik = units[n]
                i = pl * 2 + par
                sbk = ring_bank()
                ubank[n] = sbk
                kr = 70 if i % 2 == 0 else 128
                c0 = c0_of(j, ik)
                S.add("pe", MM(ps[sbk][:, c0:512], ka[i][0:kr, ts(ik, 128)], qa[i][0:kr, j * 512 + c0:(j + 1) * 512], True, True),
                      reads=[kaB[i], qaB[i]], writes=[psB[sbk]])

            finals = []

            def fin1(pl, j, o_):
                A, Bk = 4 + 2 * o_, 5 + 2 * o_
                S.add("act", ACT(rden[o_][64:128, :], ps[A][64:128, :], AF.Ln), reads=[psB[A]], writes=[rdenB[o_]])
                S.add("act", ACT(rden[o_][0:64, :], ps[Bk][0:64, :], AF.Ln), reads=[psB[Bk]], writes=[rdenB[o_]])
                S.add("act", ACT(rden[o_][:, :], rden[o_][:, :], AF.Exp, scale=-1.0), reads=[rdenB[o_]], writes=[rdenB[o_]])

            def fin2(pl, j, o_):
                A, Bk = 4 + 2 * o_, 5 + 2 * o_
                pair = half * 2 + pl
                pb_ = ring_bank()
                S.add("pe", MM(ps[pb_][:, :], swapm[:, :], rden[o_][:, :], True, True), reads=[rdenB[o_], swapB], writes=[psB[pb_]])
                S.add("act", ACT(rden[o_][:, :], ps[pb_][:, :], AF.Identity), reads=[psB[pb_]], writes=[rdenB[o_]])
                S.add("dve", TT(ya[0:64, pair, ts(j)], ps[A][0:64, :], rden[o_][0:64, :], ALU.mult),
                      reads=[psB[A], rdenB[o_]], writes=[yaB])
                S.add("dve", TT(ya[64:128, pair, ts(j)], ps[Bk][64:128, :], rden[o_][64:128, :], ALU.mult),
                      reads=[psB[Bk], rdenB[o_]], writes=[yaB])

            for n in range(min(DEPTH, len(units))):
                emit_S(n)
            for n, (pl, j, par, ik) in enumerate(units):
                i = pl * 2 + par
                hd = half * 4 + i
                nk = 4 * j + 4
                if ik == 0 and par == 0:
                    o_ = cnt["o"] % 2
                    cnt["o"] += 1
                ob = 4 + 2 * o_ + par
                sbk = ubank[n]
                p_ = cnt["p"] % 3
                cnt["p"] += 1
                c0 = c0_of(j, ik)
                if ik >= 4 * j:
                    S.add("dve", TT(ps[sbk][:, c0:c0 + 128], ps[sbk][:, c0:c0 + 128], masks[:, 0, 0:128], ALU.add),
                          reads=[psB[sbk], masksB], writes=[psB[sbk]])
                S.add("act", ACT(pt[p_][:, c0:512], ps[sbk][:, c0:512], AF.Exp), reads=[psB[sbk]], writes=[ptB[p_]])
                if n + DEPTH < len(units):
                    emit_S(n + DEPTH)
                S.add("pe", MM(ps[ob][:, c0:512], V[:, ik, hd // 2, hd % 2, :], pt[p_][:, c0:512], ik == 0, ik == nk - 1),
                      reads=[VB, ptB[p_]], writes=[psB[ob]])
                for f_ in list(finals):
                    f_[0] -= 1
                    if f_[0] <= 0:
                        fin2(*f_[1])
                        finals.remove(f_)
                if ik == nk - 1 and par == 1:
                    fin1(pl, j, o_)
                    finals.append([3, (pl, j, o_)])
            for f_ in finals:
                fin2(*f_[1])
        release(mA)
        if dbg is not None and l == 0:
            S.add("dve", CP(dbgt[:, 0:64], ya[:, 0, 0:64]), reads=[yaB], writes=[dbgtB])
            S.add("dve", CP(dbgt[:, 64:128], ya[:, 3, 1984:2048]), reads=[yaB], writes=[dbgtB])

        yc = alloc([128, 4, T], BF16, "yconf")
        ycB = Buf("yconf")
        mC = mark()
        Wa, WaB = wload(colblk(w_in[l], A0))
        Wg, WgB = wload(colblk(w_in[l], G0))
        u = alloc([128, 30 + T], BF16, "u")
        uB = Buf("u")
        v = [alloc([128, T], F32, "v") for _ in range(4)]
        vB = [Buf(f"v{i}") for i in range(4)]
        sg = [alloc([128, 512], F32, "sg") for _ in range(1)]
        sgB = [Buf(f"sg{i}") for i in range(1)]
        dg = alloc([128, 31, 128], BF16, "dg")
        dgB = Buf("dg")
        S.add("dve", MS(u[:, 0:30], 0.0), writes=[uB])
        k_ = 0
        for c in range(4):
            dwc = C_DW + c * 31
            for k in range(31):
                S.add("dve", TS(dg[:, k, :], ident[:, :], pp[:, l, dwc + k:dwc + k + 1], None, ALU.mult),
                      reads=[identB, ppB], writes=[dgB])
            for tt in range(4):
                ba = bank(0, 4)
                proj_fm(Wa, WaB, c * 128, None, tt, ba)
                bg = bank(4, 8)
                proj_fm(Wg, WgB, c * 128, None, tt, bg)
                s_ = 0
                S.add("act", ACT(sg[s_][:], ps[bg][:, :], AF.Sigmoid), reads=[psB[bg]], writes=[sgB[s_]])
                S.add("dve", TT(u[:, 30 + tt * 512:30 + (tt + 1) * 512], ps[ba][:, :], sg[s_][:], ALU.mult),
                      reads=[psB[ba], sgB[s_]], writes=[uB])
            for tt in range(4):
                bc = bank(0, 8)
                S.add("pe", [MM(ps[bc][:, :], dg[:, k, :], u[:, k + tt * 512:k + tt * 512 + 512], k == 0, k == 30) for k in range(31)],
                      reads=[dgB, uB], writes=[psB[bc]])
                S.add("act", ACT(v[c][:, ts(tt)], ps[bc][:, :], AF.Identity, bias=pp[:, l, C_DB + c:C_DB + c + 1]),
                      reads=[psB[bc], ppB], writes=[vB[c]])
        vb = alloc([128, 4, 512], BF16, "vb")
        vs = alloc([128, 4, 512], BF16, "vs")
        vbB, vsB = Buf("vb"), Buf("vs")
        st_ = [alloc([128, 512], F32, f"lnst{i}") for i in range(3)]
        stB = [Buf(f"lnst{i}") for i in range(3)]
        t1 = [alloc([128, 512], F32, "t1") for _ in range(1)]
        t1B = [Buf(f"t1{i}") for i in range(1)]
        for tt in range(4):
            for c in range(4):
                S.add("act", ACT(vb[:, c, :], v[c][:, ts(tt)], AF.Identity), reads=[vB[c]], writes=[vbB])
                S.add("act", ACT(vs[:, c, :], v[c][:, ts(tt)], AF.Square), reads=[vB[c]], writes=[vsB])
            bm, bq = bank(0, 4), bank(4, 8)
            S.add("pe", [MM(ps[bm][:, :], ones_bf[:, :], vb[:, c, :], c == 0, c == 3) for c in range(4)],
                  reads=[vbB, onesB], writes=[psB[bm]])
            S.add("pe", [MM(ps[bq][:, :], ones_bf[:, :], vs[:, c, :], c == 0, c == 3) for c in range(4)],
                  reads=[vsB, onesB], writes=[psB[bq]])
            mu, musq, var = st_
            S.add("act", ACT(mu[:], ps[bm][:, :], AF.Identity, scale=1.0 / 512), reads=[psB[bm]], writes=[stB[0]])
            S.add("act", ACT(musq[:], ps[bm][:, :], AF.Square, scale=1.0 / 512), reads=[psB[bm]], writes=[stB[1]])
            S.add("dve", STT(var[:], ps[bq][:, :], 1.0 / 512, musq[:], ALU.mult, ALU.subtract),
                  reads=[psB[bq], stB[1]], writes=[stB[2]])
            S.add("act", ACT(var[:], var[:], AF.Ln, bias=eps_t[:, 1:2]), reads=[stB[2], onesB], writes=[stB[2]])
            S.add("act", ACT(var[:], var[:], AF.Exp, scale=-0.5), reads=[stB[2]], writes=[stB[2]])
            for c in range(4):
                s_ = 0
                S.add("dve", TT(v[c][:, ts(tt)], v[c][:, ts(tt)], mu[:], ALU.subtract), reads=[vB[c], stB[0]], writes=[vB[c]])
                S.add("dve", TT(v[c][:, ts(tt)], v[c][:, ts(tt)], var[:], ALU.mult), reads=[vB[c], stB[2]], writes=[vB[c]])
                S.add("act", ACT(yc[:, c, ts(tt)], v[c][:, ts(tt)], AF.Silu, bias=pp[:, l, C_LNB + c:C_LNB + c + 1],
                                 scale=pp[:, l, C_LNG + c:C_LNG + c + 1]), reads=[vB[c], ppB], writes=[ycB])
        if dbg is not None and l == 0:
            pass
        release(mC)

        ysc = alloc([128, 4, T], BF16, "ysc")
        yscB = Buf("ysc")
        mS = mark()
        Wx, WxB = wload(colblk(w_in[l], SX0))
        Wc, WcB = wload(colblk(w_in[l], SC0))
        vp = alloc([128, 2 + T], F32, "vp")
        vpB = Buf("vp")
        cv = [alloc([128, T], F32, "cv") for _ in range(4)]
        cvB = [Buf(f"cv{i}") for i in range(4)]
        xt = [alloc([128, 512], F32, "xt") for _ in range(2)]
        xtB = [Buf(f"xt{i}") for i in range(2)]
        S.add("dve", MS(vp[:, 0:2], 0.0), writes=[vpB])
        k_ = 0
        for c in range(4):
            for tt in range(4):
                bx = bank(0, 4)
                proj_fm(Wx, WxB, c * 128, None, tt, bx)
                bc = bank(4, 8)
                proj_fm(Wc, WcB, c * 128, None, tt, bc)
                s_ = k_ % 2
                k_ += 1
                S.add("act", ACT(xt[s_][:], ps[bx][:, :], AF.Identity), reads=[psB[bx]], writes=[xtB[s_]])
                S.add("dve", TT(vp[:, 2 + tt * 512:2 + (tt + 1) * 512], ps[bc][:, :], xt[s_][:], ALU.mult),
                      reads=[psB[bc], xtB[s_]], writes=[vpB])
            swc = C_SW + c * 3
            S.add("dve", TS(cv[c][:], vp[:, 2:2 + T], pp[:, l, swc + 2:swc + 3], None, ALU.mult), reads=[vpB, ppB], writes=[cvB[c]])
            S.add("dve", STT(cv[c][:], vp[:, 1:1 + T], pp[:, l, swc + 1:swc + 2], cv[c][:], ALU.mult, ALU.add),
                  reads=[vpB, ppB, cvB[c]], writes=[cvB[c]])
            S.add("dve", STT(cv[c][:], vp[:, 0:T], pp[:, l, swc:swc + 1], cv[c][:], ALU.mult, ALU.add),
                  reads=[vpB, ppB, cvB[c]], writes=[cvB[c]])
        Wb, WbB = wload(colblk(w_in[l], SB0))
        for c in range(4):
            for tt in range(4):
                bb = bank(0, 8)
                proj_fm(Wb, WbB, c * 128, None, tt, bb)
                S.add("dve", TT(ysc[:, c, ts(tt)], ps[bb][:, :], cv[c][:, ts(tt)], ALU.mult),
                      reads=[psB[bb], cvB[c]], writes=[yscB])
        if dbg is not None and l == 0:
            pass
        release(mS)

        ypl = alloc([128, 4, T], BF16, "ypool")
        yplB = Buf("ypool")
        mP = mark()
        Wp, WpB = wload(colblk(w_in[l], P0))
        pw = alloc([128, 4, 128], BF16, "poolw")
        pwB = Buf("poolw")
        S.add("pool", DMA(pw[:], pool_w[l].rearrange("g c d -> c g d")), writes=[pwB], dma=True, lane="pw")
        pb = alloc([128, 16 + T], F32, "pb")
        PA = alloc([128, 16 + T], F32, "PA")
        PBt = alloc([128, 16 + T], F32, "PB")
        pbB, PAB, PBB = Buf("pb"), Buf("PA"), Buf("PB")
        dbf = alloc([128, T], BF16, "dbf")
        dbfB = Buf("dbf")
        S.add("dve", [MS(pb[:, 0:16], 0.0)], writes=[pbB])
        S.add("dve", [MS(PA[:, 0:16], 0.0)], writes=[PAB])
        S.add("dve", [MS(PBt[:, 0:16], 0.0)], writes=[PBB])
        for g in range(4):
            wdw = POOLW[g]
            for tt in range(4):
                b = bank(0, 8)
                proj_fm(Wp, WpB, g * 128, None, tt, b)
                S.add("act", ACT(pb[:, 16 + tt * 512:16 + (tt + 1) * 512], ps[b][:, :], AF.Identity), reads=[psB[b]], writes=[pbB])
            src, srcB = pb, pbB
            bufs = [(PA, PAB), (PBt, PBB)]
            for lev in range(g + 1):
                d_ = 1 << lev
                dst, dstB = bufs[lev % 2]
                S.add("dve", TT(dst[:, 16:16 + T], src[:, 16:16 + T], src[:, 16 - d_:16 - d_ + T], ALU.add),
                      reads=[srcB], writes=[dstB])
                src, srcB = dst, dstB
            S.add("dve", TS(src[:, 16:16 + T], src[:, 16:16 + T], 1.0 / wdw, None, ALU.mult), reads=[srcB], writes=[srcB])
            S.add("dve", TT(src[:, 16:32], src[:, 16:32], pp[:, l, C_CORR + g * 16:C_CORR + (g + 1) * 16], ALU.mult),
                  reads=[srcB, ppB], writes=[srcB])
            S.add("dve", TT(dbf[:], src[:, 16:16 + T], pb[:, 16:16 + T], ALU.subtract), reads=[srcB, pbB], writes=[dbfB])
            for tt in range(4):
                b = bank(0, 8)
                S.add("pe", MM(ps[b][:, :], pw[:, g, :], dbf[:, ts(tt)], True, True), reads=[pwB, dbfB], writes=[psB[b]])
                S.add("act", ACT(ypl[:, g, ts(tt)], ps[b][:, :], AF.Identity, scale=pp[:, l, C_PSC + g:C_PSC + g + 1]),
                      reads=[psB[b], ppB], writes=[yplB])
        if dbg is not None and l == 0:
            pass
            pass
        release(mP)

        mM = mark()
        acc = [[alloc([128, 512], F32, "acc") for _ in range(4)] for _ in range(2)]
        accB = [[Buf(f"acc{a}{b_}") for b_ in range(4)] for a in range(2)]
        wbr = [alloc([128, 4, 256], BF16, "wbr") for _ in range(2)]
        wbrB = [Buf(f"wbr{i}") for i in range(2)]
        gt = [alloc([128, 512], F32, "gt") for _ in range(2)]
        gtB = [Buf(f"gt{i}") for i in range(2)]
        prod = [alloc([128, 512], F32, "prod") for _ in range(2)]
        prodB = [Buf(f"prod{i}") for i in range(2)]
        stg = [alloc([128, 512], BF16, "stg") for _ in range(2)]
        stgB = [Buf(f"stg{i}") for i in range(2)]
        ys = [(ya, yaB), (yc, ycB), (ypl, yplB), (ysc, yscB)]
        k_ = 0
        kb = 0
        ks = 0
        for mg in range(8):
            for br in range(4):
                Wgt, WgtB = wload(colblk(w_in[l], GATE0 + br * 2048 + mg * 256, 256), "c256")
                wb_ = kb % 2
                kb += 1
                S.add("pool", DMA(wbr[wb_][:], w_br[l, br][:, mg * 256:(mg + 1) * 256].rearrange("(kc p) n -> p kc n", p=128)),
                      writes=[wbrB[wb_]], dma=True, lane=f"wbr{wb_}")
                yt_, ytB = ys[br]
                for m2 in range(2):
                    m = mg * 2 + m2
                    for tt in range(4):
                        bg = bank(0, 4)
                        proj_fm(Wgt, WgtB, m2 * 128, None, tt, bg)
                        bp = bank(4, 8)
                        S.add("pe", [MM(ps[bp][:, :], wbr[wb_][:, k2, m2 * 128:(m2 + 1) * 128], yt_[:, k2, ts(tt)], k2 == 0, k2 == 3)
                                     for k2 in range(4)], reads=[wbrB[wb_], ytB], writes=[psB[bp]])
                        s_ = k_ % 2
                        k_ += 1
                        bgc = C_BGATE + br * 16 + m
                        S.add("act", ACT(gt[s_][:], ps[bg][:, :], AF.Sigmoid, bias=pp[:, l, bgc:bgc + 1]),
                              reads=[psB[bg], ppB], writes=[gtB[s_]])
                        a_, aB = acc[m2][tt], accB[m2][tt]
                        if br == 0:
                            S.add("dve", TT(a_[:], ps[bp][:, :], gt[s_][:], ALU.mult), reads=[psB[bp], gtB[s_]], writes=[aB])
                        else:
                            S.add("dve", TT(prod[s_][:], ps[bp][:, :], gt[s_][:], ALU.mult), reads=[psB[bp], gtB[s_]], writes=[prodB[s_]])
                            if br < 3:
                                S.add("dve", TT(a_[:], a_[:], prod[s_][:], ALU.add), reads=[aB, prodB[s_]], writes=[aB])
                            else:
                                g_ = ks % 2
                                ks += 1
                                S.add("dve", TT(stg[g_][:], a_[:], prod[s_][:], ALU.add), reads=[aB, prodB[s_]], writes=[stgB[g_]])
                                S.add("sp", DMA(mrg[m][:, ts(tt)], stg[g_][:]), reads=[stgB[g_]], writes=[mrgB[m]], dma=True, lane=f"stg{g_}")
        release(mM)
        release(mR)
        if dbg is not None and l == 0:
            pass
            pass

        m0 = mark()
        for c in range(NCH):
            S.add("sp", DMA(h[:, c, :], mrg[c]), reads=[mrgB[c]], writes=[hB[c]], dma=True, lane=f"h{c % 4}")
        xb = [alloc([128, T], F32, "xo") for _ in range(3)]
        xbB = [Buf(f"xo{i}") for i in range(3)]
        for nb in range(4):
            Wo, WoB = wload(colblk(w_out[l], nb * 512))
            for m4 in range(4):
                m = nb * 4 + m4
                s_ = m % 3
                S.add("sp", DMA(xb[s_][:], Xsrc[m]), reads=[XsrcB[m]], writes=[xbB[s_]], dma=True, lane=f"xo{s_}")
                for tt in range(4):
                    b = bank(0, 8)
                    proj_fm(Wo, WoB, m4 * 128, None, tt, b)
                    S.add("dve", STT(xb[s_][:, ts(tt)], ps[b][:, :], modT[:, l, 32 + m:33 + m], xb[s_][:, ts(tt)], ALU.mult, ALU.add),
                          reads=[psB[b], modB[l], xbB[s_]], writes=[xbB[s_]])
                S.add("sp", DMA(Xdst[m], xb[s_][:]), reads=[xbB[s_]], writes=[XdstB[m]], dma=True, lane=f"xo{s_}")
        release(m0)

    def mlp_phase(l, Xsrc, XsrcB, Xdst, XdstB, side=None):
        m0 = mark()
        acc = [alloc([128, 1024], F32, "macc") for _ in range(NCH)]
        accB = [Buf(f"macc{i}") for i in range(NCH)]
        hid = alloc([128, 8, 1024], BF16, "hid")
        hidB = Buf("hid")
        xb = [alloc([128, 1024], F32, "xm") for _ in range(2)]
        xbB = [Buf(f"xm{i}") for i in range(2)]
        rl = [alloc([128, 512], F32, "rl") for _ in range(2)]
        rlB = [Buf(f"rl{i}") for i in range(2)]
        cn = {"k": 0, "x": 0, "w": 0}

        def side_step():
            cn["w"] += 1
            if side is not None and cn["w"] % 4 == 0:
                next(side, None)

        def xupd(m, th):
            s_ = cn["x"] % 2
            cn["x"] += 1
            S.add("sp", DMA(xb[s_][:], Xsrc[m][:, ts(th, 1024)]), reads=[XsrcB[m]], writes=[xbB[s_]], dma=True, lane=f"xm{s_}")
            S.add("dve", STT(xb[s_][:], acc[m][:], modT[:, l, 80 + m:81 + m], xb[s_][:], ALU.mult, ALU.add),
                  reads=[accB[m], modB[l], xbB[s_]], writes=[xbB[s_]])
            S.add("sp", DMA(Xdst[m][:, ts(th, 1024)], xb[s_][:]), reads=[xbB[s_]], writes=[XdstB[m]], dma=True, lane=f"xm{s_}")

        for th in range(2):
            for G in range(8):
                for jb in range(2):
                    W1, W1B = wload(colblk(w_m1[l], G * 1024 + jb * 512))
                    for j4 in range(4):
                        j = jb * 4 + j4
                        for t2 in range(2):
                            tt = th * 2 + t2
                            b = bank(0, 3)
                            proj_fm(W1, W1B, j4 * 128, None, tt, b)
                            r_ = cn["k"] % 2
                            cn["k"] += 1
                            S.add("act", ACT(rl[r_][:], ps[b][:, :], AF.Relu), reads=[psB[b]], writes=[rlB[r_]])
                            S.add("dve", TT(hid[:, j, ts(t2)], rl[r_][:], rl[r_][:], ALU.mult), reads=[rlB[r_]], writes=[hidB])
                    side_step()
                for nb in range(4):
                    W2, W2B = wload(w_m2[l][G * 1024:(G + 1) * 1024, nb * 512:(nb + 1) * 512].rearrange("(j p) n -> p j n", p=128), "r8")
                    for m4 in range(4):
                        m = nb * 4 + m4
                        if th == 1 and G == 0:
                            xupd(m, 0)
                        for t2 in range(2):
                            b = bank(4, 7)
                            S.add("pe", [MM(ps[b][:, :], W2[:, j, m4 * 128:(m4 + 1) * 128], hid[:, j, ts(t2)], j == 0, j == 7)
                                         for j in range(8)], reads=[W2B, hidB], writes=[psB[b]])
                            if G == 0:
                                S.add("act", ACT(acc[m][:, ts(t2)], ps[b][:, :], AF.Identity), reads=[psB[b]], writes=[accB[m]])
                            else:
                                S.add("dve", TT(acc[m][:, ts(t2)], acc[m][:, ts(t2)], ps[b][:, :], ALU.add),
                                      reads=[psB[b], accB[m]], writes=[accB[m]])
                    side_step()
        for m in range(NCH):
            xupd(m, 1)
        if side is not None:
            for _ in side:
                pass
        release(m0)

    if dbg is not None:
        dbgt = alloc([128, dbg], F32, "dbgt")
        dbgtB = Buf("dbgt")
        R0 = mark()
    nl = len(layers)
    for _ in mod_gen(layers[0], range(0, 8), "first"):
        pass
    for li, l in enumerate(layers):
        Xa, XaB = (xin, xinB) if li == 0 else (xs2, xs2B)
        Xb, XbB = (xs1, xs1B) if li == 0 else (xs3, xs3B)
        Xc, XcB = (yout, youtB) if li == nl - 1 else (xs2, xs2B)
        norm_phase(Xa, XaB, l, 0, 0)
        if li == 0:
            for _ in mod_gen(l, range(8, 24), "second"):
                pass
        mixer_phase(l, Xa, XaB, Xb, XbB)
        norm_phase(Xb, XbB, l, 16, 48)
        side = mod_gen(layers[li + 1]) if li + 1 < nl else None
        mlp_phase(l, Xb, XbB, Xc, XcB, side)
    if dbg is not None:
        S.add("sp", DMA(dbg32[:, :], dbgt[:]), reads=[dbgtB], writes=[dbgB], dma=True, lane="dbg")
    S.barrier()
    S.add("sp", [])
    semkeys = S.finalize()
    return nc, S, semkeys


def emit(nc, S, semkeys):
    from contextlib import ExitStack
    with ExitStack() as es:
        sems = {}
        for i, k in enumerate(semkeys):
            sems[k] = es.enter_context(nc.semaphore(f"sm{i}"))
        block = es.enter_context(nc.Block())
        by_eng = {e: [] for e in ENGS}
        for op in S.ops:
            by_eng[op.eng].append(op)

        def run(eng_name, eng):
            seen = {}
            for op in by_eng[eng_name]:
                waits = {}
                for d in op.deps:
                    dop = S.ops[d]
                    if dop.ev is None:
                        continue
                    if eng_name == "pe" and dop.eng == "pe" and not dop.dma:
                        continue
                    k, v = dop.ev
                    if waits.get(k, 0) < v:
                        waits[k] = v
                for k, v in waits.items():
                    if seen.get(k, 0) >= v:
                        continue
                    eng.wait_ge(sems[k], v)
                    seen[k] = v
                n = len(op.fns)
                for i, fn in enumerate(op.fns):
                    ins = fn(eng)
                    if i == n - 1 and op.sig:
                        ins.then_inc(sems[op.ev[0]], 16 if op.dma else 1)

        @block.tensor
        def _(e):
            run("pe", e)

        @block.scalar
        def _(e):
            run("act", e)

        @block.vector
        def _(e):
            run("dve", e)

        @block.gpsimd
        def _(e):
            run("pool", e)

        @block.sync
        def _(e):
            run("sp", e)
    return nc


def _pack_pp(inp):
    pp = np.zeros((L, 128, NPP), np.float32)
    for l in range(L):
        p = pp[l]
        p[:, C_BADA:C_BADA + 96] = inp["b_ada"][l].reshape(96, 128).T
        p[:, C_GAIN:C_GAIN + 16] = inp["norm_gain"][l, 0].reshape(16, 128).T
        p[:, C_GAIN + 16:C_GAIN + 32] = inp["norm_gain"][l, 1].reshape(16, 128).T
        p[:, C_BGATE:C_BGATE + 64] = inp["b_gate"][l].reshape(64, 128).T
        p[:, C_DW:C_DW + 124] = inp["conf_dw"][l].reshape(31, 4, 128).transpose(2, 1, 0).reshape(128, 124)
        p[:, C_DB:C_DB + 4] = inp["conf_db"][l].reshape(4, 128).T
        p[:, C_LNG:C_LNG + 4] = inp["conf_ln_g"][l].reshape(4, 128).T
        p[:, C_LNB:C_LNB + 4] = inp["conf_ln_b"][l].reshape(4, 128).T
        p[:, C_PSC:C_PSC + 4] = inp["pool_scale"][l].reshape(4, 128).T
        p[:, C_SW:C_SW + 12] = inp["sconv_w"][l].reshape(3, 4, 128).transpose(2, 1, 0).reshape(128, 12)
        p[0:64, C_QG] = inp["q_gain"][l]
        p[64:128, C_QG] = inp["q_gain"][l]
        p[0:64, C_KG] = inp["k_gain"][l]
        p[64:128, C_KG] = inp["k_gain"][l]
        p[0:8, C_BF] = inp["b_f"][l]
        for g, w in enumerate(POOLW):
            for t in range(16):
                p[:, C_CORR + g * 16 + t] = float(w) / float(min(t + 1, w))
    return pp


def _masks():
    k = np.arange(128)[:, None]
    q = np.arange(512)[None, :]
    return np.stack([np.where(q >= k + 128 * i, 0.0, -30000.0).astype(np.float32) for i in range(4)], axis=0)


def make_in_maps(inp, xT_list):
    pp = _pack_pp(inp)
    masks = _masks()
    shared = {
        "w_ada": np.ascontiguousarray(inp["w_ada"], dtype=np.float32),
        "w_in": np.ascontiguousarray(inp["w_in"], dtype=np.float32),
        "w_branch": np.ascontiguousarray(inp["w_branch"], dtype=np.float32),
        "w_out": np.ascontiguousarray(inp["w_out"], dtype=np.float32),
        "w_mlp1": np.ascontiguousarray(inp["w_mlp1"], dtype=np.float32),
        "w_mlp2": np.ascontiguousarray(inp["w_mlp2"], dtype=np.float32),
        "pool_w": np.ascontiguousarray(inp["pool_w"], dtype=np.float32),
        "pp": pp,
        "masks": masks,
        "ident": np.eye(128, dtype=np.float32),
        "swapm": np.roll(np.eye(128, dtype=np.float32), 64, axis=0),
    }
    maps = []
    for b in range(8):
        m = dict(shared)
        m["xT"] = xT_list[b]
        m["cT"] = np.ascontiguousarray(np.asarray(inp["c"][b], np.float32).reshape(KC, 128).T)
        maps.append(m)
    return maps


_NC_CACHE = {}


def get_nc(layers=(0, 1), dbg=None):
    key = (tuple(layers), dbg)
    if key not in _NC_CACHE:
        nc, S, semkeys = build_nc(layers=layers, dbg=dbg)
        emit(nc, S, semkeys)
        _NC_CACHE[key] = nc
    return _NC_CACHE[key]


def kernel(**inputs):
    inp = {k: np.asarray(v) for k, v in inputs.items()}
    x = np.asarray(inp["x"], np.float32)
    xT = [np.ascontiguousarray(x[b].T).reshape(NCH, 128, T) for b in range(8)]
    nc = get_nc((0, 1))
    maps = make_in_maps(inp, xT)
    res = run_bass_kernel_spmd(nc, maps, core_ids=list(range(8)))
    out = np.empty((8, T, D), np.float32)
    for b in range(8):
        out[b] = res.results[b]["yT"].reshape(D, T).T
    return out
```
